# Optimizing a Trainium2 kernel written in Bass

```python
import math
import jax, jax.numpy as jnp
from jax import lax
import numpy as np

D_MODEL = 2048
BATCH = 4
SEQ = 4096
DEPTH = 1

MIX_WIDTH = D_MODEL
ATTN_WIDTH = MIX_WIDTH // 2
ATTN_HEADS = 8
HEAD_DIM = ATTN_WIDTH // ATTN_HEADS
Q_LORA = D_MODEL // 4
KV_LORA = D_MODEL // 8
IDX_HEADS = 16
IDX_DIM = 64
IDX_TOPK_MAX = 256
CONV_WIDTH = MIX_WIDTH - ATTN_WIDTH
CONV_GROUPS = 8
CONV_GROUP_DIM = CONV_WIDTH // CONV_GROUPS
CONV_K = 3
D_FF = 4 * D_MODEL
N_BUCKETS = 32
MAX_DISTANCE = 128
Q_BLOCK = 128
EPS = 1e-6
IN_COLS = Q_LORA + KV_LORA + IDX_DIM + IDX_HEADS + 3 * CONV_WIDTH

kernel_name = 'hybrid_dsa_shortconv_adaln_layer'


def rms_norm(x, g):
    x32 = x.astype(jnp.float32)
    y = x32 * lax.rsqrt(jnp.mean(x32 * x32, axis=-1, keepdims=True) + EPS)
    return (y * g.astype(jnp.float32)).astype(x.dtype)


def layer_norm(x, g, b):
    x32 = x.astype(jnp.float32)
    mu = jnp.mean(x32, axis=-1, keepdims=True)
    xc = x32 - mu
    y = xc * lax.rsqrt(jnp.mean(xc * xc, axis=-1, keepdims=True) + EPS)
    return (y * g.astype(jnp.float32) + b.astype(jnp.float32)).astype(x.dtype)


def modulate(h, shift, scale):
    return h * (1 + scale[:, None, :]) + shift[:, None, :]


def t5_bucket(n):
    max_exact = N_BUCKETS // 2
    nf = jnp.maximum(n, 1).astype(jnp.float32)
    large = max_exact + (jnp.log(nf / max_exact) / math.log(MAX_DISTANCE / max_exact)
                         * (N_BUCKETS - max_exact)).astype(jnp.int32)
    large = jnp.minimum(large, N_BUCKETS - 1)
    return jnp.where(n < max_exact, n, large)


def causal_short_conv(u, w, b):
    S = u.shape[1]
    up = jnp.pad(u, ((0, 0), (CONV_K - 1, 0), (0, 0)))
    y = b
    for j in range(CONV_K):
        y = y + w[j] * up[:, j:j + S]
    return y


def dsa_attention(cq, ckv, k_idx, w_idx, w_uq, w_uk, q_abs_norm_g, w_iq, rel_bias):
    Bn, S, _ = cq.shape
    topk = min(IDX_TOPK_MAX, S // 4)
    nb = S // Q_BLOCK
    q = (cq @ w_uq).reshape(Bn, S, ATTN_HEADS, HEAD_DIM)
    q_abs = jnp.einsum('bshd,hdc->bshc', q, w_uk)
    q_abs = rms_norm(q_abs, q_abs_norm_g)
    iq = (cq @ w_iq).reshape(Bn, S, IDX_HEADS, IDX_DIM)
    w_idx = w_idx * (IDX_HEADS ** -0.5 * IDX_DIM ** -0.5)

    def to_blocks(a):
        return a.reshape((Bn, nb, Q_BLOCK) + a.shape[2:]).swapaxes(0, 1)

    key_pos = jnp.arange(S, dtype=jnp.int32)
    attn_scale = KV_LORA ** -0.5
    neg = jnp.finfo(jnp.float32).min

    def block(args):
        qa, iqb, wb, blk = args
        q_pos = blk * Q_BLOCK + jnp.arange(Q_BLOCK, dtype=jnp.int32)
        lg = jnp.einsum('bqhd,bsd->bqhs', iqb, k_idx)
        score = jnp.einsum('bqhs,bqh->bqs', jax.nn.relu(lg), wb).astype(jnp.float32)
        causal = key_pos[None, :] <= q_pos[:, None]
        score = jnp.where(causal[None], score, neg)
        _, sel = lax.top_k(score, topk)
        valid = sel <= q_pos[None, :, None]
        kv_sel = jax.vmap(lambda kv, i: kv[i])(ckv, sel)
        logits = jnp.einsum('bqhc,bqkc->bhqk', qa, kv_sel).astype(jnp.float32) * attn_scale
        bucket = t5_bucket(jnp.maximum(q_pos[None, :, None] - sel, 0))
        bias = rel_bias[bucket]
        logits = logits + jnp.moveaxis(bias, -1, 1).astype(jnp.float32)
        logits = jnp.where(valid[:, None], logits, neg)
        p = jax.nn.softmax(logits, axis=-1).astype(kv_sel.dtype)
        return jnp.einsum('bhqk,bqkc->bqhc', p, kv_sel)

    o = lax.map(block, (to_blocks(q_abs), to_blocks(iq), to_blocks(w_idx),
                        jnp.arange(nb, dtype=jnp.int32)))
    return o.swapaxes(0, 1).reshape(Bn, S, ATTN_HEADS, KV_LORA)


def setup_inputs(seed: int = 0) -> dict:
    key = jax.random.key(seed)
    ks = jax.random.split(key, 32)
    f32 = jnp.float32

    def nrm(k, shape, scale):
        return jax.random.normal(k, shape, f32) * scale

    def gain(k, shape):
        return 1.0 + 0.02 * jax.random.normal(k, shape, f32)

    L = DEPTH
    return {
        'x': nrm(ks[0], (BATCH, SEQ, D_MODEL), 1.0),
        'c': nrm(ks[1], (BATCH, D_MODEL), 1.0),
        'w_ada': nrm(ks[2], (L, D_MODEL, 6 * D_MODEL), D_MODEL ** -0.5),
        'b_ada': nrm(ks[3], (L, 6 * D_MODEL), 0.02),
        'norm1_g': gain(ks[4], (L, D_MODEL)),
        'norm2_g': gain(ks[5], (L, D_MODEL)),
        'w_in': nrm(ks[6], (L, D_MODEL, IN_COLS), D_MODEL ** -0.5),
        'cq_norm_g': gain(ks[7], (L, Q_LORA)),
        'w_uq': nrm(ks[8], (L, Q_LORA, ATTN_HEADS * HEAD_DIM), Q_LORA ** -0.5),
        'w_uk': nrm(ks[9], (L, ATTN_HEADS, HEAD_DIM, KV_LORA), HEAD_DIM ** -0.5),
        'kv_norm_g': gain(ks[10], (L, KV_LORA)),
        'q_abs_norm_g': gain(ks[11], (L, KV_LORA)),
        'w_uv': nrm(ks[12], (L, ATTN_HEADS, KV_LORA, HEAD_DIM), KV_LORA ** -0.5),
        'w_iq': nrm(ks[13], (L, Q_LORA, IDX_HEADS * IDX_DIM), Q_LORA ** -0.5),
        'idx_k_norm_g': gain(ks[14], (L, IDX_DIM)),
        'idx_k_norm_b': nrm(ks[15], (L, IDX_DIM), 0.02),
        'rel_bias': nrm(ks[16], (N_BUCKETS, ATTN_HEADS), 0.5),
        'conv_w': nrm(ks[17], (L, CONV_K, CONV_WIDTH), CONV_K ** -0.5),
        'conv_b': nrm(ks[18], (L, CONV_WIDTH), 0.02),
        'attn_out_norm_g': gain(ks[19], (L, ATTN_HEADS, HEAD_DIM)),
        'conv_out_norm_g': gain(ks[20], (L, CONV_GROUPS, CONV_GROUP_DIM)),
        'w_out': nrm(ks[21], (L, MIX_WIDTH, D_MODEL), MIX_WIDTH ** -0.5),
        'w_mlp1': nrm(ks[22], (L, D_MODEL, D_FF), D_MODEL ** -0.5),
        'b_mlp1': nrm(ks[23], (L, D_FF), 0.02),
        'w_mlp2': nrm(ks[24], (L, D_FF, D_MODEL), D_FF ** -0.5),
        'b_mlp2': nrm(ks[25], (L, D_MODEL), 0.02),
    }


def reference(x, c, w_ada, b_ada, norm1_g, norm2_g, w_in, cq_norm_g, w_uq, w_uk,
              kv_norm_g, q_abs_norm_g, w_uv, w_iq, idx_k_norm_g, idx_k_norm_b, rel_bias,
              conv_w, conv_b, attn_out_norm_g, conv_out_norm_g, w_out, w_mlp1, b_mlp1,
              w_mlp2, b_mlp2):
    Bn, S, D = x.shape
    c_act = jax.nn.silu(c)
    o0 = Q_LORA
    o1 = o0 + KV_LORA
    o2 = o1 + IDX_DIM
    o3 = o2 + IDX_HEADS
    o4 = o3 + CONV_WIDTH
    o5 = o4 + CONV_WIDTH
    for l in range(DEPTH):
        mod = c_act @ w_ada[l] + b_ada[l]
        sh1, sc1, g1, sh2, sc2, g2 = jnp.split(mod, 6, axis=-1)

        h = modulate(rms_norm(x, norm1_g[l]), sh1, sc1)
        proj = h @ w_in[l]
        cq = rms_norm(proj[..., :o0], cq_norm_g[l])
        ckv = rms_norm(proj[..., o0:o1], kv_norm_g[l])
        k_idx = layer_norm(proj[..., o1:o2], idx_k_norm_g[l], idx_k_norm_b[l])
        w_idx = proj[..., o2:o3]
        gate_b = proj[..., o3:o4]
        gate_c = proj[..., o4:o5]
        h_conv = proj[..., o5:]

        o_lat = dsa_attention(cq, ckv, k_idx, w_idx, w_uq[l], w_uk[l], q_abs_norm_g[l],
                              w_iq[l], rel_bias)
        y_attn = jnp.einsum('bshc,hcv->bshv', o_lat, w_uv[l])
        y_attn = rms_norm(y_attn, attn_out_norm_g[l])

        y_conv = gate_b * causal_short_conv(gate_c * h_conv, conv_w[l], conv_b[l])
        y_conv = rms_norm(y_conv.reshape(Bn, S, CONV_GROUPS, CONV_GROUP_DIM), conv_out_norm_g[l])

        y_mix = jnp.concatenate([y_attn.reshape(Bn, S, ATTN_WIDTH),
                                 y_conv.reshape(Bn, S, CONV_WIDTH)], axis=-1)
        x = x + g1[:, None, :] * (y_mix @ w_out[l])

        h2 = modulate(rms_norm(x, norm2_g[l]), sh2, sc2)
        a = jnp.square(jax.nn.relu(h2 @ w_mlp1[l] + b_mlp1[l]))
        x = x + g2[:, None, :] * (a @ w_mlp2[l] + b_mlp2[l])
    return x
```

```python
import contextlib
import math
import os

import numpy as np
import concourse.bass as bass
import concourse.mybir as mybir
from concourse.bass_utils import run_bass_kernel_spmd

F32 = mybir.dt.float32
BF16 = mybir.dt.bfloat16
AF = mybir.ActivationFunctionType
ALU = mybir.AluOpType
AX = mybir.AxisListType

ENGS = ['pe', 'act', 'dve', 'pool', 'sp']
EPS = 1e-6
NITER = 22


class Op:
    __slots__ = ('eng', 'fn', 'deps', 'ms', 'is_dma', 'sem', 'semval', 'needed', 'grp', 'pos')


class Prog:
    def __init__(self, nc):
        self.nc = nc
        self.ops = {e: [] for e in ENGS}
        self.lastw = {}
        self.readers = {}
        self.dma_sems = {}
        self.esem = {}
        self.final_waits = []

    def dma_sem(self, key, total_mode=False):
        if key not in self.dma_sems:
            h = self.nc.alloc_semaphore(name="d_" + "_".join(str(k) for k in (key if isinstance(key, tuple) else (key,))))
            self.dma_sems[key] = [h, 0, total_mode]
        return self.dma_sems[key]

    def emit(self, eng, fn, reads=(), writes=(), dma_key=None, total_mode=False):
        o = Op()
        o.eng = eng
        o.fn = fn
        o.is_dma = dma_key is not None
        o.needed = False
        o.ms = 0
        o.grp = None
        deps = []
        for r in reads:
            w = self.lastw.get(r)
            if w is not None:
                deps.append(w)
            if isinstance(r, tuple) and r[0] == 'pb':
                deps.extend(x for x in self.readers.get(r, ()) if x.eng != eng)
        for w_ in writes:
            w = self.lastw.get(w_)
            if w is not None:
                deps.append(w)
            deps.extend(self.readers.get(w_, ()))
        best = {}
        dl = []
        for d in deps:
            if d.is_dma:
                if all(d is not q for q in dl):
                    dl.append(d)
                continue
            if d.eng == 'pe' and eng == 'pe' and not o.is_dma:
                continue
            b = best.get(d.eng)
            if b is None or d.pos > b.pos:
                best[d.eng] = d
        o.deps = dl + list(best.values())
        for r in reads:
            self.readers.setdefault(r, []).append(o)
        for w_ in writes:
            self.lastw[w_] = o
            self.readers[w_] = []
        if o.is_dma:
            s = self.dma_sem(dma_key, total_mode)
            if total_mode:
                for d in o.deps:
                    assert not (d.is_dma and d.grp is s), "total-mode DMA group has an internal dependency: %r" % (dma_key,)
            s[1] += 16
            o.sem = s[0]
            o.semval = s[1]
            o.grp = s
        o.pos = len(self.ops[eng])
        self.ops[eng].append(o)
        for d in o.deps:
            d.needed = True
        return o

    def finalize(self, block):
        nc = self.nc
        for e in ENGS:
            self.esem[e] = nc.alloc_semaphore(name="e_" + e)
        for e in ENGS:
            c = 0
            for o in self.ops[e]:
                if (not o.is_dma) and o.needed:
                    c += 1
                    o.ms = c
        bname = {'pe': 'tensor', 'act': 'scalar', 'dve': 'vector', 'pool': 'gpsimd', 'sp': 'sync'}
        stats = {}
        for e in ENGS:
            ops = self.ops[e]
            esem = self.esem
            final_waits = self.final_waits if e == 'sp' else []
            nwait = [0]

            def body(h, ops=ops, e=e, final_waits=final_waits, nwait=nwait):
                waited = {}
                for o in ops:
                    for d in o.deps:
                        if d.is_dma:
                            sem = d.sem
                            val = d.grp[1] if d.grp[2] else d.semval
                            k = ('d', id(d.grp))
                        else:
                            sem = esem[d.eng]
                            val = d.ms
                            k = ('e', d.eng)
                        if waited.get(k, 0) >= val:
                            continue
                        h.wait_ge(sem, val)
                        nwait[0] += 1
                        waited[k] = val
                    ins = o.fn(h)
                    if o.is_dma:
                        ins.then_inc(o.sem, 16)
                    elif o.needed:
                        ins.then_inc(esem[e], 1)
                for key in final_waits:
                    s = self.dma_sems[key]
                    h.wait_ge(s[0], s[1])
            getattr(block, bname[e])(body)
            stats[e] = (len(ops), nwait[0])
        return stats


PRM = {}
_o = 0
for _n, _w in [('c', 16), ('bada', 96), ('n1g', 16), ('n2g', 16), ('cqg', 4), ('kvg', 2), ('qag', 2),
               ('idxg', 1), ('idxb', 1), ('cw', 24), ('cb', 8), ('ag', 8), ('cg', 8), ('b1', 64)]:
    PRM[_n] = (_o, _o + _w)
    _o += _w
NPRM = _o


def _pcol(v):
    v = np.asarray(v, np.float32).reshape(-1, 128)
    return np.ascontiguousarray(v.T)


def build(stop=None, dbg=()):
    nc = bass.Bass("TRN2", target_bir_lowering=False)
    P = Prog(nc)
    E = P.emit

    def din(name, shape, dt=F32):
        return nc.dram_tensor(name, list(shape), dt, kind="ExternalInput").ap()

    x_b = din("x_b", [4096, 2048])
    x_own = din("x_own", [2048, 2048])
    x_halo = din("x_halo", [32, 2048])
    prm_d = din("prm", [128, NPRM])
    hv_d = din("hv", [128, 32])
    cmask_d = din("cmask", [128, 256])
    bias3_d = din("bias3", [3, 128, 1024])
    rb31_d = din("rb31", [1, 8])
    ident_d = din("ident", [128, 128])
    b2_d = din("b2", [1, 2048])
    w_ada = din("w_ada", [2048, 12288])
    w_in = din("w_in", [2048, 3920])
    w_uq = din("w_uq", [512, 1024])
    w_uk = din("w_uk", [8, 128, 256])
    w_uv = din("w_uv", [8, 256, 128])
    w_iq = din("w_iq", [512, 1024])
    w_out = din("w_out", [2048, 2048])
    w1 = din("w1", [2048, 8192])
    w2 = din("w2", [8192, 2048])
    out_d = nc.dram_tensor("out", [2048, 2048], F32, kind="ExternalOutput").ap()
    dbg_d = {}
    for name, shape in dbg:
        dbg_d[name] = nc.dram_tensor("dbg_" + name, list(shape), F32, kind="ExternalOutput").ap()

    def dscr(name, shape, dt=BF16):
        return nc.dram_tensor(name, list(shape), dt, kind="Internal").ap()

    s_win = dscr("s_win", [2048, 3920])
    s_wout = dscr("s_wout", [2048, 2048])
    s_w1 = dscr("s_w1", [2048, 8192])
    s_w2 = dscr("s_w2", [8192, 2048])
    s_wuq = dscr("s_wuq", [512, 1024])
    s_wiq = dscr("s_wiq", [512, 1024])
    gsc = dscr("gsc", [2, 2048], F32)

    with contextlib.ExitStack() as es:
        def SB(name, shape, dt):
            return es.enter_context(nc.sbuf_tensor("sb_" + name, list(shape), dt))

        ckvT = SB("ckvT", [128, 2, 4096], BF16)
        ckvk = SB("ckvk", [128, 32, 256], BF16)
        kx = SB("kx", [128, 4096], BF16)
        ident = SB("identb", [128, 128], BF16)
        ident4 = SB("ident4", [128, 512], BF16)
        ones = SB("ones", [128, 128], BF16)
        bd64 = SB("bd64", [128, 128], BF16)
        cst = SB("cst", [128, 8], F32)
        prm = SB("prm", [128, NPRM], F32)
        modT = SB("modT", [128, 96], F32)
        AA = SB("AA", [128, 32], F32)
        cact = SB("cact", [128, 16], BF16)
        biasb = SB("biasb", [128, 3, 1024], BF16)
        rb31 = SB("rb31", [128, 8], F32)
        cmask = SB("cmask", [128, 256], F32)
        wuk = SB("wuk", [128, 8, 256], BF16)
        wuv = SB("wuv", [128, 8, 2, 128], BF16)
        widxw = SB("widxw", [128, 16, 16], BF16)
        uh = SB("uh", [128, 8, 16, 2], F32)
        hv = SB("hv", [128, 32], F32)
        ring = [SB("ring%d" % i, [128, 8192], BF16) for i in range(2)]
        xo = SB("xo", [128, 2, 2048], F32)
        hT = SB("hT", [128, 16, 256], BF16)
        ymix = SB("ymix", [128, 16, 256], BF16)
        tf = SB("tf", [128, 3, 1024], F32)
        tb = SB("tb", [128, 3, 1024], BF16)
        rbuf = SB("rbuf", [128, 2, 512], F32)
        pTb = SB("pTb", [128, 2, 512], BF16)
        iqT = SB("iqT", [128, 8, 128], BF16)
        qa = SB("qa", [128, 2, 1024], BF16)
        cqT = SB("cqT", [128, 4, 256], BF16)
        ubuf = SB("ubuf", [128, 2, 130], F32)
        ycb = SB("ycb", [128, 256], F32)
        cols = SB("cols", [128, 64], F32)
        wab = SB("wab", [128, 2, 16], F32)
        wsg = SB("wsg", [128, 2, 16], F32)
        ovl = SB("ovl", [128, 16384], BF16)
        score = ovl[:, 0:8192].bitcast(F32)
        mb = ovl[:, 8192:12288]
        gcb = ovl[:, 12288:16384].bitcast(F32).rearrange("p (g t) -> p g t", g=8)
        aT = ovl[:, :].rearrange("p (f t) -> p f t", f=64)
        wk = ovl[:, 0:6144].rearrange("p (k n) -> p k n", k=16)
        pb = [es.enter_context(nc.psum_tensor("pb%d" % i, [128, 512], F32)) for i in range(8)]
        block = es.enter_context(nc.Block())

        def pbf(i):
            return pb[i][:, :].bitcast(BF16)

        def TF(i, a=0, b=1024):
            return [('tf', i, q) for q in range(a // 256, (b + 255) // 256)]

        def TB(i, a=0, b=1024):
            return [('tb', i, q) for q in range(a // 256, (b + 255) // 256)]

        def PB(i, a=0, b=512):
            return [('pb', i)]

        def COL(*idx):
            return [('col', i) for i in idx]

        def mm(out, lhsT, rhs, start, stop, reads, writes):
            E('pe', lambda h: h.matmul(out, lhsT=lhsT, rhs=rhs, start=start, stop=stop), reads=reads, writes=writes)

        def tr(out, in_, idn, reads, writes):
            E('pe', lambda h: h.transpose(out, in_, idn), reads=reads, writes=writes)

        def act(out, in_, func, reads, writes, **kw):
            E('act', lambda h: h.activation(out=out, in_=in_, func=func, **kw), reads=reads, writes=writes)

        def ts(eng, out, in0, s1, s2, op0, op1, reads, writes, accum_out=None):
            def f(h):
                if op1 is None:
                    return h.tensor_scalar(out=out, in0=in0, scalar1=s1, scalar2=None, op0=op0)
                if accum_out is not None:
                    return h.tensor_scalar(out=out, in0=in0, scalar1=s1, scalar2=s2, op0=op0, op1=op1, accum_out=accum_out)
                return h.tensor_scalar(out=out, in0=in0, scalar1=s1, scalar2=s2, op0=op0, op1=op1)
            E(eng, f, reads=reads, writes=writes)

        def stt(out, in0, scalar, in1, op0, op1, reads, writes):
            E('dve', lambda h: h.scalar_tensor_tensor(out=out, in0=in0, scalar=scalar, in1=in1, op0=op0, op1=op1),
              reads=reads, writes=writes)

        def tt(eng, out, in0, in1, op, reads, writes):
            E(eng, lambda h: h.tensor_tensor(out=out, in0=in0, in1=in1, op=op), reads=reads, writes=writes)

        def cp(eng, out, in_, reads, writes):
            if eng == 'act':
                act(out, in_, AF.Copy, reads, writes)
            else:
                E(eng, lambda h: h.tensor_copy(out=out, in_=in_), reads=reads, writes=writes)

        def dma(eng, out, in_, reads, writes, key, total=False, slow=False):
            if slow:
                return E(eng, lambda h: h.dma_start(out=out, in_=in_, allow_slow_non_contiguous=True), reads=reads, writes=writes, dma_key=key, total_mode=total)
            return E(eng, lambda h: h.dma_start(out=out, in_=in_), reads=reads, writes=writes, dma_key=key, total_mode=total)

        def rsqrt(dst, src, scale, np_, reads, writes):
            act(dst, src, AF.Ln, reads, writes, scale=scale, bias=cst[0:np_, 0:1])
            act(dst, dst, AF.Exp, writes, writes, scale=-0.5)

        def dump(name, src, keys):
            if name in dbg_d:
                dma('pool', dbg_d[name], src, keys, [('dbg', name)], ('dbg',), total=True)

        def pc(name, i=None):
            a, b = PRM[name]
            if i is None:
                return prm[:, a:b]
            return prm[:, a + i:a + i + 1]

        class _Stop(Exception):
            pass

        def checkpoint(name):
            if stop == name:
                raise _Stop()

        def setup_dma(out, in_, key, eng='sp'):
            dma(eng, out, in_, [], [key], ('setup',), total=True)

        def emit_all():
            setup_dma(prm[:, :], prm_d, ('prm',))
            setup_dma(hv[:, :], hv_d, ('hv',))
            setup_dma(cmask[:, :], cmask_d, ('cmask',))
            setup_dma(rb31[:, :], rb31_d.partition_broadcast(128), ('rb31',))
            setup_dma(tf[:, 0, 0:128], ident_d, TF(0, 0, 128)[0])
            E('dve', lambda h: h.memset(cst[:, 0:1], EPS), writes=[('cst',)])
            E('dve', lambda h: h.memset(cst[:, 1:2], 0.5), writes=[('cst',)])
            E('dve', lambda h: h.memset(ones[:, :], 1.0), writes=[('ones',)])
            E('dve', lambda h: h.memset(bd64[:, :], 0.0), writes=[('bd64',)])
            E('dve', lambda h: h.memset(bd64[0:64, 0:64], 1.0 / 64), writes=[('bd64',)])
            E('dve', lambda h: h.memset(bd64[64:128, 64:128], 1.0 / 64), writes=[('bd64',)])
            cp('dve', ident[:, :], tf[:, 0, 0:128], TF(0, 0, 128), [('ident',)])
            for i in range(4):
                cp('dve', ident4[:, i * 128:(i + 1) * 128], tf[:, 0, 0:128], TF(0, 0, 128), [('ident4',)])
            dma('pool', wuk[:, :, :], w_uk.rearrange("h d c -> d h c"), [], [('wuk',)], ('setup2',), total=True)
            dma('pool', wuv[:, :, :, :], w_uv.rearrange("h (cc c) v -> c h cc v", cc=2), [], [('wuv',)], ('setup2',), total=True)
            dma('pool', widxw[:, :, :], w_in[:, 832:848].rearrange("(kc p) n -> p kc n", p=128), [], [('widxw',)], ('setup2',), total=True)
            for bi in range(3):
                dma('sp', tf[:, 1, :], bias3_d[bi], [], TF(1), ('bld',))
                tt('dve', tf[:, 1, :].rearrange("p (h t) -> p h t", h=8), tf[:, 1, :].rearrange("p (h t) -> p h t", h=8),
                   rb31[:, :].unsqueeze(2).to_broadcast([128, 8, 128]), ALU.subtract, TF(1) + [('rb31',)], TF(1))
                ts('dve', biasb[:, bi, :], tf[:, 1, :], 16.0, None, ALU.mult, None, TF(1), [('biasb',)])

            checkpoint('S')
            stored = set()
            ring_ctr = [0]

            def ring_view(slot, kc, n):
                return ring[slot][:, 0:kc * n].rearrange("p (k n) -> p k n", k=kc)

            def load_w(src, scr, r0, r1, c0, c1, tag):
                slot = ring_ctr[0] % 2
                ring_ctr[0] += 1
                kc = (r1 - r0) // 128
                n = c1 - c0
                view = ring_view(slot, kc, n)
                key = ('ring', slot)
                sk = ('scr', tag, r0, c0)
                if scr is None or sk not in stored:
                    dma('pool', view, src[r0:r1, c0:c1].rearrange("(k p) n -> p k n", p=128), [], [key], ('rl', slot))
                    if scr is not None:
                        dma('sp', scr[r0:r1, c0:c1].rearrange("(k p) n -> p k n", p=128), view, [key], [sk], ('rs', slot))
                        stored.add(sk)
                else:
                    dma('sp', view, scr[r0:r1, c0:c1].rearrange("(k p) n -> p k n", p=128), [sk], [key], ('rl', slot))
                return view, key

            act(cact[:, :], pc('c'), AF.Silu, [('prm',)], [('cact',)])

            def mod_tile(nt):
                view, key = load_w(w_ada, None, 0, 2048, nt * 512, (nt + 1) * 512, 'ada')
                for mt in range(4):
                    m = nt * 4 + mt
                    for kc in range(16):
                        mm(pb[7][:, m:m + 1], view[:, kc, mt * 128:(mt + 1) * 128], cact[:, kc:kc + 1], kc == 0, kc == 15,
                           [key, ('cact',)], PB(7, 0, 96))
                tt('dve', modT[:, nt * 4:nt * 4 + 4], pb[7][:, nt * 4:nt * 4 + 4], prm[:, PRM['bada'][0] + nt * 4:PRM['bada'][0] + nt * 4 + 4],
                   ALU.add, PB(7, 0, 96) + [('prm',)], [('modT', nt)])

            for nt in range(8):
                mod_tile(nt)
            stt(AA[:, 0:16], modT[:, 16:32], 1.0, pc('n1g'), ALU.add, ALU.mult,
                [('modT', i) for i in range(4, 8)] + [('prm',)], [('A1',)])
            A1K = [('A1',)] + [('modT', i) for i in range(0, 4)]
            if stop == '0':
                dump('modT', modT[:, :], [('modT', i) for i in range(8)])
            checkpoint('0')

            def rms_T(src, np_, dstT, c0, Acol, Bcol, srckeys, dstkey, abkeys, trb):
                xh = tb[0:np_, 0:2, :].rearrange("p a b -> p (a b)")
                jk = mb[0:np_, 0:2048]
                act(jk, src, AF.Square, srckeys + [('mb',)], [('mb',)] + COL(0), accum_out=cols[0:np_, 0:1])
                rsqrt(cols[0:np_, 1:2], cols[0:np_, 0:1], 1.0 / 2048, np_, COL(0), COL(1))
                ts('dve', xh, src, cols[0:np_, 1:2], None, ALU.mult, None, srckeys + COL(1), TB(0) + TB(1))
                for half in range(2):
                    bank = trb[half]
                    for j in range(8):
                        kc = half * 8 + j
                        tr(pbf(bank)[:, j * 128:j * 128 + np_], xh[:, kc * 128:(kc + 1) * 128], ident[0:np_, 0:np_],
                           TB(0) + TB(1) + [('ident',)], PB(bank))
                    for j in range(8):
                        kc = half * 8 + j
                        o = dstT[:, kc, c0:c0 + np_]
                        i_ = pbf(bank)[:, j * 128:j * 128 + np_]
                        if j % 2 == 0:
                            act(o, i_, AF.Identity, PB(bank) + abkeys, [dstkey], scale=Acol(kc), bias=Bcol(kc))
                        else:
                            ts('dve', o, i_, Acol(kc), Bcol(kc), ALU.mult, ALU.add, PB(bank) + abkeys, [dstkey])

            A1c = lambda kc: AA[:, kc:kc + 1]
            B1c = lambda kc: modT[:, kc:kc + 1]
            A2c = lambda kc: AA[:, 16 + kc:17 + kc]
            B2c = lambda kc: modT[:, 48 + kc:49 + kc]

            dma('pool', wk[:, :, 0:320], w_in[:, 512:832].rearrange("(k p) n -> p k n", p=128), [], [('wk', 0)], ('setup2',), total=True)
            dma('pool', wk[:, :, 320:384], w_in[:, 768:832].rearrange("(k p) n -> p k n", p=128), [], [('wk', 1)], ('setup2',), total=True)
            checkpoint('A0')
            NTA = 16
            for ta in range(NTA):
                for blk in range(2):
                    g = ta * 2 + blk
                    dma('sp', xo[:, blk, :], x_b[g * 128:(g + 1) * 128, :], [], [('xo', blk)], ('xl', blk))
                    rms_T(xo[:, blk, :], 128, hT, blk * 128, A1c, B1c, [('xo', blk)], ('hT', blk), A1K, (5, 6))
                    checkpoint('A1_%d' % ta)
                hk = [('hT', 0), ('hT', 1)]
                for mt in range(3):
                    for kc in range(16):
                        mm(pb[mt][:, 0:256], wk[:, kc, mt * 128:(mt + 1) * 128], hT[:, kc, :], kc == 0, kc == 15,
                           hk + [('wk', 0), ('wk', 1)], PB(mt, 0, 256))
                t0 = ta * 256
                checkpoint('A2_%d' % ta)
                for cc in range(2):
                    act(tb[:, 2, cc * 256:(cc + 1) * 256], pb[cc][:, 0:256], AF.Square, PB(cc, 0, 256), TB(2, cc * 256, cc * 256 + 256))
                for cc in range(2):
                    mm(pb[3][:, 0:256], ones[:, :], tb[:, 2, cc * 256:(cc + 1) * 256], cc == 0, cc == 1,
                       TB(2, cc * 256, cc * 256 + 256) + [('ones',)], PB(3, 0, 256))
                rsqrt(tf[:, 0, 0:256], pb[3][:, 0:256], 1.0 / 256, 128, PB(3, 0, 256), TF(0, 0, 256))
                for cc in range(2):
                    stt(ckvT[:, cc, t0:t0 + 256], pb[cc][:, 0:256], pc('kvg', cc), tf[:, 0, 0:256], ALU.mult, ALU.mult,
                        PB(cc, 0, 256) + TF(0, 0, 256) + [('prm',)], [('ckvT', ta)])
                checkpoint('A3_%d' % ta)
                for blk in range(2):
                    for cc in range(2):
                        j = blk * 2 + cc
                        tr(pbf(4)[:, j * 128:(j + 1) * 128], ckvT[:, cc, t0 + blk * 128:t0 + (blk + 1) * 128], ident[:, :],
                           [('ckvT', ta), ('ident',)], PB(4, 0, 256))
                cp('act', ckvk[:, 2 * ta:2 * ta + 2, :], pbf(4)[:, 0:512].rearrange("p (b c) -> p b c", b=2), PB(4, 0, 256), [('ckvk', ta)])
                checkpoint('A4_%d' % ta)
                cp('act', tf[:, 1, 0:256], pb[2][:, 0:256], PB(2, 0, 256), TF(1, 0, 256))
                cp('act', tb[:, 2, 0:256], tf[:, 1, 0:256], TF(1, 0, 256), TB(2, 0, 256))
                tt('dve', tb[:, 2, 256:512], tf[:, 1, 0:256], tb[:, 2, 0:256], ALU.subtract, TF(1, 0, 256) + TB(2, 0, 256), TB(2, 256, 512))
                mm(pb[3][:, 256:512], bd64[:, :], tb[:, 2, 0:256], True, False, TB(2, 0, 256) + [('bd64',)], PB(3, 256, 512))
                mm(pb[3][:, 256:512], bd64[:, :], tb[:, 2, 256:512], False, True, TB(2, 256, 512) + [('bd64',)], PB(3, 256, 512))
                tt('dve', tf[:, 1, 256:512], tf[:, 1, 0:256], pb[3][:, 256:512], ALU.subtract, TF(1, 0, 256) + PB(3, 256, 512), TF(1, 256, 512))
                act(tf[:, 1, 512:768], tf[:, 1, 256:512], AF.Square, TF(1, 256, 512), TF(1, 512, 768))
                cp('act', tb[:, 2, 512:768], tf[:, 1, 512:768], TF(1, 512, 768), TB(2, 512, 768))
                tt('dve', tb[:, 2, 768:1024], tf[:, 1, 512:768], tb[:, 2, 512:768], ALU.subtract, TF(1, 512, 768) + TB(2, 512, 768), TB(2, 768, 1024))
                mm(pb[4][:, 256:512], bd64[:, :], tb[:, 2, 512:768], True, False, TB(2, 512, 768) + [('bd64',)], PB(4, 256, 512))
                mm(pb[4][:, 256:512], bd64[:, :], tb[:, 2, 768:1024], False, True, TB(2, 768, 1024) + [('bd64',)], PB(4, 256, 512))
                rsqrt(tf[:, 1, 768:1024], pb[4][:, 256:512], 1.0, 128, PB(4, 256, 512), TF(1, 768, 1024))
                tt('dve', tf[:, 2, 0:256], tf[:, 1, 256:512], tf[:, 1, 768:1024], ALU.mult, TF(1, 256, 512) + TF(1, 768, 1024), TF(2, 0, 256))
                act(kx[:, t0:t0 + 256], tf[:, 2, 0:256], AF.Identity, TF(2, 0, 256) + [('prm',)], [('kx', ta)],
                    scale=pc('idxg', 0), bias=pc('idxb', 0))
                checkpoint('A5_%d' % ta)
                if ta < 8:
                    mod_tile(8 + 2 * ta)
                    mod_tile(9 + 2 * ta)
                checkpoint('A6_%d' % ta)
            ALLMOD = [('modT', i) for i in range(24)]
            stt(AA[:, 16:32], modT[:, 64:80], 1.0, pc('n2g'), ALU.add, ALU.mult, ALLMOD + [('prm',)], [('A2',)])
            A2K = [('A2',)] + ALLMOD
            dma('sp', gsc[0].rearrange("(m p) -> p m", p=128), modT[:, 32:48], ALLMOD, [('gsc', 0)], ('gsc',), total=True, slow=True)
            dma('sp', gsc[1].rearrange("(m p) -> p m", p=128), modT[:, 80:96], ALLMOD, [('gsc', 1)], ('gsc',), total=True, slow=True)
            KEYS_ALL = [('ckvT', i) for i in range(NTA)] + [('ckvk', i) for i in range(NTA)] + [('kx', i) for i in range(NTA)]
            dump('ckvT', ckvT[:, 0, 0:2048], KEYS_ALL)
            dump('kx', kx[:, 0:2048], KEYS_ALL)
            dump('modT', modT[:, :], ALLMOD)

            OVK_A = [('score',), ('mb',), ('wk', 0), ('wk', 1)] + [('gcb', g) for g in range(8)]
            OVK_M = [('aT', f) for f in range(64)]

            def barrier(keys):
                E('pool', lambda h: h.memset(cols[:, 63:64], 0.0), reads=[], writes=list(keys) + COL(63))

            if stop != 'A':
                barrier(OVK_A + OVK_M)
                dma('sp', xo[0:32, 0, :], x_halo, [], [('xo', 0)], ('xl', 0))
                rms_T(xo[0:32, 0, :], 32, hT, 0, A1c, B1c, [('xo', 0)], ('hT', 0), A1K, (5, 6))
                for part in range(2):
                    for half in range(2):
                        c0 = 1872 + part * 1024 + half * 512
                        view, key = load_w(w_in, s_win, 0, 2048, c0, c0 + 512, 'win')
                        for gg in range(4):
                            g = half * 4 + gg
                            bank = gg % 2
                            for kc in range(16):
                                mm(pb[bank][:, 0:32], view[:, kc, gg * 128:(gg + 1) * 128], hT[:, kc, 0:32], kc == 0, kc == 15,
                                   [key, ('hT', 0)], PB(bank, 0, 32))
                            uv = uh[:, g, :, :].rearrange("p s j -> p (s j)")
                            if part == 0:
                                tt('dve', uv, pb[bank][:, 0:32], hv[:, :], ALU.mult, PB(bank, 0, 32) + [('hv',)], [('uh', g)])
                            else:
                                tt('dve', uv, pb[bank][:, 0:32], uv, ALU.mult, PB(bank, 0, 32) + [('uh', g)], [('uh', g)])
                dump('uh', uh[:, :, :, :].rearrange("p g s j -> p (g s j)"), [('uh', g) for g in range(8)])

            checkpoint('H')
            NT = 0 if stop == 'A' else (int(stop[1:].rstrip('a')) if (stop and stop[0] == 'T') else 8)
            for ti in range(NT):
                last_dbg = (ti == NT - 1)
                barrier(OVK_A + OVK_M)
                for blk in range(2):
                    r0 = ti * 256 + blk * 128
                    dma('sp', xo[:, blk, :], x_own[r0:r0 + 128, :], [], [('xo', blk)], ('xl', blk))
                    rms_T(xo[:, blk, :], 128, hT, blk * 128, A1c, B1c, [('xo', blk)], ('hT', blk), A1K, (5, 6))
                hk = [('hT', 0), ('hT', 1)]
                checkpoint('C0a_%d' % ti)
                view, key = load_w(w_in, s_win, 0, 2048, 0, 512, 'win')
                for mt in range(4):
                    for kc in range(16):
                        mm(pb[mt][:, 0:256], view[:, kc, mt * 128:(mt + 1) * 128], hT[:, kc, :], kc == 0, kc == 15, hk + [key], PB(mt, 0, 256))
                checkpoint('C0b_%d' % ti)
                for mt in range(4):
                    act(tb[:, 2, mt * 256:(mt + 1) * 256], pb[mt][:, 0:256], AF.Square, PB(mt, 0, 256), TB(2, mt * 256, mt * 256 + 256))
                    cp('act', tf[:, 0, mt * 256:(mt + 1) * 256], pb[mt][:, 0:256], PB(mt, 0, 256), TF(0, mt * 256, mt * 256 + 256))
                for mt in range(4):
                    mm(pb[4][:, 0:256], ones[:, :], tb[:, 2, mt * 256:(mt + 1) * 256], mt == 0, mt == 3, TB(2, mt * 256, mt * 256 + 256) + [('ones',)], PB(4, 0, 256))
                rsqrt(tf[:, 1, 0:256], pb[4][:, 0:256], 1.0 / 512, 128, PB(4, 0, 256), TF(1, 0, 256))
                for mt in range(4):
                    stt(cqT[:, mt, :], tf[:, 0, mt * 256:(mt + 1) * 256], pc('cqg', mt), tf[:, 1, 0:256], ALU.mult, ALU.mult,
                        TF(0, mt * 256, mt * 256 + 256) + TF(1, 0, 256) + [('prm',)], [('cqT',)])
                checkpoint('C1_%d' % ti)
                for blk in range(2):
                    for kc in range(16):
                        mm(pb[4][:, 256 + blk * 16:256 + (blk + 1) * 16], hT[:, kc, blk * 128:(blk + 1) * 128], widxw[:, kc, :], kc == 0, kc == 15,
                           hk + [('widxw',)], PB(4, 256, 288))
                act(wab[:, :, :], pb[4][:, 256:288].rearrange("p (b h) -> p b h", b=2), AF.Abs, PB(4, 256, 288), [('wab',)], scale=1.0 / 32)
                act(wsg[:, :, :], pb[4][:, 256:288].rearrange("p (b h) -> p b h", b=2), AF.Sign, PB(4, 256, 288), [('wsg',)])
                checkpoint('C2_%d' % ti)
                for half in range(2):
                    c0 = 1872 + half * 512
                    view, key = load_w(w_in, s_win, 0, 2048, c0, c0 + 512, 'win')
                    for gg in range(4):
                        g = half * 4 + gg
                        bank = gg % 2
                        for kc in range(16):
                            mm(pb[bank][:, 0:256], view[:, kc, gg * 128:(gg + 1) * 128], hT[:, kc, :], kc == 0, kc == 15, hk + [key], PB(bank, 0, 256))
                        cp('act', gcb[:, g, :], pb[bank][:, 0:256], PB(bank, 0, 256), [('gcb', g)])
                cwa, _ = PRM['cw']
                for half in range(2):
                    c0 = 2896 + half * 512
                    view, key = load_w(w_in, s_win, 0, 2048, c0, c0 + 512, 'win')
                    for gg in range(4):
                        g = half * 4 + gg
                        bank = gg % 2
                        for kc in range(16):
                            mm(pb[bank][:, 0:256], view[:, kc, gg * 128:(gg + 1) * 128], hT[:, kc, :], kc == 0, kc == 15, hk + [key], PB(bank, 0, 256))
                        tt('dve', ubuf[:, :, 2:130], pb[bank][:, 0:256].rearrange("p (b t) -> p b t", b=2),
                           gcb[:, g, :].rearrange("p (b t) -> p b t", b=2), ALU.mult, PB(bank, 0, 256) + [('gcb', g)], [('ubuf',)])
                        cp('pool', ubuf[:, :, 0:2], uh[:, g, 2 * ti:2 * ti + 2, :], [('uh', g)], [('ubuf', 'h')])
                        gv = gcb[:, g, :].rearrange("p (b t) -> p b t", b=2)
                        act(gv, ubuf[:, :, 2:130], AF.Identity, [('ubuf',), ('prm',)], [('gcb', g)],
                            scale=prm[:, cwa + 16 + g:cwa + 17 + g], bias=pc('cb', g))
                        stt(gv, ubuf[:, :, 1:129], prm[:, cwa + 8 + g:cwa + 9 + g], gv, ALU.mult, ALU.add,
                            [('ubuf',), ('ubuf', 'h'), ('gcb', g), ('prm',)], [('gcb', g)])
                        stt(gv, ubuf[:, :, 0:128], prm[:, cwa + g:cwa + g + 1], gv, ALU.mult, ALU.add,
                            [('ubuf',), ('ubuf', 'h'), ('gcb', g), ('prm',)], [('gcb', g)])
                for half in range(2):
                    c0 = 848 + half * 512
                    view, key = load_w(w_in, s_win, 0, 2048, c0, c0 + 512, 'win')
                    for gg in range(4):
                        g = half * 4 + gg
                        bank = gg % 2
                        for kc in range(16):
                            mm(pb[bank][:, 0:256], view[:, kc, gg * 128:(gg + 1) * 128], hT[:, kc, :], kc == 0, kc == 15, hk + [key], PB(bank, 0, 256))
                        tt('dve', ycb[:, :], pb[bank][:, 0:256], gcb[:, g, :], ALU.mult, PB(bank, 0, 256) + [('gcb', g)], [('ycb',)])
                        act(tb[:, 2, 0:256], ycb[:, :], AF.Square, [('ycb',)], TB(2, 0, 256))
                        mm(pb[2 + bank][:, 0:256], ones[:, :], tb[:, 2, 0:256], True, True, TB(2, 0, 256) + [('ones',)], PB(2 + bank, 0, 256))
                        rsqrt(tf[:, 1, 256:512], pb[2 + bank][:, 0:256], 1.0 / 128, 128, PB(2 + bank, 0, 256), TF(1, 256, 512))
                        stt(ymix[:, 8 + g, :], ycb[:, :], pc('cg', g), tf[:, 1, 256:512], ALU.mult, ALU.mult,
                            [('ycb',)] + TF(1, 256, 512) + [('prm',)], [('ymix', 8 + g)])
                checkpoint('C3_%d' % ti)
                if last_dbg:
                    dump('cqT', cqT[:, 0, :], [('cqT',)])
                    dump('yconv', ymix[:, 8, :], [('ymix', 8)])
                vq, kq = load_w(w_uq, s_wuq, 0, 512, 0, 1024, 'wuq')
                vi, ki = load_w(w_iq, s_wiq, 0, 512, 0, 1024, 'wiq')
                for blk in range(2):
                    j = 2 * ti + blk
                    nkc = 2 * j + 2
                    nk = nkc * 128
                    tsl = slice(blk * 128, (blk + 1) * 128)
                    for half in range(2):
                        for pp in range(4):
                            pr = half * 4 + pp
                            for kc in range(4):
                                mm(pb[half][:, pp * 128:(pp + 1) * 128], vi[:, kc, pr * 128:(pr + 1) * 128], cqT[:, kc, tsl], kc == 0, kc == 3,
                                   [ki, ('cqT',)], PB(half))
                        cp('act', iqT[:, half * 4:half * 4 + 4, :], pb[half][:, :].rearrange("p (a t) -> p a t", a=4), PB(half), [('iqT',)])
                    qT = tb[:, 2, :]
                    for half in range(2):
                        for hh in range(4):
                            h_ = half * 4 + hh
                            for kc in range(4):
                                mm(pb[2 + half][:, hh * 128:(hh + 1) * 128], vq[:, kc, h_ * 128:(h_ + 1) * 128], cqT[:, kc, tsl], kc == 0, kc == 3,
                                   [kq, ('cqT',)], PB(2 + half))
                        cp('dve', qT[:, half * 512:(half + 1) * 512], pb[2 + half][:, :], PB(2 + half), TB(2, half * 512, half * 512 + 512))
                    QK = TB(2)
                    for cc in range(2):
                        for h_ in range(8):
                            bank = 4 + cc * 2 + h_ // 4
                            mm(pb[bank][:, (h_ % 4) * 128:(h_ % 4 + 1) * 128], wuk[:, h_, cc * 128:(cc + 1) * 128], qT[:, h_ * 128:(h_ + 1) * 128],
                               True, True, QK + [('wuk',)], PB(bank, (h_ % 4) * 128, (h_ % 4) * 128 + 128))
                    sq = tb[:, 0:2, :]
                    for cc in range(2):
                        for hf in range(2):
                            bank = 4 + cc * 2 + hf
                            act(sq[:, cc, hf * 512:(hf + 1) * 512], pb[bank][:, :], AF.Square, PB(bank), TB(cc, hf * 512, hf * 512 + 512))
                    for hf in range(2):
                        for cc in range(2):
                            mm(pb[hf][:, :], ones[:, :], sq[:, cc, hf * 512:(hf + 1) * 512], cc == 0, cc == 1, TB(cc, hf * 512, hf * 512 + 512) + [('ones',)], PB(hf))
                        rsqrt(tf[:, 0, hf * 512:(hf + 1) * 512], pb[hf][:, :], 1.0 / 256, 128, PB(hf), TF(0, hf * 512, hf * 512 + 512))
                    for cc in range(2):
                        for hf in range(2):
                            bank = 4 + cc * 2 + hf
                            stt(qa[:, cc, hf * 512:(hf + 1) * 512], pb[bank][:, :], pc('qag', cc), tf[:, 0, hf * 512:(hf + 1) * 512], ALU.mult, ALU.mult,
                                PB(bank) + TF(0, hf * 512, hf * 512 + 512) + [('prm',)], [('qa', hf)])
                    checkpoint('Q_%d_%d' % (ti, blk))
                    cnt = 0
                    for k0 in range(0, nk, 512):
                        kw = min(512, nk - k0)
                        for h_ in range(16):
                            hp = h_ % 2
                            pr = h_ // 2
                            bank = 2 + (cnt % 2)
                            rb = cnt % 2
                            cnt += 1
                            kta = [('kx', i) for i in range(k0 // 256, (k0 + kw) // 256)]
                            mm(pb[bank][:, 0:kw], iqT[hp * 64:(hp + 1) * 64, pr, :], kx[hp * 64:(hp + 1) * 64, k0:k0 + kw], True, True,
                               [('iqT',)] + kta, PB(bank, 0, kw))
                            act(rbuf[:, rb, 0:kw], pb[bank][:, 0:kw], AF.Relu, PB(bank, 0, kw) + [('wab',)], [('rbuf', rb)], scale=wab[:, blk, h_:h_ + 1])
                            if h_ == 0:
                                ts('dve', score[:, k0:k0 + kw], rbuf[:, rb, 0:kw], wsg[:, blk, 0:1], None, ALU.mult, None,
                                   [('rbuf', rb), ('wsg',)], [('score',)])
                            else:
                                stt(score[:, k0:k0 + kw], rbuf[:, rb, 0:kw], wsg[:, blk, h_:h_ + 1], score[:, k0:k0 + kw], ALU.mult, ALU.add,
                                    [('rbuf', rb), ('wsg',), ('score',)], [('score',)])
                    checkpoint('X_%d_%d' % (ti, blk))
                    lo = cols[:, 8:9]
                    hi = cols[:, 9:10]
                    mid = cols[:, 10:11]
                    cntc = cols[:, 11:12]
                    ge = cols[:, 12:13]
                    d1 = cols[:, 13:14]
                    d2 = cols[:, 14:15]
                    if j >= 1:
                        E('dve', lambda h, nk=nk: h.tensor_reduce(out=hi, in_=score[:, 0:nk], axis=AX.X, op=ALU.max), reads=[('score',)], writes=COL(9))
                        E('dve', lambda h, nk=nk: h.tensor_reduce(out=lo, in_=score[:, 0:nk], axis=AX.X, op=ALU.min), reads=[('score',)], writes=COL(8))
                        ts('dve', lo, lo, -1.0, None, ALU.add, None, COL(8), COL(8))
                    tt('dve', score[:, nk - 256:nk], score[:, nk - 256:nk], cmask[:, :], ALU.add, [('score',), ('cmask',)], [('score',)])
                    if j >= 1:
                        for it in range(NITER):
                            stt(mid, lo, hi, cst[:, 1:2], ALU.add, ALU.mult, COL(8, 9) + [('cst',)], COL(10))
                            ts('dve', mb[:, 0:nk], score[:, 0:nk], mid, 0.0, ALU.is_gt, ALU.add, [('score',), ('mb',)] + COL(10),
                               [('mb',)] + COL(11), accum_out=cntc)
                            ts('dve', ge, cntc, 255.5, None, ALU.is_ge, None, COL(11), COL(12))
                            tt('dve', d1, mid, lo, ALU.subtract, COL(10, 8), COL(13))
                            tt('dve', d2, hi, mid, ALU.subtract, COL(10, 9), COL(14))
                            stt(lo, d1, ge, lo, ALU.mult, ALU.add, COL(13, 12, 8), COL(8))
                            stt(hi, d2, ge, mid, ALU.mult, ALU.add, COL(14, 12, 10), COL(9))
                        ts('dve', mb[:, 0:nk], score[:, 0:nk], lo, -30000.0, ALU.is_le, ALU.mult, [('score',), ('mb',)] + COL(8), [('mb',)])
                    else:
                        ts('dve', mb[:, 0:nk], score[:, 0:nk], -1.0e29, -30000.0, ALU.is_le, ALU.mult, [('score',), ('mb',)], [('mb',)])
                    checkpoint('M_%d_%d' % (ti, blk))
                    if last_dbg and blk == 1:
                        dump('score', score[:, 0:512], [('score',)])
                        dump('thr', cols[:, 0:16], COL(8, 9, 11))
                    for hg in range(2):
                        hs = slice(hg * 512, (hg + 1) * 512)
                        for kc in range(nkc):
                            lb = 2 + (kc % 2)
                            pk = kc % 2
                            ksl = slice(kc * 128, (kc + 1) * 128)
                            kta = [('ckvT', kc // 2)]
                            bi = nkc - 1 - kc
                            near = (bi <= 2)
                            mm(pb[lb][:, :], ckvT[:, 0, ksl], qa[:, 0, hs], True, False, kta + [('qa', hg)], PB(lb))
                            mm(pb[lb][:, :], ckvT[:, 1, ksl], qa[:, 1, hs], False, False, kta + [('qa', hg)], PB(lb))
                            mm(pb[lb][:, :], mb[:, ksl], ident4[:, :], False, not near, [('mb',), ('ident4',)], PB(lb))
                            if near:
                                mm(pb[lb][:, :], ident[:, :], biasb[:, bi, hs], False, True, [('biasb',), ('ident',)], PB(lb))
                            act(pTb[:, pk, :], pb[lb][:, :], AF.Exp, PB(lb), [('pTb', pk)], scale=1.0 / 16)
                            for cc in range(2):
                                mm(pb[4 + cc][:, :], ckvk[:, kc, cc * 128:(cc + 1) * 128], pTb[:, pk, :], kc == 0, kc == nkc - 1,
                                   [('ckvk', kc // 2), ('pTb', pk)], PB(4 + cc))
                            mm(pb[6][:, :], ones[:, :], pTb[:, pk, :], kc == 0, kc == nkc - 1, [('ones',), ('pTb', pk)], PB(6))
                        oT = tb[:, 0:2, 0:512]
                        for cc in range(2):
                            cp('dve' if cc == 0 else 'act', oT[:, cc, :], pb[4 + cc][:, :], PB(4 + cc), TB(cc, 0, 512))
                        act(tf[:, 2, 0:512], pb[6][:, :], AF.Square, PB(6), TF(2, 0, 512), scale=math.sqrt(EPS))
                        for hl in range(4):
                            h_ = hg * 4 + hl
                            for cc in range(2):
                                mm(pb[7][:, hl * 128:(hl + 1) * 128], wuv[:, h_, cc, :], oT[:, cc, hl * 128:(hl + 1) * 128], cc == 0, cc == 1,
                                   TB(cc, 0, 512) + [('wuv',)], PB(7))
                        act(tb[:, 2, 0:512], pb[7][:, :], AF.Square, PB(7), TB(2, 0, 512))
                        mm(pb[0][:, :], ones[:, :], tb[:, 2, 0:512], True, True, TB(2, 0, 512) + [('ones',)], PB(0))
                        stt(tf[:, 2, 512:1024], pb[0][:, :], 1.0 / 128, tf[:, 2, 0:512], ALU.mult, ALU.add, PB(0) + TF(2, 0, 512), TF(2, 512, 1024))
                        act(tf[:, 2, 512:1024], tf[:, 2, 512:1024], AF.Ln, TF(2, 512, 1024), TF(2, 512, 1024))
                        act(tf[:, 2, 512:1024], tf[:, 2, 512:1024], AF.Exp, TF(2, 512, 1024), TF(2, 512, 1024), scale=-0.5)
                        for hl in range(4):
                            h_ = hg * 4 + hl
                            stt(ymix[:, h_, tsl], pb[7][:, hl * 128:(hl + 1) * 128], pc('ag', h_), tf[:, 2, 512 + hl * 128:512 + (hl + 1) * 128],
                                ALU.mult, ALU.mult, PB(7) + TF(2, 512, 1024) + [('prm',)], [('ymix', h_)])
                checkpoint('AT_%d' % ti)
                if last_dbg:
                    dump('yattn', ymix[:, 0, :], [('ymix', 0)])
                YK = [('ymix', i) for i in range(16)]
                for nb in range(4):
                    view, key = load_w(w_out, s_wout, 0, 2048, nb * 512, (nb + 1) * 512, 'wout')
                    dma('sp', tf[:, 0, 0:512], gsc[0:1, nb * 512:(nb + 1) * 512].partition_broadcast(128), [('gsc', 0)], TF(0, 0, 512), ('bc', 0))
                    for blk in range(2):
                        for kc in range(16):
                            mm(pb[blk][:, :], ymix[:, kc, blk * 128:(blk + 1) * 128], view[:, kc, :], kc == 0, kc == 15, YK + [key], PB(blk))
                        tt('dve', tf[:, 1, blk * 512:(blk + 1) * 512], pb[blk][:, :], tf[:, 0, 0:512], ALU.mult, PB(blk) + TF(0, 0, 512), TF(1, blk * 512, blk * 512 + 512))
                        tt('pool', xo[:, blk, nb * 512:(nb + 1) * 512], xo[:, blk, nb * 512:(nb + 1) * 512], tf[:, 1, blk * 512:(blk + 1) * 512], ALU.add,
                           TF(1, blk * 512, blk * 512 + 512) + [('xo', blk)], [('xo', blk)])
                if last_dbg:
                    dump('x1', xo[:, 0, :], [('xo', 0)])
                if stop == 'T%da' % NT and last_dbg:
                    break
                barrier(OVK_A + OVK_M)
                for blk in range(2):
                    rms_T(xo[:, blk, :], 128, hT, blk * 128, A2c, B2c, [('xo', blk)], ('hT', blk), A2K, (5, 6))
                b1a, _ = PRM['b1']
                for fg in range(16):
                    view, key = load_w(w1, s_w1, 0, 2048, fg * 512, (fg + 1) * 512, 'w1')
                    for ft in range(4):
                        f = fg * 4 + ft
                        bank = f % 4
                        rb = f % 2
                        for kc in range(16):
                            mm(pb[bank][:, 0:256], view[:, kc, ft * 128:(ft + 1) * 128], hT[:, kc, :], kc == 0, kc == 15, hk + [key], PB(bank, 0, 256))
                        act(rbuf[:, rb, 0:256], pb[bank][:, 0:256], AF.Relu, PB(bank, 0, 256) + [('prm',)], [('rbuf', rb)], bias=prm[:, b1a + f:b1a + f + 1])
                        stt(aT[:, f, :], pb[bank][:, 0:256], prm[:, b1a + f:b1a + f + 1], rbuf[:, rb, 0:256], ALU.add, ALU.mult,
                            PB(bank, 0, 256) + [('rbuf', rb), ('prm',)], [('aT', f)])
                for nb in range(4):
                    nsl = slice(nb * 512, (nb + 1) * 512)
                    dma('sp', tf[:, 0, 0:512], gsc[1:2, nsl].partition_broadcast(128), [('gsc', 1)], TF(0, 0, 512), ('bc', 0))
                    dma('sp', tf[:, 0, 512:1024], b2_d[0:1, nsl].partition_broadcast(128), [], TF(0, 512, 1024), ('bc', 1))
                    tt('pool', tf[:, 0, 512:1024], tf[:, 0, 512:1024], tf[:, 0, 0:512], ALU.mult, TF(0, 0, 512) + TF(0, 512, 1024), TF(0, 512, 1024))
                    for blk in range(2):
                        tt('pool', xo[:, blk, nsl], xo[:, blk, nsl], tf[:, 0, 512:1024], ALU.add, TF(0, 512, 1024) + [('xo', blk)], [('xo', blk)])
                    for fq in range(4):
                        view, key = load_w(w2, s_w2, fq * 2048, (fq + 1) * 2048, nb * 512, (nb + 1) * 512, 'w2')
                        for kc in range(16):
                            f = fq * 16 + kc
                            for blk in range(2):
                                mm(pb[4 + blk][:, :], aT[:, f, blk * 128:(blk + 1) * 128], view[:, kc, :], f == 0, f == 63, [('aT', f), key], PB(4 + blk))
                    for blk in range(2):
                        tt('dve', tf[:, 1, blk * 512:(blk + 1) * 512], pb[4 + blk][:, :], tf[:, 0, 0:512], ALU.mult, PB(4 + blk) + TF(0, 0, 512), TF(1, blk * 512, blk * 512 + 512))
                        tt('pool', xo[:, blk, nsl], xo[:, blk, nsl], tf[:, 1, blk * 512:(blk + 1) * 512], ALU.add, TF(1, blk * 512, blk * 512 + 512) + [('xo', blk)], [('xo', blk)])
                for blk in range(2):
                    r0 = ti * 256 + blk * 128
                    dma('sp', out_d[r0:r0 + 128, :], xo[:, blk, :], [('xo', blk)], [('out', ti, blk)], ('out', blk))
        try:
            emit_all()
        except _Stop:
            pass
        for k in list(P.dma_sems.keys()):
            P.final_waits.append(k)
        stats = P.finalize(block)
        if os.environ.get("KDEBUG"):
            print("ops/waits per engine:", stats)
    return nc


def _t5_bucket(n):
    n = np.asarray(n, np.int32)
    nf = np.maximum(n, 1).astype(np.float32)
    large = 16 + (np.log(nf / np.float32(16)) / np.float32(math.log(128 / 16)) * np.float32(16)).astype(np.int32)
    large = np.minimum(large, 31)
    return np.where(n < 16, n, large)


def make_inputs(inp, core):
    b, r = core // 2, core % 2
    x = np.asarray(inp['x'], np.float32)
    qbs = [2 * j + r for j in range(16)]
    x_own = np.concatenate([x[b, q * 128:(q + 1) * 128] for q in qbs], 0)
    x_halo = np.zeros((32, 2048), np.float32)
    hvv = np.zeros((32,), np.float32)
    for j, q in enumerate(qbs):
        if q > 0:
            x_halo[2 * j:2 * j + 2] = x[b, q * 128 - 2:q * 128]
            hvv[2 * j:2 * j + 2] = 1.0
    prm = np.zeros((128, NPRM), np.float32)

    def put(name, arr):
        a, e = PRM[name]
        prm[:, a:e] = arr
    put('c', _pcol(inp['c'][b]))
    put('bada', _pcol(inp['b_ada'][0]))
    put('n1g', _pcol(inp['norm1_g'][0]))
    put('n2g', _pcol(inp['norm2_g'][0]))
    put('cqg', _pcol(inp['cq_norm_g'][0]))
    put('kvg', _pcol(inp['kv_norm_g'][0]))
    put('qag', _pcol(inp['q_abs_norm_g'][0]))
    put('idxg', np.tile(np.asarray(inp['idx_k_norm_g'][0], np.float32), 2)[:, None])
    put('idxb', np.tile(np.asarray(inp['idx_k_norm_b'][0], np.float32), 2)[:, None])
    cw = np.asarray(inp['conv_w'][0], np.float32)
    put('cw', np.concatenate([_pcol(cw[jj]) for jj in range(3)], 1))
    put('cb', _pcol(inp['conv_b'][0]))
    put('ag', _pcol(np.asarray(inp['attn_out_norm_g'][0]).reshape(-1)))
    put('cg', _pcol(np.asarray(inp['conv_out_norm_g'][0]).reshape(-1)))
    put('b1', _pcol(inp['b_mlp1'][0]))
    NEG = np.float32(-1.0e30)
    tri = np.where(np.arange(128)[None, :] <= np.arange(128)[:, None], np.float32(0), NEG).astype(np.float32)
    cm = np.zeros((128, 256), np.float32)
    if r == 0:
        cm[:, 0:128] = tri
        cm[:, 128:256] = NEG
    else:
        cm[:, 128:256] = tri
    rel = np.asarray(inp['rel_bias'], np.float32)
    sI = np.arange(128)[:, None]
    tI = np.arange(128)[None, :]
    idx0 = _t5_bucket(np.maximum(tI - sI, 0))
    idx1 = _t5_bucket(tI - sI + 128)
    idxf = np.full((128, 128), 31, np.int64)
    order = [idxf, idx0, idx1] if r == 0 else [idx0, idx1, idxf]
    bias3 = np.stack([np.transpose(rel[ix], (0, 2, 1)).reshape(128, 1024) for ix in order], 0).astype(np.float32)
    m = {
        'x_b': np.ascontiguousarray(x[b]), 'x_own': x_own, 'x_halo': x_halo, 'prm': prm,
        'hv': np.ascontiguousarray(np.broadcast_to(hvv[None, :], (128, 32))).astype(np.float32),
        'cmask': cm, 'bias3': bias3, 'rb31': np.ascontiguousarray(rel[31:32, :]),
        'ident': np.eye(128, dtype=np.float32), 'b2': np.ascontiguousarray(np.asarray(inp['b_mlp2'], np.float32).reshape(1, 2048)),
        'w_ada': np.asarray(inp['w_ada'][0], np.float32), 'w_in': np.asarray(inp['w_in'][0], np.float32),
        'w_uq': np.asarray(inp['w_uq'][0], np.float32), 'w_uk': np.asarray(inp['w_uk'][0], np.float32),
        'w_uv': np.asarray(inp['w_uv'][0], np.float32), 'w_iq': np.asarray(inp['w_iq'][0], np.float32),
        'w_out': np.asarray(inp['w_out'][0], np.float32), 'w1': np.asarray(inp['w_mlp1'][0], np.float32),
        'w2': np.asarray(inp['w_mlp2'][0], np.float32),
    }
    return m


_NC_CACHE = {}


def kernel(**inp):
    if 'nc' not in _NC_CACHE:
        _NC_CACHE['nc'] = build()
    nc = _NC_CACHE['nc']
    in_maps = [make_inputs(inp, i) for i in range(8)]
    res = run_bass_kernel_spmd(nc, in_maps, core_ids=list(range(8)))
    out = np.zeros((4, 4096, 2048), np.float32)
    for i in range(8):
        b, r = i // 2, i % 2
        o = res.results[i]['out']
        for j in range(16):
            q = 2 * j + r
            out[b, q * 128:(q + 1) * 128] = o[j * 128:(j + 1) * 128]
    return out
```

```python
import contextlib
import math
import os

import numpy as np
import concourse.bass as bass
import concourse.mybir as mybir
from concourse.bass_utils import run_bass_kernel_spmd

F32 = mybir.dt.float32
BF16 = mybir.dt.bfloat16
AF = mybir.ActivationFunctionType
ALU = mybir.AluOpType
AX = mybir.AxisListType

ENGS = ['pe', 'act', 'dve', 'pool', 'sp']
EPS = 1e-6
NITER = 22


class Op:
    __slots__ = ('eng', 'fn', 'deps', 'ms', 'is_dma', 'sem', 'semval', 'needed', 'grp', 'pos')


class Prog:
    def __init__(self, nc):
        self.nc = nc
        self.ops = {e: [] for e in ENGS}
        self.lastw = {}
        self.readers = {}
        self.dma_sems = {}
        self.esem = {}
        self.final_waits = []

    def dma_sem(self, key, total_mode=False):
        if key not in self.dma_sems:
            h = self.nc.alloc_semaphore(name="d_" + "_".join(str(k) for k in (key if isinstance(key, tuple) else (key,))))
            self.dma_sems[key] = [h, 0, total_mode]
        return self.dma_sems[key]

    def emit(self, eng, fn, reads=(), writes=(), dma_key=None, total_mode=False):
        o = Op()
        o.eng = eng
        o.fn = fn
        o.is_dma = dma_key is not None
        o.needed = False
        o.ms = 0
        o.grp = None
        deps = []
        for r in reads:
            w = self.lastw.get(r)
            if w is not None:
                deps.append(w)
            if isinstance(r, tuple) and r[0] == 'pb':
                deps.extend(x for x in self.readers.get(r, ()) if x.eng != eng)
        for w_ in writes:
            w = self.lastw.get(w_)
            if w is not None:
                deps.append(w)
            deps.extend(self.readers.get(w_, ()))
        best = {}
        dl = []
        for d in deps:
            if d.is_dma:
                if all(d is not q for q in dl):
                    dl.append(d)
                continue
            if d.eng == 'pe' and eng == 'pe' and not o.is_dma:
                continue
            b = best.get(d.eng)
            if b is None or d.pos > b.pos:
                best[d.eng] = d
        o.deps = dl + list(best.values())
        for r in reads:
            self.readers.setdefault(r, []).append(o)
        for w_ in writes:
            self.lastw[w_] = o
            self.readers[w_] = []
        if o.is_dma:
            s = self.dma_sem(dma_key, total_mode)
            if total_mode:
                for d in o.deps:
                    assert not (d.is_dma and d.grp is s), "total-mode DMA group has an internal dependency: %r" % (dma_key,)
            s[1] += 16
            o.sem = s[0]
            o.semval = s[1]
            o.grp = s
        o.pos = len(self.ops[eng])
        self.ops[eng].append(o)
        for d in o.deps:
            d.needed = True
        return o

    def finalize(self, block):
        nc = self.nc
        for e in ENGS:
            self.esem[e] = nc.alloc_semaphore(name="e_" + e)
        for e in ENGS:
            c = 0
            for o in self.ops[e]:
                if (not o.is_dma) and o.needed:
                    c += 1
                    o.ms = c
        bname = {'pe': 'tensor', 'act': 'scalar', 'dve': 'vector', 'pool': 'gpsimd', 'sp': 'sync'}
        stats = {}
        for e in ENGS:
            ops = self.ops[e]
            esem = self.esem
            final_waits = self.final_waits if e == 'sp' else []
            nwait = [0]

            def body(h, ops=ops, e=e, final_waits=final_waits, nwait=nwait):
                waited = {}
                for o in ops:
                    for d in o.deps:
                        if d.is_dma:
                            sem = d.sem
                            val = d.grp[1] if d.grp[2] else d.semval
                            k = ('d', id(d.grp))
                        else:
                            sem = esem[d.eng]
                            val = d.ms
                            k = ('e', d.eng)
                        if waited.get(k, 0) >= val:
                            continue
                        h.wait_ge(sem, val)
                        nwait[0] += 1
                        waited[k] = val
                    ins = o.fn(h)
                    if o.is_dma:
                        ins.then_inc(o.sem, 16)
                    elif o.needed:
                        ins.then_inc(esem[e], 1)
                for key in final_waits:
                    s = self.dma_sems[key]
                    h.wait_ge(s[0], s[1])
            getattr(block, bname[e])(body)
            stats[e] = (len(ops), nwait[0])
        return stats


PRM = {}
_o = 0
for _n, _w in [('c', 16), ('bada', 96), ('n1g', 16), ('n2g', 16), ('cqg', 4), ('kvg', 2), ('qag', 2),
               ('idxg', 1), ('idxb', 1), ('cw', 24), ('cb', 8), ('ag', 8), ('cg', 8), ('b1', 64)]:
    PRM[_n] = (_o, _o + _w)
    _o += _w
NPRM = _o


def _pcol(v):
    v = np.asarray(v, np.float32).reshape(-1, 128)
    return np.ascontiguousarray(v.T)


def build(stop=None, dbg=()):
    nc = bass.Bass("TRN2", target_bir_lowering=False)
    P = Prog(nc)
    E = P.emit

    def din(name, shape, dt=F32):
        return nc.dram_tensor(name, list(shape), dt, kind="ExternalInput").ap()

    x_b = din("x_b", [4096, 2048])
    x_own = din("x_own", [2048, 2048])
    x_halo = din("x_halo", [32, 2048])
    prm_d = din("prm", [128, NPRM])
    hv_d = din("hv", [128, 32])
    cmask_d = din("cmask", [128, 256])
    bias3_d = din("bias3", [3, 128, 1024])
    rb31_d = din("rb31", [1, 8])
    ident_d = din("ident", [128, 128])
    b2_d = din("b2", [1, 2048])
    w_ada = din("w_ada", [2048, 12288])
    w_in = din("w_in", [2048, 3920])
    w_uq = din("w_uq", [512, 1024])
    w_uk = din("w_uk", [8, 128, 256])
    w_uv = din("w_uv", [8, 256, 128])
    w_iq = din("w_iq", [512, 1024])
    w_out = din("w_out", [2048, 2048])
    w1 = din("w1", [2048, 8192])
    w2 = din("w2", [8192, 2048])
    out_d = nc.dram_tensor("out", [2048, 2048], F32, kind="ExternalOutput").ap()
    dbg_d = {}
    for name, shape in dbg:
        dbg_d[name] = nc.dram_tensor("dbg_" + name, list(shape), F32, kind="ExternalOutput").ap()

    def dscr(name, shape, dt=BF16):
        return nc.dram_tensor(name, list(shape), dt, kind="Internal").ap()

    s_win = dscr("s_win", [2048, 3920])
    s_wout = dscr("s_wout", [2048, 2048])
    s_w1 = dscr("s_w1", [2048, 8192])
    s_w2 = dscr("s_w2", [8192, 2048])
    s_wuq = dscr("s_wuq", [512, 1024])
    s_wiq = dscr("s_wiq", [512, 1024])
    gsc = dscr("gsc", [2, 2048], F32)

    with contextlib.ExitStack() as es:
        def SB(name, shape, dt):
            return es.enter_context(nc.sbuf_tensor("sb_" + name, list(shape), dt))

        ckvT = SB("ckvT", [128, 2, 4096], BF16)
        ckvk = SB("ckvk", [128, 32, 256], BF16)
        kx = SB("kx", [128, 4096], BF16)
        ident = SB("identb", [128, 128], BF16)
        ident4 = SB("ident4", [128, 512], BF16)
        ones = SB("ones", [128, 128], BF16)
        bd64 = SB("bd64", [128, 128], BF16)
        cst = SB("cst", [128, 8], F32)
        prm = SB("prm", [128, NPRM], F32)
        modT = SB("modT", [128, 96], F32)
        AA = SB("AA", [128, 32], F32)
        cact = SB("cact", [128, 16], BF16)
        biasb = SB("biasb", [128, 3, 1024], BF16)
        rb31 = SB("rb31", [128, 8], F32)
        cmask = SB("cmask", [128, 256], F32)
        wuk = SB("wuk", [128, 8, 256], BF16)
        wuv = SB("wuv", [128, 8, 2, 128], BF16)
        widxw = SB("widxw", [128, 16, 16], BF16)
        uh = SB("uh", [128, 8, 16, 2], F32)
        hv = SB("hv", [128, 32], F32)
        ring = [SB("ring%d" % i, [128, 8192], BF16) for i in range(2)]
        xo = SB("xo", [128, 2, 2048], F32)
        hT = SB("hT", [128, 16, 256], BF16)
        ymix = SB("ymix", [128, 16, 256], BF16)
        tf = SB("tf", [128, 3, 1024], F32)
        tb = SB("tb", [128, 3, 1024], BF16)
        rbuf = SB("rbuf", [128, 2, 512], F32)
        pTb = SB("pTb", [128, 2, 512], BF16)
        iqT = SB("iqT", [128, 8, 128], BF16)
        qa = SB("qa", [128, 2, 1024], BF16)
        cqT = SB("cqT", [128, 4, 256], BF16)
        ubuf = SB("ubuf", [128, 2, 130], F32)
        ycb = SB("ycb", [128, 256], F32)
        cols = SB("cols", [128, 64], F32)
        pw2 = SB("pw2", [128, NITER + 1], F32)
        wab = SB("wab", [128, 2, 16], F32)
        wsg = SB("wsg", [128, 2, 16], F32)
        ovl = SB("ovl", [128, 16384], BF16)
        score = ovl[:, 0:8192].bitcast(F32)
        mb = ovl[:, 8192:12288]
        gcb = ovl[:, 12288:16384].bitcast(F32).rearrange("p (g t) -> p g t", g=8)
        aT = ovl[:, :].rearrange("p (f t) -> p f t", f=64)
        wk = ovl[:, 0:6144].rearrange("p (k n) -> p k n", k=16)
        pb = [es.enter_context(nc.psum_tensor("pb%d" % i, [128, 512], F32)) for i in range(8)]
        block = es.enter_context(nc.Block())

        def pbf(i):
            return pb[i][:, :].bitcast(BF16)

        def TF(i, a=0, b=1024):
            return [('tf', i, q) for q in range(a // 256, (b + 255) // 256)]

        def TB(i, a=0, b=1024):
            return [('tb', i, q) for q in range(a // 256, (b + 255) // 256)]

        def PB(i, a=0, b=512):
            return [('pb', i)]

        def COL(*idx):
            return [('col', i) for i in idx]

        def mm(out, lhsT, rhs, start, stop, reads, writes):
            E('pe', lambda h: h.matmul(out, lhsT=lhsT, rhs=rhs, start=start, stop=stop), reads=reads, writes=writes)

        def tr(out, in_, idn, reads, writes):
            E('pe', lambda h: h.transpose(out, in_, idn), reads=reads, writes=writes)

        def act(out, in_, func, reads, writes, **kw):
            E('act', lambda h: h.activation(out=out, in_=in_, func=func, **kw), reads=reads, writes=writes)

        def ts(eng, out, in0, s1, s2, op0, op1, reads, writes, accum_out=None):
            def f(h):
                if op1 is None:
                    return h.tensor_scalar(out=out, in0=in0, scalar1=s1, scalar2=None, op0=op0)
                if accum_out is not None:
                    return h.tensor_scalar(out=out, in0=in0, scalar1=s1, scalar2=s2, op0=op0, op1=op1, accum_out=accum_out)
                return h.tensor_scalar(out=out, in0=in0, scalar1=s1, scalar2=s2, op0=op0, op1=op1)
            E(eng, f, reads=reads, writes=writes)

        def stt(out, in0, scalar, in1, op0, op1, reads, writes):
            E('dve', lambda h: h.scalar_tensor_tensor(out=out, in0=in0, scalar=scalar, in1=in1, op0=op0, op1=op1),
              reads=reads, writes=writes)

        def tt(eng, out, in0, in1, op, reads, writes):
            E(eng, lambda h: h.tensor_tensor(out=out, in0=in0, in1=in1, op=op), reads=reads, writes=writes)

        def cp(eng, out, in_, reads, writes):
            if eng == 'act':
                act(out, in_, AF.Copy, reads, writes)
            else:
                E(eng, lambda h: h.tensor_copy(out=out, in_=in_), reads=reads, writes=writes)

        def dma(eng, out, in_, reads, writes, key, total=False, slow=False):
            if slow:
                return E(eng, lambda h: h.dma_start(out=out, in_=in_, allow_slow_non_contiguous=True), reads=reads, writes=writes, dma_key=key, total_mode=total)
            return E(eng, lambda h: h.dma_start(out=out, in_=in_), reads=reads, writes=writes, dma_key=key, total_mode=total)

        def rsqrt(dst, src, scale, np_, reads, writes):
            act(dst, src, AF.Ln, reads, writes, scale=scale, bias=cst[0:np_, 0:1])
            act(dst, dst, AF.Exp, writes, writes, scale=-0.5)

        def dump(name, src, keys):
            if name in dbg_d:
                dma('pool', dbg_d[name], src, keys, [('dbg', name)], ('dbg',), total=True)

        def pc(name, i=None):
            a, b = PRM[name]
            if i is None:
                return prm[:, a:b]
            return prm[:, a + i:a + i + 1]

        class _Stop(Exception):
            pass

        def checkpoint(name):
            if stop == name:
                raise _Stop()

        def setup_dma(out, in_, key, eng='sp'):
            dma(eng, out, in_, [], [key], ('setup',), total=True)

        def emit_all():
            setup_dma(prm[:, :], prm_d, ('prm',))
            setup_dma(hv[:, :], hv_d, ('hv',))
            setup_dma(cmask[:, :], cmask_d, ('cmask',))
            setup_dma(rb31[:, :], rb31_d.partition_broadcast(128), ('rb31',))
            setup_dma(tf[:, 0, 0:128], ident_d, TF(0, 0, 128)[0])
            E('dve', lambda h: h.memset(cst[:, 0:1], EPS), writes=[('cst',)])
            E('dve', lambda h: h.memset(cst[:, 1:2], 0.5), writes=[('cst',)])
            for k in range(NITER + 1):
                E('dve', lambda h, k=k: h.memset(pw2[:, k:k + 1], 2.0 ** -(k + 1)), writes=[('pw2',)])
            E('dve', lambda h: h.memset(ones[:, :], 1.0), writes=[('ones',)])
            E('dve', lambda h: h.memset(bd64[:, :], 0.0), writes=[('bd64',)])
            E('dve', lambda h: h.memset(bd64[0:64, 0:64], 1.0 / 64), writes=[('bd64',)])
            E('dve', lambda h: h.memset(bd64[64:128, 64:128], 1.0 / 64), writes=[('bd64',)])
            cp('dve', ident[:, :], tf[:, 0, 0:128], TF(0, 0, 128), [('ident',)])
            for i in range(4):
                cp('dve', ident4[:, i * 128:(i + 1) * 128], tf[:, 0, 0:128], TF(0, 0, 128), [('ident4',)])
            dma('pool', wuk[:, :, :], w_uk.rearrange("h d c -> d h c"), [], [('wuk',)], ('setup2',), total=True)
            dma('pool', wuv[:, :, :, :], w_uv.rearrange("h (cc c) v -> c h cc v", cc=2), [], [('wuv',)], ('setup2',), total=True)
            dma('pool', widxw[:, :, :], w_in[:, 832:848].rearrange("(kc p) n -> p kc n", p=128), [], [('widxw',)], ('setup2',), total=True)
            for bi in range(3):
                dma('sp', tf[:, 1, :], bias3_d[bi], [], TF(1), ('bld',))
                tt('dve', tf[:, 1, :].rearrange("p (h t) -> p h t", h=8), tf[:, 1, :].rearrange("p (h t) -> p h t", h=8),
                   rb31[:, :].unsqueeze(2).to_broadcast([128, 8, 128]), ALU.subtract, TF(1) + [('rb31',)], TF(1))
                ts('dve', biasb[:, bi, :], tf[:, 1, :], 16.0, None, ALU.mult, None, TF(1), [('biasb',)])

            checkpoint('S')
            castkeys = {}
            ring_ctr = [0]

            def ring_view(slot, kc, n):
                return ring[slot][:, 0:kc * n].rearrange("p (k n) -> p k n", k=kc)

            def load_w(src, scr, r0, r1, c0, c1, tag):
                slot = ring_ctr[0] % 2
                ring_ctr[0] += 1
                kc = (r1 - r0) // 128
                n = c1 - c0
                view = ring_view(slot, kc, n)
                key = ('ring', slot)
                if scr is None:
                    dma('pool', view, src[r0:r1, c0:c1].rearrange("(k p) n -> p k n", p=128), [], [key], ('rl', slot))
                else:
                    dma('sp', view, scr[r0:r1, c0:c1].rearrange("(k p) n -> p k n", p=128), castkeys[tag], [key], ('rl', slot))
                return view, key

            def cast_weight(src, scr, tag, rows, chunk):
                for r0 in range(0, rows, chunk):
                    dma('pool', scr[r0:r0 + chunk, :], src[r0:r0 + chunk, :], [], [('scrw', tag, r0)], ('cast', tag), total=True)
                castkeys[tag] = [('scrw', tag, r0) for r0 in range(0, rows, chunk)]

            act(cact[:, :], pc('c'), AF.Silu, [('prm',)], [('cact',)])

            def mod_tile(nt):
                view, key = load_w(w_ada, None, 0, 2048, nt * 512, (nt + 1) * 512, 'ada')
                for mt in range(4):
                    m = nt * 4 + mt
                    for kc in range(16):
                        mm(pb[7][:, m:m + 1], view[:, kc, mt * 128:(mt + 1) * 128], cact[:, kc:kc + 1], kc == 0, kc == 15,
                           [key, ('cact',)], PB(7, 0, 96))
                tt('dve', modT[:, nt * 4:nt * 4 + 4], pb[7][:, nt * 4:nt * 4 + 4], prm[:, PRM['bada'][0] + nt * 4:PRM['bada'][0] + nt * 4 + 4],
                   ALU.add, PB(7, 0, 96) + [('prm',)], [('modT', nt)])

            for nt in range(8):
                mod_tile(nt)
            stt(AA[:, 0:16], modT[:, 16:32], 1.0, pc('n1g'), ALU.add, ALU.mult,
                [('modT', i) for i in range(4, 8)] + [('prm',)], [('A1',)])
            A1K = [('A1',)] + [('modT', i) for i in range(0, 4)]
            if stop == '0':
                dump('modT', modT[:, :], [('modT', i) for i in range(8)])
            checkpoint('0')

            def rms_T(src, np_, dstT, c0, Acol, Bcol, srckeys, dstkey, abkeys, trb):
                xh = tb[0:np_, 0:2, :].rearrange("p a b -> p (a b)")
                jk = mb[0:np_, 0:2048]
                act(jk, src, AF.Square, srckeys + [('mb',)], [('mb',)] + COL(0), accum_out=cols[0:np_, 0:1])
                rsqrt(cols[0:np_, 1:2], cols[0:np_, 0:1], 1.0 / 2048, np_, COL(0), COL(1))
                ts('dve', xh, src, cols[0:np_, 1:2], None, ALU.mult, None, srckeys + COL(1), TB(0) + TB(1))
                for half in range(2):
                    bank = trb[half]
                    for j in range(8):
                        kc = half * 8 + j
                        tr(pbf(bank)[:, j * 128:j * 128 + np_], xh[:, kc * 128:(kc + 1) * 128], ident[0:np_, 0:np_],
                           TB(0) + TB(1) + [('ident',)], PB(bank))
                    for j in range(8):
                        kc = half * 8 + j
                        o = dstT[:, kc, c0:c0 + np_]
                        i_ = pbf(bank)[:, j * 128:j * 128 + np_]
                        if j % 2 == 0:
                            act(o, i_, AF.Identity, PB(bank) + abkeys, [dstkey], scale=Acol(kc), bias=Bcol(kc))
                        else:
                            ts('dve', o, i_, Acol(kc), Bcol(kc), ALU.mult, ALU.add, PB(bank) + abkeys, [dstkey])

            A1c = lambda kc: AA[:, kc:kc + 1]
            B1c = lambda kc: modT[:, kc:kc + 1]
            A2c = lambda kc: AA[:, 16 + kc:17 + kc]
            B2c = lambda kc: modT[:, 48 + kc:49 + kc]

            dma('pool', wk[:, :, 0:320], w_in[:, 512:832].rearrange("(k p) n -> p k n", p=128), [], [('wk', 0)], ('setup2',), total=True)
            dma('pool', wk[:, :, 320:384], w_in[:, 768:832].rearrange("(k p) n -> p k n", p=128), [], [('wk', 1)], ('setup2',), total=True)
            cast_weight(w_in, s_win, 'win', 2048, 512)
            cast_weight(w_uq, s_wuq, 'wuq', 512, 512)
            cast_weight(w_iq, s_wiq, 'wiq', 512, 512)
            cast_weight(w_out, s_wout, 'wout', 2048, 1024)
            checkpoint('A0')
            NTA = 16
            for ta in range(NTA):
                for blk in range(2):
                    g = ta * 2 + blk
                    dma('sp', xo[:, blk, :], x_b[g * 128:(g + 1) * 128, :], [], [('xo', blk)], ('xl', blk))
                    rms_T(xo[:, blk, :], 128, hT, blk * 128, A1c, B1c, [('xo', blk)], ('hT', blk), A1K, (5, 6))
                    checkpoint('A1_%d' % ta)
                hk = [('hT', 0), ('hT', 1)]
                for mt in range(3):
                    for kc in range(16):
                        mm(pb[mt][:, 0:256], wk[:, kc, mt * 128:(mt + 1) * 128], hT[:, kc, :], kc == 0, kc == 15,
                           hk + [('wk', 0), ('wk', 1)], PB(mt, 0, 256))
                t0 = ta * 256
                checkpoint('A2_%d' % ta)
                for cc in range(2):
                    act(tb[:, 2, cc * 256:(cc + 1) * 256], pb[cc][:, 0:256], AF.Square, PB(cc, 0, 256), TB(2, cc * 256, cc * 256 + 256))
                for cc in range(2):
                    mm(pb[3][:, 0:256], ones[:, :], tb[:, 2, cc * 256:(cc + 1) * 256], cc == 0, cc == 1,
                       TB(2, cc * 256, cc * 256 + 256) + [('ones',)], PB(3, 0, 256))
                rsqrt(tf[:, 0, 0:256], pb[3][:, 0:256], 1.0 / 256, 128, PB(3, 0, 256), TF(0, 0, 256))
                for cc in range(2):
                    stt(ckvT[:, cc, t0:t0 + 256], pb[cc][:, 0:256], pc('kvg', cc), tf[:, 0, 0:256], ALU.mult, ALU.mult,
                        PB(cc, 0, 256) + TF(0, 0, 256) + [('prm',)], [('ckvT', ta)])
                checkpoint('A3_%d' % ta)
                for blk in range(2):
                    for cc in range(2):
                        j = blk * 2 + cc
                        tr(pbf(4)[:, j * 128:(j + 1) * 128], ckvT[:, cc, t0 + blk * 128:t0 + (blk + 1) * 128], ident[:, :],
                           [('ckvT', ta), ('ident',)], PB(4, 0, 256))
                cp('act', ckvk[:, 2 * ta:2 * ta + 2, :], pbf(4)[:, 0:512].rearrange("p (b c) -> p b c", b=2), PB(4, 0, 256), [('ckvk', ta)])
                checkpoint('A4_%d' % ta)
                cp('act', tf[:, 1, 0:256], pb[2][:, 0:256], PB(2, 0, 256), TF(1, 0, 256))
                cp('act', tb[:, 2, 0:256], tf[:, 1, 0:256], TF(1, 0, 256), TB(2, 0, 256))
                tt('dve', tb[:, 2, 256:512], tf[:, 1, 0:256], tb[:, 2, 0:256], ALU.subtract, TF(1, 0, 256) + TB(2, 0, 256), TB(2, 256, 512))
                mm(pb[3][:, 256:512], bd64[:, :], tb[:, 2, 0:256], True, False, TB(2, 0, 256) + [('bd64',)], PB(3, 256, 512))
                mm(pb[3][:, 256:512], bd64[:, :], tb[:, 2, 256:512], False, True, TB(2, 256, 512) + [('bd64',)], PB(3, 256, 512))
                tt('dve', tf[:, 1, 256:512], tf[:, 1, 0:256], pb[3][:, 256:512], ALU.subtract, TF(1, 0, 256) + PB(3, 256, 512), TF(1, 256, 512))
                act(tf[:, 1, 512:768], tf[:, 1, 256:512], AF.Square, TF(1, 256, 512), TF(1, 512, 768))
                cp('act', tb[:, 2, 512:768], tf[:, 1, 512:768], TF(1, 512, 768), TB(2, 512, 768))
                tt('dve', tb[:, 2, 768:1024], tf[:, 1, 512:768], tb[:, 2, 512:768], ALU.subtract, TF(1, 512, 768) + TB(2, 512, 768), TB(2, 768, 1024))
                mm(pb[4][:, 256:512], bd64[:, :], tb[:, 2, 512:768], True, False, TB(2, 512, 768) + [('bd64',)], PB(4, 256, 512))
                mm(pb[4][:, 256:512], bd64[:, :], tb[:, 2, 768:1024], False, True, TB(2, 768, 1024) + [('bd64',)], PB(4, 256, 512))
                rsqrt(tf[:, 1, 768:1024], pb[4][:, 256:512], 1.0, 128, PB(4, 256, 512), TF(1, 768, 1024))
                tt('dve', tf[:, 2, 0:256], tf[:, 1, 256:512], tf[:, 1, 768:1024], ALU.mult, TF(1, 256, 512) + TF(1, 768, 1024), TF(2, 0, 256))
                act(kx[:, t0:t0 + 256], tf[:, 2, 0:256], AF.Identity, TF(2, 0, 256) + [('prm',)], [('kx', ta)],
                    scale=pc('idxg', 0), bias=pc('idxb', 0))
                checkpoint('A5_%d' % ta)
                if ta < 8:
                    mod_tile(8 + 2 * ta)
                    mod_tile(9 + 2 * ta)
                checkpoint('A6_%d' % ta)
            cast_weight(w1, s_w1, 'w1', 2048, 256)
            cast_weight(w2, s_w2, 'w2', 8192, 1024)
            ALLMOD = [('modT', i) for i in range(24)]
            stt(AA[:, 16:32], modT[:, 64:80], 1.0, pc('n2g'), ALU.add, ALU.mult, ALLMOD + [('prm',)], [('A2',)])
            A2K = [('A2',)] + ALLMOD
            dma('sp', gsc[0].rearrange("(m p) -> p m", p=128), modT[:, 32:48], ALLMOD, [('gsc', 0)], ('gsc',), total=True, slow=True)
            dma('sp', gsc[1].rearrange("(m p) -> p m", p=128), modT[:, 80:96], ALLMOD, [('gsc', 1)], ('gsc',), total=True, slow=True)
            KEYS_ALL = [('ckvT', i) for i in range(NTA)] + [('ckvk', i) for i in range(NTA)] + [('kx', i) for i in range(NTA)]
            dump('ckvT', ckvT[:, 0, 0:2048], KEYS_ALL)
            dump('kx', kx[:, 0:2048], KEYS_ALL)
            dump('modT', modT[:, :], ALLMOD)

            OVK_A = [('score',), ('mb',), ('wk', 0), ('wk', 1)] + [('gcb', g) for g in range(8)]
            OVK_M = [('aT', f) for f in range(64)]

            def barrier(keys):
                E('pool', lambda h: h.memset(cols[:, 63:64], 0.0), reads=[], writes=list(keys) + COL(63))

            if stop != 'A':
                barrier(OVK_A + OVK_M)
                dma('sp', xo[0:32, 0, :], x_halo, [], [('xo', 0)], ('xl', 0))
                rms_T(xo[0:32, 0, :], 32, hT, 0, A1c, B1c, [('xo', 0)], ('hT', 0), A1K, (5, 6))
                for part in range(2):
                    for half in range(2):
                        c0 = 1872 + part * 1024 + half * 512
                        view, key = load_w(w_in, s_win, 0, 2048, c0, c0 + 512, 'win')
                        for gg in range(4):
                            g = half * 4 + gg
                            bank = gg % 2
                            for kc in range(16):
                                mm(pb[bank][:, 0:32], view[:, kc, gg * 128:(gg + 1) * 128], hT[:, kc, 0:32], kc == 0, kc == 15,
                                   [key, ('hT', 0)], PB(bank, 0, 32))
                            uv = uh[:, g, :, :].rearrange("p s j -> p (s j)")
                            if part == 0:
                                tt('dve', uv, pb[bank][:, 0:32], hv[:, :], ALU.mult, PB(bank, 0, 32) + [('hv',)], [('uh', g)])
                            else:
                                tt('dve', uv, pb[bank][:, 0:32], uv, ALU.mult, PB(bank, 0, 32) + [('uh', g)], [('uh', g)])
                dump('uh', uh[:, :, :, :].rearrange("p g s j -> p (g s j)"), [('uh', g) for g in range(8)])

            checkpoint('H')
            NT = 0 if stop == 'A' else (int(stop[1:].rstrip('a')) if (stop and stop[0] == 'T') else 8)
            for ti in range(NT):
                last_dbg = (ti == NT - 1)
                barrier(OVK_A + OVK_M)
                for blk in range(2):
                    r0 = ti * 256 + blk * 128
                    dma('sp', xo[:, blk, :], x_own[r0:r0 + 128, :], [], [('xo', blk)], ('xl', blk))
                    rms_T(xo[:, blk, :], 128, hT, blk * 128, A1c, B1c, [('xo', blk)], ('hT', blk), A1K, (5, 6))
                hk = [('hT', 0), ('hT', 1)]
                checkpoint('C0a_%d' % ti)
                view, key = load_w(w_in, s_win, 0, 2048, 0, 512, 'win')
                for mt in range(4):
                    for kc in range(16):
                        mm(pb[mt][:, 0:256], view[:, kc, mt * 128:(mt + 1) * 128], hT[:, kc, :], kc == 0, kc == 15, hk + [key], PB(mt, 0, 256))
                checkpoint('C0b_%d' % ti)
                for mt in range(4):
                    act(tb[:, 2, mt * 256:(mt + 1) * 256], pb[mt][:, 0:256], AF.Square, PB(mt, 0, 256), TB(2, mt * 256, mt * 256 + 256))
                    cp('act', tf[:, 0, mt * 256:(mt + 1) * 256], pb[mt][:, 0:256], PB(mt, 0, 256), TF(0, mt * 256, mt * 256 + 256))
                for mt in range(4):
                    mm(pb[4][:, 0:256], ones[:, :], tb[:, 2, mt * 256:(mt + 1) * 256], mt == 0, mt == 3, TB(2, mt * 256, mt * 256 + 256) + [('ones',)], PB(4, 0, 256))
                rsqrt(tf[:, 1, 0:256], pb[4][:, 0:256], 1.0 / 512, 128, PB(4, 0, 256), TF(1, 0, 256))
                for mt in range(4):
                    stt(cqT[:, mt, :], tf[:, 0, mt * 256:(mt + 1) * 256], pc('cqg', mt), tf[:, 1, 0:256], ALU.mult, ALU.mult,
                        TF(0, mt * 256, mt * 256 + 256) + TF(1, 0, 256) + [('prm',)], [('cqT',)])
                checkpoint('C1_%d' % ti)
                for blk in range(2):
                    for kc in range(16):
                        mm(pb[4][:, 256 + blk * 16:256 + (blk + 1) * 16], hT[:, kc, blk * 128:(blk + 1) * 128], widxw[:, kc, :], kc == 0, kc == 15,
                           hk + [('widxw',)], PB(4, 256, 288))
                act(wab[:, :, :], pb[4][:, 256:288].rearrange("p (b h) -> p b h", b=2), AF.Abs, PB(4, 256, 288), [('wab',)], scale=1.0 / 32)
                act(wsg[:, :, :], pb[4][:, 256:288].rearrange("p (b h) -> p b h", b=2), AF.Sign, PB(4, 256, 288), [('wsg',)])
                checkpoint('C2_%d' % ti)
                for half in range(2):
                    c0 = 1872 + half * 512
                    view, key = load_w(w_in, s_win, 0, 2048, c0, c0 + 512, 'win')
                    for gg in range(4):
                        g = half * 4 + gg
                        bank = gg % 2
                        for kc in range(16):
                            mm(pb[bank][:, 0:256], view[:, kc, gg * 128:(gg + 1) * 128], hT[:, kc, :], kc == 0, kc == 15, hk + [key], PB(bank, 0, 256))
                        cp('act', gcb[:, g, :], pb[bank][:, 0:256], PB(bank, 0, 256), [('gcb', g)])
                cwa, _ = PRM['cw']
                for half in range(2):
                    c0 = 2896 + half * 512
                    view, key = load_w(w_in, s_win, 0, 2048, c0, c0 + 512, 'win')
                    for gg in range(4):
                        g = half * 4 + gg
                        bank = gg % 2
                        for kc in range(16):
                            mm(pb[bank][:, 0:256], view[:, kc, gg * 128:(gg + 1) * 128], hT[:, kc, :], kc == 0, kc == 15, hk + [key], PB(bank, 0, 256))
                        tt('dve', ubuf[:, :, 2:130], pb[bank][:, 0:256].rearrange("p (b t) -> p b t", b=2),
                           gcb[:, g, :].rearrange("p (b t) -> p b t", b=2), ALU.mult, PB(bank, 0, 256) + [('gcb', g)], [('ubuf',)])
                        cp('pool', ubuf[:, :, 0:2], uh[:, g, 2 * ti:2 * ti + 2, :], [('uh', g)], [('ubuf', 'h')])
                        gv = gcb[:, g, :].rearrange("p (b t) -> p b t", b=2)
                        act(gv, ubuf[:, :, 2:130], AF.Identity, [('ubuf',), ('prm',)], [('gcb', g)],
                            scale=prm[:, cwa + 16 + g:cwa + 17 + g], bias=pc('cb', g))
                        stt(gv, ubuf[:, :, 1:129], prm[:, cwa + 8 + g:cwa + 9 + g], gv, ALU.mult, ALU.add,
                            [('ubuf',), ('ubuf', 'h'), ('gcb', g), ('prm',)], [('gcb', g)])
                        stt(gv, ubuf[:, :, 0:128], prm[:, cwa + g:cwa + g + 1], gv, ALU.mult, ALU.add,
                            [('ubuf',), ('ubuf', 'h'), ('gcb', g), ('prm',)], [('gcb', g)])
                for half in range(2):
                    c0 = 848 + half * 512
                    view, key = load_w(w_in, s_win, 0, 2048, c0, c0 + 512, 'win')
                    for gg in range(4):
                        g = half * 4 + gg
                        bank = gg % 2
                        for kc in range(16):
                            mm(pb[bank][:, 0:256], view[:, kc, gg * 128:(gg + 1) * 128], hT[:, kc, :], kc == 0, kc == 15, hk + [key], PB(bank, 0, 256))
                        tt('dve', ycb[:, :], pb[bank][:, 0:256], gcb[:, g, :], ALU.mult, PB(bank, 0, 256) + [('gcb', g)], [('ycb',)])
                        act(tb[:, 2, 0:256], ycb[:, :], AF.Square, [('ycb',)], TB(2, 0, 256))
                        mm(pb[2 + bank][:, 0:256], ones[:, :], tb[:, 2, 0:256], True, True, TB(2, 0, 256) + [('ones',)], PB(2 + bank, 0, 256))
                        rsqrt(tf[:, 1, 256:512], pb[2 + bank][:, 0:256], 1.0 / 128, 128, PB(2 + bank, 0, 256), TF(1, 256, 512))
                        stt(ymix[:, 8 + g, :], ycb[:, :], pc('cg', g), tf[:, 1, 256:512], ALU.mult, ALU.mult,
                            [('ycb',)] + TF(1, 256, 512) + [('prm',)], [('ymix', 8 + g)])
                checkpoint('C3_%d' % ti)
                if last_dbg:
                    dump('cqT', cqT[:, 0, :], [('cqT',)])
                    dump('yconv', ymix[:, 8, :], [('ymix', 8)])
                vq, kq = load_w(w_uq, s_wuq, 0, 512, 0, 1024, 'wuq')
                vi, ki = load_w(w_iq, s_wiq, 0, 512, 0, 1024, 'wiq')
                for blk in range(2):
                    j = 2 * ti + blk
                    nkc = 2 * j + 2
                    nk = nkc * 128
                    tsl = slice(blk * 128, (blk + 1) * 128)
                    for half in range(2):
                        for pp in range(4):
                            pr = half * 4 + pp
                            for kc in range(4):
                                mm(pb[half][:, pp * 128:(pp + 1) * 128], vi[:, kc, pr * 128:(pr + 1) * 128], cqT[:, kc, tsl], kc == 0, kc == 3,
                                   [ki, ('cqT',)], PB(half))
                        cp('act', iqT[:, half * 4:half * 4 + 4, :], pb[half][:, :].rearrange("p (a t) -> p a t", a=4), PB(half), [('iqT',)])
                    qT = tb[:, 2, :]
                    for half in range(2):
                        for hh in range(4):
                            h_ = half * 4 + hh
                            for kc in range(4):
                                mm(pb[2 + half][:, hh * 128:(hh + 1) * 128], vq[:, kc, h_ * 128:(h_ + 1) * 128], cqT[:, kc, tsl], kc == 0, kc == 3,
                                   [kq, ('cqT',)], PB(2 + half))
                        cp('dve', qT[:, half * 512:(half + 1) * 512], pb[2 + half][:, :], PB(2 + half), TB(2, half * 512, half * 512 + 512))
                    QK = TB(2)
                    for cc in range(2):
                        for h_ in range(8):
                            bank = 4 + cc * 2 + h_ // 4
                            mm(pb[bank][:, (h_ % 4) * 128:(h_ % 4 + 1) * 128], wuk[:, h_, cc * 128:(cc + 1) * 128], qT[:, h_ * 128:(h_ + 1) * 128],
                               True, True, QK + [('wuk',)], PB(bank, (h_ % 4) * 128, (h_ % 4) * 128 + 128))
                    sq = tb[:, 0:2, :]
                    for cc in range(2):
                        for hf in range(2):
                            bank = 4 + cc * 2 + hf
                            act(sq[:, cc, hf * 512:(hf + 1) * 512], pb[bank][:, :], AF.Square, PB(bank), TB(cc, hf * 512, hf * 512 + 512))
                    for hf in range(2):
                        for cc in range(2):
                            mm(pb[hf][:, :], ones[:, :], sq[:, cc, hf * 512:(hf + 1) * 512], cc == 0, cc == 1, TB(cc, hf * 512, hf * 512 + 512) + [('ones',)], PB(hf))
                        rsqrt(tf[:, 0, hf * 512:(hf + 1) * 512], pb[hf][:, :], 1.0 / 256, 128, PB(hf), TF(0, hf * 512, hf * 512 + 512))
                    for cc in range(2):
                        for hf in range(2):
                            bank = 4 + cc * 2 + hf
                            stt(qa[:, cc, hf * 512:(hf + 1) * 512], pb[bank][:, :], pc('qag', cc), tf[:, 0, hf * 512:(hf + 1) * 512], ALU.mult, ALU.mult,
                                PB(bank) + TF(0, hf * 512, hf * 512 + 512) + [('prm',)], [('qa', hf)])
                    checkpoint('Q_%d_%d' % (ti, blk))
                    cnt = 0
                    for k0 in range(0, nk, 512):
                        kw = min(512, nk - k0)
                        for h_ in range(16):
                            hp = h_ % 2
                            pr = h_ // 2
                            bank = 2 + (cnt % 2)
                            rb = cnt % 2
                            cnt += 1
                            kta = [('kx', i) for i in range(k0 // 256, (k0 + kw) // 256)]
                            mm(pb[bank][:, 0:kw], iqT[hp * 64:(hp + 1) * 64, pr, :], kx[hp * 64:(hp + 1) * 64, k0:k0 + kw], True, True,
                               [('iqT',)] + kta, PB(bank, 0, kw))
                            act(rbuf[:, rb, 0:kw], pb[bank][:, 0:kw], AF.Relu, PB(bank, 0, kw) + [('wab',)], [('rbuf', rb)], scale=wab[:, blk, h_:h_ + 1])
                            if h_ == 0:
                                ts('dve', score[:, k0:k0 + kw], rbuf[:, rb, 0:kw], wsg[:, blk, 0:1], None, ALU.mult, None,
                                   [('rbuf', rb), ('wsg',)], [('score',)])
                            else:
                                stt(score[:, k0:k0 + kw], rbuf[:, rb, 0:kw], wsg[:, blk, h_:h_ + 1], score[:, k0:k0 + kw], ALU.mult, ALU.add,
                                    [('rbuf', rb), ('wsg',), ('score',)], [('score',)])
                    checkpoint('X_%d_%d' % (ti, blk))
                    lo = cols[:, 8:9]
                    hi = cols[:, 9:10]
                    mid = cols[:, 10:11]
                    cntc = cols[:, 11:12]
                    ge = cols[:, 12:13]
                    d1 = cols[:, 13:14]
                    d2 = cols[:, 14:15]
                    d1 = cols[:, 14:15]
                    halfs = cols[:, 16:16 + NITER + 1]
                    tcol = cols[:, 13:14]
                    if j >= 1:
                        E('dve', lambda h, nk=nk: h.tensor_reduce(out=hi, in_=score[:, 0:nk], axis=AX.X, op=ALU.max), reads=[('score',)], writes=COL(9))
                        E('dve', lambda h, nk=nk: h.tensor_reduce(out=lo, in_=score[:, 0:nk], axis=AX.X, op=ALU.min), reads=[('score',)], writes=COL(8))
                        ts('dve', lo, lo, -1.0, None, ALU.add, None, COL(8), COL(8))
                        tt('dve', d1, hi, lo, ALU.subtract, COL(8, 9), COL(14))
                        ts('dve', halfs, pw2[:, :], d1, None, ALU.mult, None, COL(14) + [('pw2',)], COL(16))
                        tt('dve', mid, lo, halfs[:, 0:1], ALU.add, COL(8, 16), COL(10))
                    tt('dve', score[:, nk - 256:nk], score[:, nk - 256:nk], cmask[:, :], ALU.add, [('score',), ('cmask',)], [('score',)])
                    if j >= 1:
                        for it in range(NITER):
                            ts('dve', mb[:, 0:nk], score[:, 0:nk], mid, 0.0, ALU.is_gt, ALU.add, [('score',), ('mb',)] + COL(10),
                               [('mb',)] + COL(11), accum_out=cntc)
                            ts('dve', tcol, cntc, 255.5, halfs[:, it:it + 1], ALU.is_ge, ALU.mult, COL(11, 16), COL(13))
                            stt(mid, mid, halfs[:, it + 1:it + 2], tcol, ALU.subtract, ALU.add, COL(10, 16, 13), COL(10))
                        ts('dve', lo, mid, halfs[:, NITER:NITER + 1], None, ALU.subtract, None, COL(10, 16), COL(8))
                        ts('dve', mb[:, 0:nk], score[:, 0:nk], lo, -30000.0, ALU.is_le, ALU.mult, [('score',), ('mb',)] + COL(8), [('mb',)])
                    else:
                        ts('dve', mb[:, 0:nk], score[:, 0:nk], -1.0e29, -30000.0, ALU.is_le, ALU.mult, [('score',), ('mb',)], [('mb',)])
                    checkpoint('M_%d_%d' % (ti, blk))
                    if last_dbg and blk == 1:
                        dump('score', score[:, 0:512], [('score',)])
                        dump('thr', cols[:, 0:16], COL(8, 9, 11))
                    for hg in range(2):
                        hs = slice(hg * 512, (hg + 1) * 512)
                        for kc in range(nkc):
                            lb = 2 + (kc % 2)
                            pk = kc % 2
                            ksl = slice(kc * 128, (kc + 1) * 128)
                            kta = [('ckvT', kc // 2)]
                            bi = nkc - 1 - kc
                            near = (bi <= 2)
                            mm(pb[lb][:, :], ckvT[:, 0, ksl], qa[:, 0, hs], True, False, kta + [('qa', hg)], PB(lb))
                            mm(pb[lb][:, :], ckvT[:, 1, ksl], qa[:, 1, hs], False, False, kta + [('qa', hg)], PB(lb))
                            mm(pb[lb][:, :], mb[:, ksl], ident4[:, :], False, not near, [('mb',), ('ident4',)], PB(lb))
                            if near:
                                mm(pb[lb][:, :], ident[:, :], biasb[:, bi, hs], False, True, [('biasb',), ('ident',)], PB(lb))
                            act(pTb[:, pk, :], pb[lb][:, :], AF.Exp, PB(lb), [('pTb', pk)], scale=1.0 / 16)
                            for cc in range(2):
                                mm(pb[4 + cc][:, :], ckvk[:, kc, cc * 128:(cc + 1) * 128], pTb[:, pk, :], kc == 0, kc == nkc - 1,
                                   [('ckvk', kc // 2), ('pTb', pk)], PB(4 + cc))
                            mm(pb[6][:, :], ones[:, :], pTb[:, pk, :], kc == 0, kc == nkc - 1, [('ones',), ('pTb', pk)], PB(6))
                        oT = tb[:, 0:2, 0:512]
                        for cc in range(2):
                            cp('dve' if cc == 0 else 'act', oT[:, cc, :], pb[4 + cc][:, :], PB(4 + cc), TB(cc, 0, 512))
                        act(tf[:, 2, 0:512], pb[6][:, :], AF.Square, PB(6), TF(2, 0, 512), scale=math.sqrt(EPS))
                        for hl in range(4):
                            h_ = hg * 4 + hl
                            for cc in range(2):
                                mm(pb[7][:, hl * 128:(hl + 1) * 128], wuv[:, h_, cc, :], oT[:, cc, hl * 128:(hl + 1) * 128], cc == 0, cc == 1,
                                   TB(cc, 0, 512) + [('wuv',)], PB(7))
                        act(tb[:, 2, 0:512], pb[7][:, :], AF.Square, PB(7), TB(2, 0, 512))
                        mm(pb[0][:, :], ones[:, :], tb[:, 2, 0:512], True, True, TB(2, 0, 512) + [('ones',)], PB(0))
                        stt(tf[:, 2, 512:1024], pb[0][:, :], 1.0 / 128, tf[:, 2, 0:512], ALU.mult, ALU.add, PB(0) + TF(2, 0, 512), TF(2, 512, 1024))
                        act(tf[:, 2, 512:1024], tf[:, 2, 512:1024], AF.Ln, TF(2, 512, 1024), TF(2, 512, 1024))
                        act(tf[:, 2, 512:1024], tf[:, 2, 512:1024], AF.Exp, TF(2, 512, 1024), TF(2, 512, 1024), scale=-0.5)
                        for hl in range(4):
                            h_ = hg * 4 + hl
                            stt(ymix[:, h_, tsl], pb[7][:, hl * 128:(hl + 1) * 128], pc('ag', h_), tf[:, 2, 512 + hl * 128:512 + (hl + 1) * 128],
                                ALU.mult, ALU.mult, PB(7) + TF(2, 512, 1024) + [('prm',)], [('ymix', h_)])
                checkpoint('AT_%d' % ti)
                if last_dbg:
                    dump('yattn', ymix[:, 0, :], [('ymix', 0)])
                YK = [('ymix', i) for i in range(16)]
                for nb in range(4):
                    view, key = load_w(w_out, s_wout, 0, 2048, nb * 512, (nb + 1) * 512, 'wout')
                    dma('sp', tf[:, 0, 0:512], gsc[0:1, nb * 512:(nb + 1) * 512].partition_broadcast(128), [('gsc', 0)], TF(0, 0, 512), ('bc', 0))
                    for blk in range(2):
                        for kc in range(16):
                            mm(pb[blk][:, :], ymix[:, kc, blk * 128:(blk + 1) * 128], view[:, kc, :], kc == 0, kc == 15, YK + [key], PB(blk))
                        tt('dve', tf[:, 1, blk * 512:(blk + 1) * 512], pb[blk][:, :], tf[:, 0, 0:512], ALU.mult, PB(blk) + TF(0, 0, 512), TF(1, blk * 512, blk * 512 + 512))
                        tt('pool', xo[:, blk, nb * 512:(nb + 1) * 512], xo[:, blk, nb * 512:(nb + 1) * 512], tf[:, 1, blk * 512:(blk + 1) * 512], ALU.add,
                           TF(1, blk * 512, blk * 512 + 512) + [('xo', blk)], [('xo', blk)])
                if last_dbg:
                    dump('x1', xo[:, 0, :], [('xo', 0)])
                if stop == 'T%da' % NT and last_dbg:
                    break
                barrier(OVK_A + OVK_M)
                for blk in range(2):
                    rms_T(xo[:, blk, :], 128, hT, blk * 128, A2c, B2c, [('xo', blk)], ('hT', blk), A2K, (5, 6))
                b1a, _ = PRM['b1']
                for fg in range(16):
                    view, key = load_w(w1, s_w1, 0, 2048, fg * 512, (fg + 1) * 512, 'w1')
                    for ft in range(4):
                        f = fg * 4 + ft
                        bank = f % 4
                        rb = f % 2
                        for kc in range(16):
                            mm(pb[bank][:, 0:256], view[:, kc, ft * 128:(ft + 1) * 128], hT[:, kc, :], kc == 0, kc == 15, hk + [key], PB(bank, 0, 256))
                        act(rbuf[:, rb, 0:256], pb[bank][:, 0:256], AF.Relu, PB(bank, 0, 256) + [('prm',)], [('rbuf', rb)], bias=prm[:, b1a + f:b1a + f + 1])
                        stt(aT[:, f, :], pb[bank][:, 0:256], prm[:, b1a + f:b1a + f + 1], rbuf[:, rb, 0:256], ALU.add, ALU.mult,
                            PB(bank, 0, 256) + [('rbuf', rb), ('prm',)], [('aT', f)])
                for nb in range(4):
                    nsl = slice(nb * 512, (nb + 1) * 512)
                    dma('sp', tf[:, 0, 0:512], gsc[1:2, nsl].partition_broadcast(128), [('gsc', 1)], TF(0, 0, 512), ('bc', 0))
                    dma('sp', tf[:, 0, 512:1024], b2_d[0:1, nsl].partition_broadcast(128), [], TF(0, 512, 1024), ('bc', 1))
                    tt('pool', tf[:, 0, 512:1024], tf[:, 0, 512:1024], tf[:, 0, 0:512], ALU.mult, TF(0, 0, 512) + TF(0, 512, 1024), TF(0, 512, 1024))
                    for blk in range(2):
                        tt('pool', xo[:, blk, nsl], xo[:, blk, nsl], tf[:, 0, 512:1024], ALU.add, TF(0, 512, 1024) + [('xo', blk)], [('xo', blk)])
                    for fq in range(4):
                        view, key = load_w(w2, s_w2, fq * 2048, (fq + 1) * 2048, nb * 512, (nb + 1) * 512, 'w2')
                        for kc in range(16):
                            f = fq * 16 + kc
                            for blk in range(2):
                                mm(pb[4 + blk][:, :], aT[:, f, blk * 128:(blk + 1) * 128], view[:, kc, :], f == 0, f == 63, [('aT', f), key], PB(4 + blk))
                    for blk in range(2):
                        tt('dve', tf[:, 1, blk * 512:(blk + 1) * 512], pb[4 + blk][:, :], tf[:, 0, 0:512], ALU.mult, PB(4 + blk) + TF(0, 0, 512), TF(1, blk * 512, blk * 512 + 512))
                        tt('pool', xo[:, blk, nsl], xo[:, blk, nsl], tf[:, 1, blk * 512:(blk + 1) * 512], ALU.add, TF(1, blk * 512, blk * 512 + 512) + [('xo', blk)], [('xo', blk)])
                for blk in range(2):
                    r0 = ti * 256 + blk * 128
                    dma('sp', out_d[r0:r0 + 128, :], xo[:, blk, :], [('xo', blk)], [('out', ti, blk)], ('out', blk))
        try:
            emit_all()
        except _Stop:
            pass
        for k in list(P.dma_sems.keys()):
            P.final_waits.append(k)
        stats = P.finalize(block)
        if os.environ.get("KDEBUG"):
            print("ops/waits per engine:", stats)
    return nc


def _t5_bucket(n):
    n = np.asarray(n, np.int32)
    nf = np.maximum(n, 1).astype(np.float32)
    large = 16 + (np.log(nf / np.float32(16)) / np.float32(math.log(128 / 16)) * np.float32(16)).astype(np.int32)
    large = np.minimum(large, 31)
    return np.where(n < 16, n, large)


def make_inputs(inp, core):
    b, r = core // 2, core % 2
    x = np.asarray(inp['x'], np.float32)
    qbs = [2 * j + r for j in range(16)]
    x_own = np.concatenate([x[b, q * 128:(q + 1) * 128] for q in qbs], 0)
    x_halo = np.zeros((32, 2048), np.float32)
    hvv = np.zeros((32,), np.float32)
    for j, q in enumerate(qbs):
        if q > 0:
            x_halo[2 * j:2 * j + 2] = x[b, q * 128 - 2:q * 128]
            hvv[2 * j:2 * j + 2] = 1.0
    prm = np.zeros((128, NPRM), np.float32)

    def put(name, arr):
        a, e = PRM[name]
        prm[:, a:e] = arr
    put('c', _pcol(inp['c'][b]))
    put('bada', _pcol(inp['b_ada'][0]))
    put('n1g', _pcol(inp['norm1_g'][0]))
    put('n2g', _pcol(inp['norm2_g'][0]))
    put('cqg', _pcol(inp['cq_norm_g'][0]))
    put('kvg', _pcol(inp['kv_norm_g'][0]))
    put('qag', _pcol(inp['q_abs_norm_g'][0]))
    put('idxg', np.tile(np.asarray(inp['idx_k_norm_g'][0], np.float32), 2)[:, None])
    put('idxb', np.tile(np.asarray(inp['idx_k_norm_b'][0], np.float32), 2)[:, None])
    cw = np.asarray(inp['conv_w'][0], np.float32)
    put('cw', np.concatenate([_pcol(cw[jj]) for jj in range(3)], 1))
    put('cb', _pcol(inp['conv_b'][0]))
    put('ag', _pcol(np.asarray(inp['attn_out_norm_g'][0]).reshape(-1)))
    put('cg', _pcol(np.asarray(inp['conv_out_norm_g'][0]).reshape(-1)))
    put('b1', _pcol(inp['b_mlp1'][0]))
    NEG = np.float32(-1.0e30)
    tri = np.where(np.arange(128)[None, :] <= np.arange(128)[:, None], np.float32(0), NEG).astype(np.float32)
    cm = np.zeros((128, 256), np.float32)
    if r == 0:
        cm[:, 0:128] = tri
        cm[:, 128:256] = NEG
    else:
        cm[:, 128:256] = tri
    rel = np.asarray(inp['rel_bias'], np.float32)
    sI = np.arange(128)[:, None]
    tI = np.arange(128)[None, :]
    idx0 = _t5_bucket(np.maximum(tI - sI, 0))
    idx1 = _t5_bucket(tI - sI + 128)
    idxf = np.full((128, 128), 31, np.int64)
    order = [idxf, idx0, idx1] if r == 0 else [idx0, idx1, idxf]
    bias3 = np.stack([np.transpose(rel[ix], (0, 2, 1)).reshape(128, 1024) for ix in order], 0).astype(np.float32)
    m = {
        'x_b': np.ascontiguousarray(x[b]), 'x_own': x_own, 'x_halo': x_halo, 'prm': prm,
        'hv': np.ascontiguousarray(np.broadcast_to(hvv[None, :], (128, 32))).astype(np.float32),
        'cmask': cm, 'bias3': bias3, 'rb31': np.ascontiguousarray(rel[31:32, :]),
        'ident': np.eye(128, dtype=np.float32), 'b2': np.ascontiguousarray(np.asarray(inp['b_mlp2'], np.float32).reshape(1, 2048)),
        'w_ada': np.asarray(inp['w_ada'][0], np.float32), 'w_in': np.asarray(inp['w_in'][0], np.float32),
        'w_uq': np.asarray(inp['w_uq'][0], np.float32), 'w_uk': np.asarray(inp['w_uk'][0], np.float32),
        'w_uv': np.asarray(inp['w_uv'][0], np.float32), 'w_iq': np.asarray(inp['w_iq'][0], np.float32),
        'w_out': np.asarray(inp['w_out'][0], np.float32), 'w1': np.asarray(inp['w_mlp1'][0], np.float32),
        'w2': np.asarray(inp['w_mlp2'][0], np.float32),
    }
    return m


_NC_CACHE = {}


def kernel(**inp):
    if 'nc' not in _NC_CACHE:
        _NC_CACHE['nc'] = build()
    nc = _NC_CACHE['nc']
    in_maps = [make_inputs(inp, i) for i in range(8)]
    res = run_bass_kernel_spmd(nc, in_maps, core_ids=list(range(8)))
    out = np.zeros((4, 4096, 2048), np.float32)
    for i in range(8):
        b, r = i // 2, i % 2
        o = res.results[i]['out']
        for j in range(16):
            q = 2 * j + r
            out[b, q * 128:(q + 1) * 128] = o[j * 128:(j + 1) * 128]
    return out
```

```python
import contextlib
import math
import os

import numpy as np
import concourse.bass as bass
import concourse.mybir as mybir
from concourse.bass_utils import run_bass_kernel_spmd

F32 = mybir.dt.float32
BF16 = mybir.dt.bfloat16
AF = mybir.ActivationFunctionType
ALU = mybir.AluOpType
AX = mybir.AxisListType

ENGS = ['pe', 'act', 'dve', 'pool', 'sp']
EPS = 1e-6
NITER = 22


class Op:
    __slots__ = ('eng', 'fn', 'deps', 'ms', 'is_dma', 'sem', 'semval', 'needed', 'grp', 'pos')


class Prog:
    def __init__(self, nc):
        self.nc = nc
        self.ops = {e: [] for e in ENGS}
        self.lastw = {}
        self.readers = {}
        self.dma_sems = {}
        self.esem = {}
        self.final_waits = []

    def dma_sem(self, key, total_mode=False):
        if key not in self.dma_sems:
            h = self.nc.alloc_semaphore(name="d_" + "_".join(str(k) for k in (key if isinstance(key, tuple) else (key,))))
            self.dma_sems[key] = [h, 0, total_mode]
        return self.dma_sems[key]

    def emit(self, eng, fn, reads=(), writes=(), dma_key=None, total_mode=False):
        o = Op()
        o.eng = eng
        o.fn = fn
        o.is_dma = dma_key is not None
        o.needed = False
        o.ms = 0
        o.grp = None
        deps = []
        for r in reads:
            w = self.lastw.get(r)
            if w is not None:
                deps.append(w)
            if isinstance(r, tuple) and r[0] == 'pb':
                deps.extend(x for x in self.readers.get(r, ()) if x.eng != eng)
        for w_ in writes:
            w = self.lastw.get(w_)
            if w is not None:
                deps.append(w)
            deps.extend(self.readers.get(w_, ()))
        best = {}
        dl = []
        for d in deps:
            if d.is_dma:
                if all(d is not q for q in dl):
                    dl.append(d)
                continue
            if d.eng == 'pe' and eng == 'pe' and not o.is_dma:
                continue
            b = best.get(d.eng)
            if b is None or d.pos > b.pos:
                best[d.eng] = d
        o.deps = dl + list(best.values())
        for r in reads:
            self.readers.setdefault(r, []).append(o)
        for w_ in writes:
            self.lastw[w_] = o
            self.readers[w_] = []
        if o.is_dma:
            s = self.dma_sem(dma_key, total_mode)
            if total_mode:
                for d in o.deps:
                    assert not (d.is_dma and d.grp is s), "total-mode DMA group has an internal dependency: %r" % (dma_key,)
            s[1] += 16
            o.sem = s[0]
            o.semval = s[1]
            o.grp = s
        o.pos = len(self.ops[eng])
        self.ops[eng].append(o)
        for d in o.deps:
            d.needed = True
        return o

    def finalize(self, block):
        nc = self.nc
        for e in ENGS:
            self.esem[e] = nc.alloc_semaphore(name="e_" + e)
        for e in ENGS:
            c = 0
            for o in self.ops[e]:
                if (not o.is_dma) and o.needed:
                    c += 1
                    o.ms = c
        bname = {'pe': 'tensor', 'act': 'scalar', 'dve': 'vector', 'pool': 'gpsimd', 'sp': 'sync'}
        stats = {}
        for e in ENGS:
            ops = self.ops[e]
            esem = self.esem
            final_waits = self.final_waits if e == 'sp' else []
            nwait = [0]

            def body(h, ops=ops, e=e, final_waits=final_waits, nwait=nwait):
                waited = {}
                for o in ops:
                    for d in o.deps:
                        if d.is_dma:
                            sem = d.sem
                            val = d.grp[1] if d.grp[2] else d.semval
                            k = ('d', id(d.grp))
                        else:
                            sem = esem[d.eng]
                            val = d.ms
                            k = ('e', d.eng)
                        if waited.get(k, 0) >= val:
                            continue
                        h.wait_ge(sem, val)
                        nwait[0] += 1
                        waited[k] = val
                    ins = o.fn(h)
                    if o.is_dma:
                        ins.then_inc(o.sem, 16)
                    elif o.needed:
                        ins.then_inc(esem[e], 1)
                for key in final_waits:
                    s = self.dma_sems[key]
                    h.wait_ge(s[0], s[1])
            getattr(block, bname[e])(body)
            stats[e] = (len(ops), nwait[0])
        return stats


PRM = {}
_o = 0
for _n, _w in [('c', 16), ('bada', 96), ('n1g', 16), ('n2g', 16), ('cqg', 4), ('kvg', 2), ('qag', 2),
               ('idxg', 1), ('idxb', 1), ('cw', 24), ('cb', 8), ('ag', 8), ('cg', 8), ('b1', 64)]:
    PRM[_n] = (_o, _o + _w)
    _o += _w
NPRM = _o


def _pcol(v):
    v = np.asarray(v, np.float32).reshape(-1, 128)
    return np.ascontiguousarray(v.T)


def build(stop=None, dbg=()):
    nc = bass.Bass("TRN2", target_bir_lowering=False)
    P = Prog(nc)
    E = P.emit

    def din(name, shape, dt=F32):
        return nc.dram_tensor(name, list(shape), dt, kind="ExternalInput").ap()

    x_b = din("x_b", [4096, 2048])
    x_own = din("x_own", [2048, 2048])
    x_halo = din("x_halo", [32, 2048])
    prm_d = din("prm", [128, NPRM])
    hv_d = din("hv", [128, 32])
    cmask_d = din("cmask", [128, 256])
    bias3_d = din("bias3", [3, 128, 1024])
    rb31_d = din("rb31", [1, 8])
    ident_d = din("ident", [128, 128])
    b2_d = din("b2", [1, 2048])
    w_ada = din("w_ada", [2048, 12288])
    w_in = din("w_in", [2048, 3920])
    w_uq = din("w_uq", [512, 1024])
    w_uk = din("w_uk", [8, 128, 256])
    w_uv = din("w_uv", [8, 256, 128])
    w_iq = din("w_iq", [512, 1024])
    w_out = din("w_out", [2048, 2048])
    w1 = din("w1", [2048, 8192])
    w2 = din("w2", [8192, 2048])
    out_d = nc.dram_tensor("out", [2048, 2048], F32, kind="ExternalOutput").ap()
    dbg_d = {}
    for name, shape in dbg:
        dbg_d[name] = nc.dram_tensor("dbg_" + name, list(shape), F32, kind="ExternalOutput").ap()

    def dscr(name, shape, dt=BF16):
        return nc.dram_tensor(name, list(shape), dt, kind="Internal").ap()

    s_tiles = dscr("s_tiles", [45, 128, 8192])
    s_win = s_wout = s_w1 = s_w2 = s_wuq = s_wiq = s_tiles
    gsc = dscr("gsc", [2, 2048], F32)

    with contextlib.ExitStack() as es:
        def SB(name, shape, dt):
            return es.enter_context(nc.sbuf_tensor("sb_" + name, list(shape), dt))

        ckvT = SB("ckvT", [128, 2, 4096], BF16)
        ckvk = SB("ckvk", [128, 32, 256], BF16)
        kx = SB("kx", [128, 4096], BF16)
        ident = SB("identb", [128, 128], BF16)
        ident4 = SB("ident4", [128, 512], BF16)
        ones = SB("ones", [128, 128], BF16)
        bd64 = SB("bd64", [128, 128], BF16)
        cst = SB("cst", [128, 8], F32)
        prm = SB("prm", [128, NPRM], F32)
        modT = SB("modT", [128, 96], F32)
        AA = SB("AA", [128, 32], F32)
        cact = SB("cact", [128, 16], BF16)
        biasb = SB("biasb", [128, 3, 1024], BF16)
        rb31 = SB("rb31", [128, 8], F32)
        cmask = SB("cmask", [128, 256], F32)
        wuk = SB("wuk", [128, 8, 256], BF16)
        wuv = SB("wuv", [128, 8, 2, 128], BF16)
        widxw = SB("widxw", [128, 16, 16], BF16)
        uh = SB("uh", [128, 8, 16, 2], F32)
        hv = SB("hv", [128, 32], F32)
        ring = [SB("ring%d" % i, [128, 8192], BF16) for i in range(2)]
        xo = SB("xo", [128, 2, 2048], F32)
        hT = SB("hT", [128, 16, 256], BF16)
        ymix = SB("ymix", [128, 16, 256], BF16)
        tf = SB("tf", [128, 3, 1024], F32)
        tb = SB("tb", [128, 3, 1024], BF16)
        rbuf = SB("rbuf", [128, 2, 512], F32)
        pTb = SB("pTb", [128, 2, 512], BF16)
        iqT = SB("iqT", [128, 8, 128], BF16)
        qa = SB("qa", [128, 2, 1024], BF16)
        cqT = SB("cqT", [128, 4, 256], BF16)
        ubuf = SB("ubuf", [128, 2, 130], F32)
        ycb = SB("ycb", [128, 256], F32)
        cols = SB("cols", [128, 64], F32)
        pw2 = SB("pw2", [128, NITER + 1], F32)
        wab = SB("wab", [128, 2, 16], F32)
        wsg = SB("wsg", [128, 2, 16], F32)
        ovl = SB("ovl", [128, 16384], BF16)
        score = ovl[:, 0:8192].bitcast(F32)
        mb = ovl[:, 8192:12288]
        gcb = ovl[:, 12288:16384].bitcast(F32).rearrange("p (g t) -> p g t", g=8)
        aT = ovl[:, :].rearrange("p (f t) -> p f t", f=64)
        wk = ovl[:, 0:6144].rearrange("p (k n) -> p k n", k=16)
        pb = [es.enter_context(nc.psum_tensor("pb%d" % i, [128, 512], F32)) for i in range(8)]
        block = es.enter_context(nc.Block())

        def pbf(i):
            return pb[i][:, :].bitcast(BF16)

        def TF(i, a=0, b=1024):
            return [('tf', i, q) for q in range(a // 256, (b + 255) // 256)]

        def TB(i, a=0, b=1024):
            return [('tb', i, q) for q in range(a // 256, (b + 255) // 256)]

        def PB(i, a=0, b=512):
            return [('pb', i)]

        def COL(*idx):
            return [('col', i) for i in idx]

        def mm(out, lhsT, rhs, start, stop, reads, writes):
            E('pe', lambda h: h.matmul(out, lhsT=lhsT, rhs=rhs, start=start, stop=stop), reads=reads, writes=writes)

        def tr(out, in_, idn, reads, writes):
            E('pe', lambda h: h.transpose(out, in_, idn), reads=reads, writes=writes)

        def act(out, in_, func, reads, writes, **kw):
            E('act', lambda h: h.activation(out=out, in_=in_, func=func, **kw), reads=reads, writes=writes)

        def ts(eng, out, in0, s1, s2, op0, op1, reads, writes, accum_out=None):
            def f(h):
                if op1 is None:
                    return h.tensor_scalar(out=out, in0=in0, scalar1=s1, scalar2=None, op0=op0)
                if accum_out is not None:
                    return h.tensor_scalar(out=out, in0=in0, scalar1=s1, scalar2=s2, op0=op0, op1=op1, accum_out=accum_out)
                return h.tensor_scalar(out=out, in0=in0, scalar1=s1, scalar2=s2, op0=op0, op1=op1)
            E(eng, f, reads=reads, writes=writes)

        def stt(out, in0, scalar, in1, op0, op1, reads, writes):
            E('dve', lambda h: h.scalar_tensor_tensor(out=out, in0=in0, scalar=scalar, in1=in1, op0=op0, op1=op1),
              reads=reads, writes=writes)

        def tt(eng, out, in0, in1, op, reads, writes):
            E(eng, lambda h: h.tensor_tensor(out=out, in0=in0, in1=in1, op=op), reads=reads, writes=writes)

        def cp(eng, out, in_, reads, writes):
            if eng == 'act':
                act(out, in_, AF.Copy, reads, writes)
            else:
                E(eng, lambda h: h.tensor_copy(out=out, in_=in_), reads=reads, writes=writes)

        def dma(eng, out, in_, reads, writes, key, total=False, slow=False):
            if slow:
                return E(eng, lambda h: h.dma_start(out=out, in_=in_, allow_slow_non_contiguous=True), reads=reads, writes=writes, dma_key=key, total_mode=total)
            return E(eng, lambda h: h.dma_start(out=out, in_=in_), reads=reads, writes=writes, dma_key=key, total_mode=total)

        def rsqrt(dst, src, scale, np_, reads, writes):
            act(dst, src, AF.Ln, reads, writes, scale=scale, bias=cst[0:np_, 0:1])
            act(dst, dst, AF.Exp, writes, writes, scale=-0.5)

        def dump(name, src, keys):
            if name in dbg_d:
                dma('pool', dbg_d[name], src, keys, [('dbg', name)], ('dbg',), total=True)

        def pc(name, i=None):
            a, b = PRM[name]
            if i is None:
                return prm[:, a:b]
            return prm[:, a + i:a + i + 1]

        class _Stop(Exception):
            pass

        def checkpoint(name):
            if stop == name:
                raise _Stop()

        def setup_dma(out, in_, key, eng='sp'):
            dma(eng, out, in_, [], [key], ('setup',), total=True)

        def emit_all():
            setup_dma(prm[:, :], prm_d, ('prm',))
            setup_dma(hv[:, :], hv_d, ('hv',))
            setup_dma(cmask[:, :], cmask_d, ('cmask',))
            setup_dma(rb31[:, :], rb31_d.partition_broadcast(128), ('rb31',))
            setup_dma(tf[:, 0, 0:128], ident_d, TF(0, 0, 128)[0])
            E('dve', lambda h: h.memset(cst[:, 0:1], EPS), writes=[('cst',)])
            E('dve', lambda h: h.memset(cst[:, 1:2], 0.5), writes=[('cst',)])
            for k in range(NITER + 1):
                E('dve', lambda h, k=k: h.memset(pw2[:, k:k + 1], 2.0 ** -(k + 1)), writes=[('pw2',)])
            E('dve', lambda h: h.memset(ones[:, :], 1.0), writes=[('ones',)])
            E('dve', lambda h: h.memset(bd64[:, :], 0.0), writes=[('bd64',)])
            E('dve', lambda h: h.memset(bd64[0:64, 0:64], 1.0 / 64), writes=[('bd64',)])
            E('dve', lambda h: h.memset(bd64[64:128, 64:128], 1.0 / 64), writes=[('bd64',)])
            cp('dve', ident[:, :], tf[:, 0, 0:128], TF(0, 0, 128), [('ident',)])
            for i in range(4):
                cp('dve', ident4[:, i * 128:(i + 1) * 128], tf[:, 0, 0:128], TF(0, 0, 128), [('ident4',)])
            dma('pool', wuk[:, :, :], w_uk.rearrange("h d c -> d h c"), [], [('wuk',)], ('setup2',), total=True)
            dma('pool', wuv[:, :, :, :], w_uv.rearrange("h (cc c) v -> c h cc v", cc=2), [], [('wuv',)], ('setup2',), total=True)
            dma('pool', widxw[:, :, :], w_in[:, 832:848].rearrange("(kc p) n -> p kc n", p=128), [], [('widxw',)], ('setup2',), total=True)
            for bi in range(3):
                dma('sp', tf[:, 1, :], bias3_d[bi], [], TF(1), ('bld',))
                tt('dve', tf[:, 1, :].rearrange("p (h t) -> p h t", h=8), tf[:, 1, :].rearrange("p (h t) -> p h t", h=8),
                   rb31[:, :].unsqueeze(2).to_broadcast([128, 8, 128]), ALU.subtract, TF(1) + [('rb31',)], TF(1))
                ts('dve', biasb[:, bi, :], tf[:, 1, :], 16.0, None, ALU.mult, None, TF(1), [('biasb',)])

            checkpoint('S')
            castkeys = {}
            ring_ctr = [0]

            def ring_view(slot, kc, n):
                return ring[slot][:, 0:kc * n].rearrange("p (k n) -> p k n", k=kc)

            tile_ids = {}

            def load_w(src, scr, r0, r1, c0, c1, tag):
                slot = ring_ctr[0] % 2
                ring_ctr[0] += 1
                kc = (r1 - r0) // 128
                n = c1 - c0
                view = ring_view(slot, kc, n)
                key = ('ring', slot)
                if scr is None:
                    dma('pool', view, src[r0:r1, c0:c1].rearrange("(k p) n -> p k n", p=128), [], [key], ('rl', slot))
                else:
                    tid = tile_ids[(tag, r0, c0)]
                    dma('sp', ring[slot][:, 0:kc * n], s_tiles[tid][:, 0:kc * n], [('scrw', tag, r0, c0)], [key], ('rl', slot))
                return view, key

            def cast_weight(src, tag, tiles):
                for (r0, r1, c0, c1) in tiles:
                    tid = len(tile_ids)
                    tile_ids[(tag, r0, c0)] = tid
                    kc = (r1 - r0) // 128
                    n = c1 - c0
                    dma('pool', s_tiles[tid][:, 0:kc * n].rearrange("p (k n) -> p k n", k=kc),
                        src[r0:r1, c0:c1].rearrange("(k p) n -> p k n", p=128), [], [('scrw', tag, r0, c0)], ('cast', tag), total=True)

            act(cact[:, :], pc('c'), AF.Silu, [('prm',)], [('cact',)])

            def mod_tile(nt):
                view, key = load_w(w_ada, None, 0, 2048, nt * 512, (nt + 1) * 512, 'ada')
                for mt in range(4):
                    m = nt * 4 + mt
                    for kc in range(16):
                        mm(pb[7][:, m:m + 1], view[:, kc, mt * 128:(mt + 1) * 128], cact[:, kc:kc + 1], kc == 0, kc == 15,
                           [key, ('cact',)], PB(7, 0, 96))
                tt('dve', modT[:, nt * 4:nt * 4 + 4], pb[7][:, nt * 4:nt * 4 + 4], prm[:, PRM['bada'][0] + nt * 4:PRM['bada'][0] + nt * 4 + 4],
                   ALU.add, PB(7, 0, 96) + [('prm',)], [('modT', nt)])

            for nt in range(8):
                mod_tile(nt)
            stt(AA[:, 0:16], modT[:, 16:32], 1.0, pc('n1g'), ALU.add, ALU.mult,
                [('modT', i) for i in range(4, 8)] + [('prm',)], [('A1',)])
            A1K = [('A1',)] + [('modT', i) for i in range(0, 4)]
            if stop == '0':
                dump('modT', modT[:, :], [('modT', i) for i in range(8)])
            checkpoint('0')

            def rms_T(src, np_, dstT, c0, Acol, Bcol, srckeys, dstkey, abkeys, trb):
                xh = tb[0:np_, 0:2, :].rearrange("p a b -> p (a b)")
                jk = mb[0:np_, 0:2048]
                act(jk, src, AF.Square, srckeys + [('mb',)], [('mb',)] + COL(0), accum_out=cols[0:np_, 0:1])
                rsqrt(cols[0:np_, 1:2], cols[0:np_, 0:1], 1.0 / 2048, np_, COL(0), COL(1))
                ts('dve', xh, src, cols[0:np_, 1:2], None, ALU.mult, None, srckeys + COL(1), TB(0) + TB(1))
                for half in range(2):
                    bank = trb[half]
                    for j in range(8):
                        kc = half * 8 + j
                        tr(pbf(bank)[:, j * 128:j * 128 + np_], xh[:, kc * 128:(kc + 1) * 128], ident[0:np_, 0:np_],
                           TB(0) + TB(1) + [('ident',)], PB(bank))
                    for j in range(8):
                        kc = half * 8 + j
                        o = dstT[:, kc, c0:c0 + np_]
                        i_ = pbf(bank)[:, j * 128:j * 128 + np_]
                        if j % 2 == 0:
                            act(o, i_, AF.Identity, PB(bank) + abkeys, [dstkey], scale=Acol(kc), bias=Bcol(kc))
                        else:
                            ts('dve', o, i_, Acol(kc), Bcol(kc), ALU.mult, ALU.add, PB(bank) + abkeys, [dstkey])

            A1c = lambda kc: AA[:, kc:kc + 1]
            B1c = lambda kc: modT[:, kc:kc + 1]
            A2c = lambda kc: AA[:, 16 + kc:17 + kc]
            B2c = lambda kc: modT[:, 48 + kc:49 + kc]

            dma('pool', wk[:, :, 0:320], w_in[:, 512:832].rearrange("(k p) n -> p k n", p=128), [], [('wk', 0)], ('setup2',), total=True)
            dma('pool', wk[:, :, 320:384], w_in[:, 768:832].rearrange("(k p) n -> p k n", p=128), [], [('wk', 1)], ('setup2',), total=True)
            cast_weight(w_in, 'win', [(0, 2048, c0, c0 + 512) for c0 in (1872, 2384, 2896, 3408, 0, 848, 1360)])
            cast_weight(w_uq, 'wuq', [(0, 512, 0, 1024)])
            cast_weight(w_iq, 'wiq', [(0, 512, 0, 1024)])
            cast_weight(w_out, 'wout', [(0, 2048, nb * 512, (nb + 1) * 512) for nb in range(4)])
            checkpoint('A0')
            NTA = 16
            for ta in range(NTA):
                for blk in range(2):
                    g = ta * 2 + blk
                    dma('sp', xo[:, blk, :], x_b[g * 128:(g + 1) * 128, :], [], [('xo', blk)], ('xl', blk))
                    rms_T(xo[:, blk, :], 128, hT, blk * 128, A1c, B1c, [('xo', blk)], ('hT', blk), A1K, (5, 6))
                    checkpoint('A1_%d' % ta)
                hk = [('hT', 0), ('hT', 1)]
                for mt in range(3):
                    for kc in range(16):
                        mm(pb[mt][:, 0:256], wk[:, kc, mt * 128:(mt + 1) * 128], hT[:, kc, :], kc == 0, kc == 15,
                           hk + [('wk', 0), ('wk', 1)], PB(mt, 0, 256))
                t0 = ta * 256
                checkpoint('A2_%d' % ta)
                for cc in range(2):
                    act(tb[:, 2, cc * 256:(cc + 1) * 256], pb[cc][:, 0:256], AF.Square, PB(cc, 0, 256), TB(2, cc * 256, cc * 256 + 256))
                for cc in range(2):
                    mm(pb[3][:, 0:256], ones[:, :], tb[:, 2, cc * 256:(cc + 1) * 256], cc == 0, cc == 1,
                       TB(2, cc * 256, cc * 256 + 256) + [('ones',)], PB(3, 0, 256))
                rsqrt(tf[:, 0, 0:256], pb[3][:, 0:256], 1.0 / 256, 128, PB(3, 0, 256), TF(0, 0, 256))
                for cc in range(2):
                    stt(ckvT[:, cc, t0:t0 + 256], pb[cc][:, 0:256], pc('kvg', cc), tf[:, 0, 0:256], ALU.mult, ALU.mult,
                        PB(cc, 0, 256) + TF(0, 0, 256) + [('prm',)], [('ckvT', ta)])
                checkpoint('A3_%d' % ta)
                for blk in range(2):
                    for cc in range(2):
                        j = blk * 2 + cc
                        tr(pbf(4)[:, j * 128:(j + 1) * 128], ckvT[:, cc, t0 + blk * 128:t0 + (blk + 1) * 128], ident[:, :],
                           [('ckvT', ta), ('ident',)], PB(4, 0, 256))
                cp('act', ckvk[:, 2 * ta:2 * ta + 2, :], pbf(4)[:, 0:512].rearrange("p (b c) -> p b c", b=2), PB(4, 0, 256), [('ckvk', ta)])
                checkpoint('A4_%d' % ta)
                cp('act', tf[:, 1, 0:256], pb[2][:, 0:256], PB(2, 0, 256), TF(1, 0, 256))
                cp('act', tb[:, 2, 0:256], tf[:, 1, 0:256], TF(1, 0, 256), TB(2, 0, 256))
                tt('dve', tb[:, 2, 256:512], tf[:, 1, 0:256], tb[:, 2, 0:256], ALU.subtract, TF(1, 0, 256) + TB(2, 0, 256), TB(2, 256, 512))
                mm(pb[3][:, 256:512], bd64[:, :], tb[:, 2, 0:256], True, False, TB(2, 0, 256) + [('bd64',)], PB(3, 256, 512))
                mm(pb[3][:, 256:512], bd64[:, :], tb[:, 2, 256:512], False, True, TB(2, 256, 512) + [('bd64',)], PB(3, 256, 512))
                tt('dve', tf[:, 1, 256:512], tf[:, 1, 0:256], pb[3][:, 256:512], ALU.subtract, TF(1, 0, 256) + PB(3, 256, 512), TF(1, 256, 512))
                act(tf[:, 1, 512:768], tf[:, 1, 256:512], AF.Square, TF(1, 256, 512), TF(1, 512, 768))
                cp('act', tb[:, 2, 512:768], tf[:, 1, 512:768], TF(1, 512, 768), TB(2, 512, 768))
                tt('dve', tb[:, 2, 768:1024], tf[:, 1, 512:768], tb[:, 2, 512:768], ALU.subtract, TF(1, 512, 768) + TB(2, 512, 768), TB(2, 768, 1024))
                mm(pb[4][:, 256:512], bd64[:, :], tb[:, 2, 512:768], True, False, TB(2, 512, 768) + [('bd64',)], PB(4, 256, 512))
                mm(pb[4][:, 256:512], bd64[:, :], tb[:, 2, 768:1024], False, True, TB(2, 768, 1024) + [('bd64',)], PB(4, 256, 512))
                rsqrt(tf[:, 1, 768:1024], pb[4][:, 256:512], 1.0, 128, PB(4, 256, 512), TF(1, 768, 1024))
                tt('dve', tf[:, 2, 0:256], tf[:, 1, 256:512], tf[:, 1, 768:1024], ALU.mult, TF(1, 256, 512) + TF(1, 768, 1024), TF(2, 0, 256))
                act(kx[:, t0:t0 + 256], tf[:, 2, 0:256], AF.Identity, TF(2, 0, 256) + [('prm',)], [('kx', ta)],
                    scale=pc('idxg', 0), bias=pc('idxb', 0))
                checkpoint('A5_%d' % ta)
                if ta < 8:
                    mod_tile(8 + 2 * ta)
                    mod_tile(9 + 2 * ta)
                checkpoint('A6_%d' % ta)
            cast_weight(w1, 'w1', [(0, 2048, fg * 512, (fg + 1) * 512) for fg in range(16)])
            cast_weight(w2, 'w2', [(fq * 2048, (fq + 1) * 2048, nb * 512, (nb + 1) * 512) for nb in range(4) for fq in range(4)])
            ALLMOD = [('modT', i) for i in range(24)]
            stt(AA[:, 16:32], modT[:, 64:80], 1.0, pc('n2g'), ALU.add, ALU.mult, ALLMOD + [('prm',)], [('A2',)])
            A2K = [('A2',)] + ALLMOD
            dma('sp', gsc[0].rearrange("(m p) -> p m", p=128), modT[:, 32:48], ALLMOD, [('gsc', 0)], ('gsc',), total=True, slow=True)
            dma('sp', gsc[1].rearrange("(m p) -> p m", p=128), modT[:, 80:96], ALLMOD, [('gsc', 1)], ('gsc',), total=True, slow=True)
            KEYS_ALL = [('ckvT', i) for i in range(NTA)] + [('ckvk', i) for i in range(NTA)] + [('kx', i) for i in range(NTA)]
            dump('ckvT', ckvT[:, 0, 0:2048], KEYS_ALL)
            dump('kx', kx[:, 0:2048], KEYS_ALL)
            dump('modT', modT[:, :], ALLMOD)

            OVK_A = [('score',), ('mb',), ('wk', 0), ('wk', 1)] + [('gcb', g) for g in range(8)]
            OVK_M = [('aT', f) for f in range(64)]

            def barrier(keys):
                E('pool', lambda h: h.memset(cols[:, 63:64], 0.0), reads=[], writes=list(keys) + COL(63))

            if stop != 'A':
                barrier(OVK_A + OVK_M)
                dma('sp', xo[0:32, 0, :], x_halo, [], [('xo', 0)], ('xl', 0))
                rms_T(xo[0:32, 0, :], 32, hT, 0, A1c, B1c, [('xo', 0)], ('hT', 0), A1K, (5, 6))
                for part in range(2):
                    for half in range(2):
                        c0 = 1872 + part * 1024 + half * 512
                        view, key = load_w(w_in, s_win, 0, 2048, c0, c0 + 512, 'win')
                        for gg in range(4):
                            g = half * 4 + gg
                            bank = gg % 2
                            for kc in range(16):
                                mm(pb[bank][:, 0:32], view[:, kc, gg * 128:(gg + 1) * 128], hT[:, kc, 0:32], kc == 0, kc == 15,
                                   [key, ('hT', 0)], PB(bank, 0, 32))
                            uv = uh[:, g, :, :].rearrange("p s j -> p (s j)")
                            if part == 0:
                                tt('dve', uv, pb[bank][:, 0:32], hv[:, :], ALU.mult, PB(bank, 0, 32) + [('hv',)], [('uh', g)])
                            else:
                                tt('dve', uv, pb[bank][:, 0:32], uv, ALU.mult, PB(bank, 0, 32) + [('uh', g)], [('uh', g)])
                dump('uh', uh[:, :, :, :].rearrange("p g s j -> p (g s j)"), [('uh', g) for g in range(8)])

            checkpoint('H')
            NT = 0 if stop == 'A' else (int(stop[1:].rstrip('a')) if (stop and stop[0] == 'T') else 8)
            for ti in range(NT):
                last_dbg = (ti == NT - 1)
                barrier(OVK_A + OVK_M)
                for blk in range(2):
                    r0 = ti * 256 + blk * 128
                    dma('sp', xo[:, blk, :], x_own[r0:r0 + 128, :], [], [('xo', blk)], ('xl', blk))
                    rms_T(xo[:, blk, :], 128, hT, blk * 128, A1c, B1c, [('xo', blk)], ('hT', blk), A1K, (5, 6))
                hk = [('hT', 0), ('hT', 1)]
                checkpoint('C0a_%d' % ti)
                view, key = load_w(w_in, s_win, 0, 2048, 0, 512, 'win')
                for mt in range(4):
                    for kc in range(16):
                        mm(pb[mt][:, 0:256], view[:, kc, mt * 128:(mt + 1) * 128], hT[:, kc, :], kc == 0, kc == 15, hk + [key], PB(mt, 0, 256))
                checkpoint('C0b_%d' % ti)
                for mt in range(4):
                    act(tb[:, 2, mt * 256:(mt + 1) * 256], pb[mt][:, 0:256], AF.Square, PB(mt, 0, 256), TB(2, mt * 256, mt * 256 + 256))
                    cp('act', tf[:, 0, mt * 256:(mt + 1) * 256], pb[mt][:, 0:256], PB(mt, 0, 256), TF(0, mt * 256, mt * 256 + 256))
                for mt in range(4):
                    mm(pb[4][:, 0:256], ones[:, :], tb[:, 2, mt * 256:(mt + 1) * 256], mt == 0, mt == 3, TB(2, mt * 256, mt * 256 + 256) + [('ones',)], PB(4, 0, 256))
                rsqrt(tf[:, 1, 0:256], pb[4][:, 0:256], 1.0 / 512, 128, PB(4, 0, 256), TF(1, 0, 256))
                for mt in range(4):
                    stt(cqT[:, mt, :], tf[:, 0, mt * 256:(mt + 1) * 256], pc('cqg', mt), tf[:, 1, 0:256], ALU.mult, ALU.mult,
                        TF(0, mt * 256, mt * 256 + 256) + TF(1, 0, 256) + [('prm',)], [('cqT',)])
                checkpoint('C1_%d' % ti)
                for blk in range(2):
                    for kc in range(16):
                        mm(pb[4][:, 256 + blk * 16:256 + (blk + 1) * 16], hT[:, kc, blk * 128:(blk + 1) * 128], widxw[:, kc, :], kc == 0, kc == 15,
                           hk + [('widxw',)], PB(4, 256, 288))
                act(wab[:, :, :], pb[4][:, 256:288].rearrange("p (b h) -> p b h", b=2), AF.Abs, PB(4, 256, 288), [('wab',)], scale=1.0 / 32)
                act(wsg[:, :, :], pb[4][:, 256:288].rearrange("p (b h) -> p b h", b=2), AF.Sign, PB(4, 256, 288), [('wsg',)])
                checkpoint('C2_%d' % ti)
                for half in range(2):
                    c0 = 1872 + half * 512
                    view, key = load_w(w_in, s_win, 0, 2048, c0, c0 + 512, 'win')
                    for gg in range(4):
                        g = half * 4 + gg
                        bank = gg % 2
                        for kc in range(16):
                            mm(pb[bank][:, 0:256], view[:, kc, gg * 128:(gg + 1) * 128], hT[:, kc, :], kc == 0, kc == 15, hk + [key], PB(bank, 0, 256))
                        cp('act', gcb[:, g, :], pb[bank][:, 0:256], PB(bank, 0, 256), [('gcb', g)])
                cwa, _ = PRM['cw']
                for half in range(2):
                    c0 = 2896 + half * 512
                    view, key = load_w(w_in, s_win, 0, 2048, c0, c0 + 512, 'win')
                    for gg in range(4):
                        g = half * 4 + gg
                        bank = gg % 2
                        for kc in range(16):
                            mm(pb[bank][:, 0:256], view[:, kc, gg * 128:(gg + 1) * 128], hT[:, kc, :], kc == 0, kc == 15, hk + [key], PB(bank, 0, 256))
                        tt('dve', ubuf[:, :, 2:130], pb[bank][:, 0:256].rearrange("p (b t) -> p b t", b=2),
                           gcb[:, g, :].rearrange("p (b t) -> p b t", b=2), ALU.mult, PB(bank, 0, 256) + [('gcb', g)], [('ubuf',)])
                        cp('pool', ubuf[:, :, 0:2], uh[:, g, 2 * ti:2 * ti + 2, :], [('uh', g)], [('ubuf', 'h')])
                        gv = gcb[:, g, :].rearrange("p (b t) -> p b t", b=2)
                        act(gv, ubuf[:, :, 2:130], AF.Identity, [('ubuf',), ('prm',)], [('gcb', g)],
                            scale=prm[:, cwa + 16 + g:cwa + 17 + g], bias=pc('cb', g))
                        stt(gv, ubuf[:, :, 1:129], prm[:, cwa + 8 + g:cwa + 9 + g], gv, ALU.mult, ALU.add,
                            [('ubuf',), ('ubuf', 'h'), ('gcb', g), ('prm',)], [('gcb', g)])
                        stt(gv, ubuf[:, :, 0:128], prm[:, cwa + g:cwa + g + 1], gv, ALU.mult, ALU.add,
                            [('ubuf',), ('ubuf', 'h'), ('gcb', g), ('prm',)], [('gcb', g)])
                for half in range(2):
                    c0 = 848 + half * 512
                    view, key = load_w(w_in, s_win, 0, 2048, c0, c0 + 512, 'win')
                    for gg in range(4):
                        g = half * 4 + gg
                        bank = gg % 2
                        for kc in range(16):
                            mm(pb[bank][:, 0:256], view[:, kc, gg * 128:(gg + 1) * 128], hT[:, kc, :], kc == 0, kc == 15, hk + [key], PB(bank, 0, 256))
                        tt('dve', ycb[:, :], pb[bank][:, 0:256], gcb[:, g, :], ALU.mult, PB(bank, 0, 256) + [('gcb', g)], [('ycb',)])
                        act(tb[:, 2, 0:256], ycb[:, :], AF.Square, [('ycb',)], TB(2, 0, 256))
                        mm(pb[2 + bank][:, 0:256], ones[:, :], tb[:, 2, 0:256], True, True, TB(2, 0, 256) + [('ones',)], PB(2 + bank, 0, 256))
                        rsqrt(tf[:, 1, 256:512], pb[2 + bank][:, 0:256], 1.0 / 128, 128, PB(2 + bank, 0, 256), TF(1, 256, 512))
                        stt(ymix[:, 8 + g, :], ycb[:, :], pc('cg', g), tf[:, 1, 256:512], ALU.mult, ALU.mult,
                            [('ycb',)] + TF(1, 256, 512) + [('prm',)], [('ymix', 8 + g)])
                checkpoint('C3_%d' % ti)
                if last_dbg:
                    dump('cqT', cqT[:, 0, :], [('cqT',)])
                    dump('yconv', ymix[:, 8, :], [('ymix', 8)])
                vq, kq = load_w(w_uq, s_wuq, 0, 512, 0, 1024, 'wuq')
                vi, ki = load_w(w_iq, s_wiq, 0, 512, 0, 1024, 'wiq')
                for blk in range(2):
                    j = 2 * ti + blk
                    nkc = 2 * j + 2
                    nk = nkc * 128
                    tsl = slice(blk * 128, (blk + 1) * 128)
                    for half in range(2):
                        for pp in range(4):
                            pr = half * 4 + pp
                            for kc in range(4):
                                mm(pb[half][:, pp * 128:(pp + 1) * 128], vi[:, kc, pr * 128:(pr + 1) * 128], cqT[:, kc, tsl], kc == 0, kc == 3,
                                   [ki, ('cqT',)], PB(half))
                        cp('act', iqT[:, half * 4:half * 4 + 4, :], pb[half][:, :].rearrange("p (a t) -> p a t", a=4), PB(half), [('iqT',)])
                    qT = tb[:, 2, :]
                    for half in range(2):
                        for hh in range(4):
                            h_ = half * 4 + hh
                            for kc in range(4):
                                mm(pb[2 + half][:, hh * 128:(hh + 1) * 128], vq[:, kc, h_ * 128:(h_ + 1) * 128], cqT[:, kc, tsl], kc == 0, kc == 3,
                                   [kq, ('cqT',)], PB(2 + half))
                        cp('dve', qT[:, half * 512:(half + 1) * 512], pb[2 + half][:, :], PB(2 + half), TB(2, half * 512, half * 512 + 512))
                    QK = TB(2)
                    for cc in range(2):
                        for h_ in range(8):
                            bank = 4 + cc * 2 + h_ // 4
                            mm(pb[bank][:, (h_ % 4) * 128:(h_ % 4 + 1) * 128], wuk[:, h_, cc * 128:(cc + 1) * 128], qT[:, h_ * 128:(h_ + 1) * 128],
                               True, True, QK + [('wuk',)], PB(bank, (h_ % 4) * 128, (h_ % 4) * 128 + 128))
                    sq = tb[:, 0:2, :]
                    for cc in range(2):
                        for hf in range(2):
                            bank = 4 + cc * 2 + hf
                            act(sq[:, cc, hf * 512:(hf + 1) * 512], pb[bank][:, :], AF.Square, PB(bank), TB(cc, hf * 512, hf * 512 + 512))
                    for hf in range(2):
                        for cc in range(2):
                            mm(pb[hf][:, :], ones[:, :], sq[:, cc, hf * 512:(hf + 1) * 512], cc == 0, cc == 1, TB(cc, hf * 512, hf * 512 + 512) + [('ones',)], PB(hf))
                        rsqrt(tf[:, 0, hf * 512:(hf + 1) * 512], pb[hf][:, :], 1.0 / 256, 128, PB(hf), TF(0, hf * 512, hf * 512 + 512))
                    for cc in range(2):
                        for hf in range(2):
                            bank = 4 + cc * 2 + hf
                            stt(qa[:, cc, hf * 512:(hf + 1) * 512], pb[bank][:, :], pc('qag', cc), tf[:, 0, hf * 512:(hf + 1) * 512], ALU.mult, ALU.mult,
                                PB(bank) + TF(0, hf * 512, hf * 512 + 512) + [('prm',)], [('qa', hf)])
                    checkpoint('Q_%d_%d' % (ti, blk))
                    cnt = 0
                    for k0 in range(0, nk, 512):
                        kw = min(512, nk - k0)
                        for h_ in range(16):
                            hp = h_ % 2
                            pr = h_ // 2
                            bank = 2 + (cnt % 2)
                            rb = cnt % 2
                            cnt += 1
                            kta = [('kx', i) for i in range(k0 // 256, (k0 + kw) // 256)]
                            mm(pb[bank][:, 0:kw], iqT[hp * 64:(hp + 1) * 64, pr, :], kx[hp * 64:(hp + 1) * 64, k0:k0 + kw], True, True,
                               [('iqT',)] + kta, PB(bank, 0, kw))
                            act(rbuf[:, rb, 0:kw], pb[bank][:, 0:kw], AF.Relu, PB(bank, 0, kw) + [('wab',)], [('rbuf', rb)], scale=wab[:, blk, h_:h_ + 1])
                            if h_ == 0:
                                ts('dve', score[:, k0:k0 + kw], rbuf[:, rb, 0:kw], wsg[:, blk, 0:1], None, ALU.mult, None,
                                   [('rbuf', rb), ('wsg',)], [('score',)])
                            else:
                                stt(score[:, k0:k0 + kw], rbuf[:, rb, 0:kw], wsg[:, blk, h_:h_ + 1], score[:, k0:k0 + kw], ALU.mult, ALU.add,
                                    [('rbuf', rb), ('wsg',), ('score',)], [('score',)])
                    checkpoint('X_%d_%d' % (ti, blk))
                    lo = cols[:, 8:9]
                    hi = cols[:, 9:10]
                    mid = cols[:, 10:11]
                    cntc = cols[:, 11:12]
                    ge = cols[:, 12:13]
                    d1 = cols[:, 13:14]
                    d2 = cols[:, 14:15]
                    d1 = cols[:, 14:15]
                    halfs = cols[:, 16:16 + NITER + 1]
                    tcol = cols[:, 13:14]
                    if j >= 1:
                        E('dve', lambda h, nk=nk: h.tensor_reduce(out=hi, in_=score[:, 0:nk], axis=AX.X, op=ALU.max), reads=[('score',)], writes=COL(9))
                        E('dve', lambda h, nk=nk: h.tensor_reduce(out=lo, in_=score[:, 0:nk], axis=AX.X, op=ALU.min), reads=[('score',)], writes=COL(8))
                        ts('dve', lo, lo, -1.0, None, ALU.add, None, COL(8), COL(8))
                        tt('dve', d1, hi, lo, ALU.subtract, COL(8, 9), COL(14))
                        ts('dve', halfs, pw2[:, :], d1, None, ALU.mult, None, COL(14) + [('pw2',)], COL(16))
                        tt('dve', mid, lo, halfs[:, 0:1], ALU.add, COL(8, 16), COL(10))
                    tt('dve', score[:, nk - 256:nk], score[:, nk - 256:nk], cmask[:, :], ALU.add, [('score',), ('cmask',)], [('score',)])
                    if j >= 1:
                        for it in range(NITER):
                            ts('dve', mb[:, 0:nk], score[:, 0:nk], mid, 0.0, ALU.is_gt, ALU.add, [('score',), ('mb',)] + COL(10),
                               [('mb',)] + COL(11), accum_out=cntc)
                            ts('dve', tcol, cntc, 255.5, halfs[:, it:it + 1], ALU.is_ge, ALU.mult, COL(11, 16), COL(13))
                            stt(mid, mid, halfs[:, it + 1:it + 2], tcol, ALU.subtract, ALU.add, COL(10, 16, 13), COL(10))
                        ts('dve', lo, mid, halfs[:, NITER:NITER + 1], None, ALU.subtract, None, COL(10, 16), COL(8))
                        ts('dve', mb[:, 0:nk], score[:, 0:nk], lo, -30000.0, ALU.is_le, ALU.mult, [('score',), ('mb',)] + COL(8), [('mb',)])
                    else:
                        ts('dve', mb[:, 0:nk], score[:, 0:nk], -1.0e29, -30000.0, ALU.is_le, ALU.mult, [('score',), ('mb',)], [('mb',)])
                    checkpoint('M_%d_%d' % (ti, blk))
                    if last_dbg and blk == 1:
                        dump('score', score[:, 0:512], [('score',)])
                        dump('thr', cols[:, 0:16], COL(8, 9, 11))
                    for hg in range(2):
                        hs = slice(hg * 512, (hg + 1) * 512)
                        def qk_step(kc):
                            lb = 2 + (kc % 2)
                            pk = kc % 2
                            ksl = slice(kc * 128, (kc + 1) * 128)
                            kta = [('ckvT', kc // 2)]
                            bi = nkc - 1 - kc
                            near = (bi <= 2)
                            mm(pb[lb][:, :], ckvT[:, 0, ksl], qa[:, 0, hs], True, False, kta + [('qa', hg)], PB(lb))
                            mm(pb[lb][:, :], ckvT[:, 1, ksl], qa[:, 1, hs], False, False, kta + [('qa', hg)], PB(lb))
                            mm(pb[lb][:, :], mb[:, ksl], ident4[:, :], False, not near, [('mb',), ('ident4',)], PB(lb))
                            if near:
                                mm(pb[lb][:, :], ident[:, :], biasb[:, bi, hs], False, True, [('biasb',), ('ident',)], PB(lb))
                            act(pTb[:, pk, :], pb[lb][:, :], AF.Exp, PB(lb), [('pTb', pk)], scale=1.0 / 16)

                        def pv_step(kc):
                            pk = kc % 2
                            for cc in range(2):
                                mm(pb[4 + cc][:, :], ckvk[:, kc, cc * 128:(cc + 1) * 128], pTb[:, pk, :], kc == 0, kc == nkc - 1,
                                   [('ckvk', kc // 2), ('pTb', pk)], PB(4 + cc))
                            mm(pb[6][:, :], ones[:, :], pTb[:, pk, :], kc == 0, kc == nkc - 1, [('ones',), ('pTb', pk)], PB(6))

                        for kc in range(nkc + 1):
                            if kc < nkc:
                                qk_step(kc)
                            if kc >= 1:
                                pv_step(kc - 1)
                        oT = tb[:, 0:2, 0:512]
                        for cc in range(2):
                            cp('dve' if cc == 0 else 'act', oT[:, cc, :], pb[4 + cc][:, :], PB(4 + cc), TB(cc, 0, 512))
                        act(tf[:, 2, 0:512], pb[6][:, :], AF.Square, PB(6), TF(2, 0, 512), scale=math.sqrt(EPS))
                        for hl in range(4):
                            h_ = hg * 4 + hl
                            for cc in range(2):
                                mm(pb[7][:, hl * 128:(hl + 1) * 128], wuv[:, h_, cc, :], oT[:, cc, hl * 128:(hl + 1) * 128], cc == 0, cc == 1,
                                   TB(cc, 0, 512) + [('wuv',)], PB(7))
                        act(tb[:, 2, 0:512], pb[7][:, :], AF.Square, PB(7), TB(2, 0, 512))
                        mm(pb[0][:, :], ones[:, :], tb[:, 2, 0:512], True, True, TB(2, 0, 512) + [('ones',)], PB(0))
                        stt(tf[:, 2, 512:1024], pb[0][:, :], 1.0 / 128, tf[:, 2, 0:512], ALU.mult, ALU.add, PB(0) + TF(2, 0, 512), TF(2, 512, 1024))
                        act(tf[:, 2, 512:1024], tf[:, 2, 512:1024], AF.Ln, TF(2, 512, 1024), TF(2, 512, 1024))
                        act(tf[:, 2, 512:1024], tf[:, 2, 512:1024], AF.Exp, TF(2, 512, 1024), TF(2, 512, 1024), scale=-0.5)
                        for hl in range(4):
                            h_ = hg * 4 + hl
                            stt(ymix[:, h_, tsl], pb[7][:, hl * 128:(hl + 1) * 128], pc('ag', h_), tf[:, 2, 512 + hl * 128:512 + (hl + 1) * 128],
                                ALU.mult, ALU.mult, PB(7) + TF(2, 512, 1024) + [('prm',)], [('ymix', h_)])
                checkpoint('AT_%d' % ti)
                if last_dbg:
                    dump('yattn', ymix[:, 0, :], [('ymix', 0)])
                YK = [('ymix', i) for i in range(16)]
                for nb in range(4):
                    view, key = load_w(w_out, s_wout, 0, 2048, nb * 512, (nb + 1) * 512, 'wout')
                    dma('sp', tf[:, 0, 0:512], gsc[0:1, nb * 512:(nb + 1) * 512].partition_broadcast(128), [('gsc', 0)], TF(0, 0, 512), ('bc', 0))
                    for blk in range(2):
                        for kc in range(16):
                            mm(pb[blk][:, :], ymix[:, kc, blk * 128:(blk + 1) * 128], view[:, kc, :], kc == 0, kc == 15, YK + [key], PB(blk))
                        tt('dve', tf[:, 1, blk * 512:(blk + 1) * 512], pb[blk][:, :], tf[:, 0, 0:512], ALU.mult, PB(blk) + TF(0, 0, 512), TF(1, blk * 512, blk * 512 + 512))
                        tt('pool', xo[:, blk, nb * 512:(nb + 1) * 512], xo[:, blk, nb * 512:(nb + 1) * 512], tf[:, 1, blk * 512:(blk + 1) * 512], ALU.add,
                           TF(1, blk * 512, blk * 512 + 512) + [('xo', blk)], [('xo', blk)])
                if last_dbg:
                    dump('x1', xo[:, 0, :], [('xo', 0)])
                if stop == 'T%da' % NT and last_dbg:
                    break
                barrier(OVK_A + OVK_M)
                for blk in range(2):
                    rms_T(xo[:, blk, :], 128, hT, blk * 128, A2c, B2c, [('xo', blk)], ('hT', blk), A2K, (5, 6))
                b1a, _ = PRM['b1']
                for fg in range(16):
                    view, key = load_w(w1, s_w1, 0, 2048, fg * 512, (fg + 1) * 512, 'w1')
                    for ft in range(4):
                        f = fg * 4 + ft
                        bank = f % 4
                        rb = f % 2
                        for kc in range(16):
                            mm(pb[bank][:, 0:256], view[:, kc, ft * 128:(ft + 1) * 128], hT[:, kc, :], kc == 0, kc == 15, hk + [key], PB(bank, 0, 256))
                        act(rbuf[:, rb, 0:256], pb[bank][:, 0:256], AF.Relu, PB(bank, 0, 256) + [('prm',)], [('rbuf', rb)], bias=prm[:, b1a + f:b1a + f + 1])
                        stt(aT[:, f, :], pb[bank][:, 0:256], prm[:, b1a + f:b1a + f + 1], rbuf[:, rb, 0:256], ALU.add, ALU.mult,
                            PB(bank, 0, 256) + [('rbuf', rb), ('prm',)], [('aT', f)])
                for nb in range(4):
                    nsl = slice(nb * 512, (nb + 1) * 512)
                    dma('sp', tf[:, 0, 0:512], gsc[1:2, nsl].partition_broadcast(128), [('gsc', 1)], TF(0, 0, 512), ('bc', 0))
                    dma('sp', tf[:, 0, 512:1024], b2_d[0:1, nsl].partition_broadcast(128), [], TF(0, 512, 1024), ('bc', 1))
                    tt('pool', tf[:, 0, 512:1024], tf[:, 0, 512:1024], tf[:, 0, 0:512], ALU.mult, TF(0, 0, 512) + TF(0, 512, 1024), TF(0, 512, 1024))
                    for blk in range(2):
                        tt('pool', xo[:, blk, nsl], xo[:, blk, nsl], tf[:, 0, 512:1024], ALU.add, TF(0, 512, 1024) + [('xo', blk)], [('xo', blk)])
                    for fq in range(4):
                        view, key = load_w(w2, s_w2, fq * 2048, (fq + 1) * 2048, nb * 512, (nb + 1) * 512, 'w2')
                        for kc in range(16):
                            f = fq * 16 + kc
                            for blk in range(2):
                                mm(pb[4 + blk][:, :], aT[:, f, blk * 128:(blk + 1) * 128], view[:, kc, :], f == 0, f == 63, [('aT', f), key], PB(4 + blk))
                    for blk in range(2):
                        tt('dve', tf[:, 1, blk * 512:(blk + 1) * 512], pb[4 + blk][:, :], tf[:, 0, 0:512], ALU.mult, PB(4 + blk) + TF(0, 0, 512), TF(1, blk * 512, blk * 512 + 512))
                        tt('pool', xo[:, blk, nsl], xo[:, blk, nsl], tf[:, 1, blk * 512:(blk + 1) * 512], ALU.add, TF(1, blk * 512, blk * 512 + 512) + [('xo', blk)], [('xo', blk)])
                for blk in range(2):
                    r0 = ti * 256 + blk * 128
                    dma('sp', out_d[r0:r0 + 128, :], xo[:, blk, :], [('xo', blk)], [('out', ti, blk)], ('out', blk))
        try:
            emit_all()
        except _Stop:
            pass
        for k in list(P.dma_sems.keys()):
            P.final_waits.append(k)
        stats = P.finalize(block)
        if os.environ.get("KDEBUG"):
            print("ops/waits per engine:", stats)
    return nc


def _t5_bucket(n):
    n = np.asarray(n, np.int32)
    nf = np.maximum(n, 1).astype(np.float32)
    large = 16 + (np.log(nf / np.float32(16)) / np.float32(math.log(128 / 16)) * np.float32(16)).astype(np.int32)
    large = np.minimum(large, 31)
    return np.where(n < 16, n, large)


def make_inputs(inp, core):
    b, r = core // 2, core % 2
    x = np.asarray(inp['x'], np.float32)
    qbs = [2 * j + r for j in range(16)]
    x_own = np.concatenate([x[b, q * 128:(q + 1) * 128] for q in qbs], 0)
    x_halo = np.zeros((32, 2048), np.float32)
    hvv = np.zeros((32,), np.float32)
    for j, q in enumerate(qbs):
        if q > 0:
            x_halo[2 * j:2 * j + 2] = x[b, q * 128 - 2:q * 128]
            hvv[2 * j:2 * j + 2] = 1.0
    prm = np.zeros((128, NPRM), np.float32)

    def put(name, arr):
        a, e = PRM[name]
        prm[:, a:e] = arr
    put('c', _pcol(inp['c'][b]))
    put('bada', _pcol(inp['b_ada'][0]))
    put('n1g', _pcol(inp['norm1_g'][0]))
    put('n2g', _pcol(inp['norm2_g'][0]))
    put('cqg', _pcol(inp['cq_norm_g'][0]))
    put('kvg', _pcol(inp['kv_norm_g'][0]))
    put('qag', _pcol(inp['q_abs_norm_g'][0]))
    put('idxg', np.tile(np.asarray(inp['idx_k_norm_g'][0], np.float32), 2)[:, None])
    put('idxb', np.tile(np.asarray(inp['idx_k_norm_b'][0], np.float32), 2)[:, None])
    cw = np.asarray(inp['conv_w'][0], np.float32)
    put('cw', np.concatenate([_pcol(cw[jj]) for jj in range(3)], 1))
    put('cb', _pcol(inp['conv_b'][0]))
    put('ag', _pcol(np.asarray(inp['attn_out_norm_g'][0]).reshape(-1)))
    put('cg', _pcol(np.asarray(inp['conv_out_norm_g'][0]).reshape(-1)))
    put('b1', _pcol(inp['b_mlp1'][0]))
    NEG = np.float32(-1.0e30)
    tri = np.where(np.arange(128)[None, :] <= np.arange(128)[:, None], np.float32(0), NEG).astype(np.float32)
    cm = np.zeros((128, 256), np.float32)
    if r == 0:
        cm[:, 0:128] = tri
        cm[:, 128:256] = NEG
    else:
        cm[:, 128:256] = tri
    rel = np.asarray(inp['rel_bias'], np.float32)
    sI = np.arange(128)[:, None]
    tI = np.arange(128)[None, :]
    idx0 = _t5_bucket(np.maximum(tI - sI, 0))
    idx1 = _t5_bucket(tI - sI + 128)
    idxf = np.full((128, 128), 31, np.int64)
    order = [idxf, idx0, idx1] if r == 0 else [idx0, idx1, idxf]
    bias3 = np.stack([np.transpose(rel[ix], (0, 2, 1)).reshape(128, 1024) for ix in order], 0).astype(np.float32)
    m = {
        'x_b': np.ascontiguousarray(x[b]), 'x_own': x_own, 'x_halo': x_halo, 'prm': prm,
        'hv': np.ascontiguousarray(np.broadcast_to(hvv[None, :], (128, 32))).astype(np.float32),
        'cmask': cm, 'bias3': bias3, 'rb31': np.ascontiguousarray(rel[31:32, :]),
        'ident': np.eye(128, dtype=np.float32), 'b2': np.ascontiguousarray(np.asarray(inp['b_mlp2'], np.float32).reshape(1, 2048)),
        'w_ada': np.asarray(inp['w_ada'][0], np.float32), 'w_in': np.asarray(inp['w_in'][0], np.float32),
        'w_uq': np.asarray(inp['w_uq'][0], np.float32), 'w_uk': np.asarray(inp['w_uk'][0], np.float32),
        'w_uv': np.asarray(inp['w_uv'][0], np.float32), 'w_iq': np.asarray(inp['w_iq'][0], np.float32),
        'w_out': np.asarray(inp['w_out'][0], np.float32), 'w1': np.asarray(inp['w_mlp1'][0], np.float32),
        'w2': np.asarray(inp['w_mlp2'][0], np.float32),
    }
    return m


_NC_CACHE = {}


def kernel(**inp):
    if 'nc' not in _NC_CACHE:
        _NC_CACHE['nc'] = build()
    nc = _NC_CACHE['nc']
    in_maps = [make_inputs(inp, i) for i in range(8)]
    res = run_bass_kernel_spmd(nc, in_maps, core_ids=list(range(8)))
    out = np.zeros((4, 4096, 2048), np.float32)
    for i in range(8):
        b, r = i // 2, i % 2
        o = res.results[i]['out']
        for j in range(16):
            q = 2 * j + r
            out[b, q * 128:(q + 1) * 128] = o[j * 128:(j + 1) * 128]
    return out
```

```python
import contextlib
import math
import os

import numpy as np
import concourse.bass as bass
import concourse.mybir as mybir
from concourse.bass_utils import run_bass_kernel_spmd

F32 = mybir.dt.float32
BF16 = mybir.dt.bfloat16
AF = mybir.ActivationFunctionType
ALU = mybir.AluOpType
AX = mybir.AxisListType

ENGS = ['pe', 'act', 'dve', 'pool', 'sp']
EPS = 1e-6
NITER = 22


class Op:
    __slots__ = ('eng', 'fn', 'deps', 'ms', 'is_dma', 'sem', 'semval', 'needed', 'grp', 'pos')


class Prog:
    def __init__(self, nc):
        self.nc = nc
        self.ops = {e: [] for e in ENGS}
        self.lastw = {}
        self.readers = {}
        self.dma_sems = {}
        self.esem = {}
        self.final_waits = []

    def dma_sem(self, key, total_mode=False):
        if key not in self.dma_sems:
            h = self.nc.alloc_semaphore(name="d_" + "_".join(str(k) for k in (key if isinstance(key, tuple) else (key,))))
            self.dma_sems[key] = [h, 0, total_mode]
        return self.dma_sems[key]

    def emit(self, eng, fn, reads=(), writes=(), dma_key=None, total_mode=False):
        o = Op()
        o.eng = eng
        o.fn = fn
        o.is_dma = dma_key is not None
        o.needed = False
        o.ms = 0
        o.grp = None
        deps = []
        for r in reads:
            w = self.lastw.get(r)
            if w is not None:
                deps.append(w)
            if isinstance(r, tuple) and r[0] == 'pb':
                deps.extend(x for x in self.readers.get(r, ()) if x.eng != eng)
        for w_ in writes:
            w = self.lastw.get(w_)
            if w is not None:
                deps.append(w)
            deps.extend(self.readers.get(w_, ()))
        best = {}
        dl = []
        for d in deps:
            if d.is_dma:
                if all(d is not q for q in dl):
                    dl.append(d)
                continue
            if d.eng == 'pe' and eng == 'pe' and not o.is_dma:
                continue
            b = best.get(d.eng)
            if b is None or d.pos > b.pos:
                best[d.eng] = d
        o.deps = dl + list(best.values())
        for r in reads:
            self.readers.setdefault(r, []).append(o)
        for w_ in writes:
            self.lastw[w_] = o
            self.readers[w_] = []
        if o.is_dma:
            s = self.dma_sem(dma_key, total_mode)
            if total_mode:
                for d in o.deps:
                    assert not (d.is_dma and d.grp is s), "total-mode DMA group has an internal dependency: %r" % (dma_key,)
            s[1] += 16
            o.sem = s[0]
            o.semval = s[1]
            o.grp = s
        o.pos = len(self.ops[eng])
        self.ops[eng].append(o)
        for d in o.deps:
            d.needed = True
        return o

    def finalize(self, block):
        nc = self.nc
        for e in ENGS:
            self.esem[e] = nc.alloc_semaphore(name="e_" + e)
        for e in ENGS:
            c = 0
            for o in self.ops[e]:
                if (not o.is_dma) and o.needed:
                    c += 1
                    o.ms = c
        bname = {'pe': 'tensor', 'act': 'scalar', 'dve': 'vector', 'pool': 'gpsimd', 'sp': 'sync'}
        stats = {}
        for e in ENGS:
            ops = self.ops[e]
            esem = self.esem
            final_waits = self.final_waits if e == 'sp' else []
            nwait = [0]

            def body(h, ops=ops, e=e, final_waits=final_waits, nwait=nwait):
                waited = {}
                for o in ops:
                    for d in o.deps:
                        if d.is_dma:
                            sem = d.sem
                            val = d.grp[1] if d.grp[2] else d.semval
                            k = ('d', id(d.grp))
                        else:
                            sem = esem[d.eng]
                            val = d.ms
                            k = ('e', d.eng)
                        if waited.get(k, 0) >= val:
                            continue
                        h.wait_ge(sem, val)
                        nwait[0] += 1
                        waited[k] = val
                    ins = o.fn(h)
                    if o.is_dma:
                        ins.then_inc(o.sem, 16)
                    elif o.needed:
                        ins.then_inc(esem[e], 1)
                for key in final_waits:
                    s = self.dma_sems[key]
                    h.wait_ge(s[0], s[1])
            getattr(block, bname[e])(body)
            stats[e] = (len(ops), nwait[0])
        return stats


PRM = {}
_o = 0
for _n, _w in [('c', 16), ('bada', 96), ('n1g', 16), ('n2g', 16), ('cqg', 4), ('kvg', 2), ('qag', 2),
               ('idxg', 1), ('idxb', 1), ('cw', 24), ('cb', 8), ('ag', 8), ('cg', 8), ('b1', 64)]:
    PRM[_n] = (_o, _o + _w)
    _o += _w
NPRM = _o


def _pcol(v):
    v = np.asarray(v, np.float32).reshape(-1, 128)
    return np.ascontiguousarray(v.T)


def build(stop=None, dbg=()):
    nc = bass.Bass("TRN2", target_bir_lowering=False)
    P = Prog(nc)
    E = P.emit

    def din(name, shape, dt=F32):
        return nc.dram_tensor(name, list(shape), dt, kind="ExternalInput").ap()

    x_b = din("x_b", [4096, 2048])
    x_own = din("x_own", [2048, 2048])
    x_halo = din("x_halo", [32, 2048])
    prm_d = din("prm", [128, NPRM])
    hv_d = din("hv", [128, 32])
    cmask_d = din("cmask", [128, 256])
    bias3_d = din("bias3", [3, 128, 1024])
    rb31_d = din("rb31", [1, 8])
    ident_d = din("ident", [128, 128])
    b2_d = din("b2", [1, 2048])
    w_ada = din("w_ada", [2048, 12288])
    w_in = din("w_in", [2048, 3920])
    w_uq = din("w_uq", [512, 1024])
    w_uk = din("w_uk", [8, 128, 256])
    w_uv = din("w_uv", [8, 256, 128])
    w_iq = din("w_iq", [512, 1024])
    w_out = din("w_out", [2048, 2048])
    w1 = din("w1", [2048, 8192])
    w2 = din("w2", [8192, 2048])
    out_d = nc.dram_tensor("out", [2048, 2048], F32, kind="ExternalOutput").ap()
    dbg_d = {}
    for name, shape in dbg:
        dbg_d[name] = nc.dram_tensor("dbg_" + name, list(shape), F32, kind="ExternalOutput").ap()

    def dscr(name, shape, dt=BF16):
        return nc.dram_tensor(name, list(shape), dt, kind="Internal").ap()

    s_tiles = dscr("s_tiles", [45, 128, 8192])
    s_win = s_wout = s_w1 = s_w2 = s_wuq = s_wiq = s_tiles
    gsc = dscr("gsc", [2, 2048], F32)

    with contextlib.ExitStack() as es:
        def SB(name, shape, dt):
            return es.enter_context(nc.sbuf_tensor("sb_" + name, list(shape), dt))

        ckvT = SB("ckvT", [128, 2, 4096], BF16)
        ckvk = SB("ckvk", [128, 32, 256], BF16)
        kx = SB("kx", [128, 4096], BF16)
        ident = SB("identb", [128, 128], BF16)
        ident4 = SB("ident4", [128, 512], BF16)
        ones = SB("ones", [128, 128], BF16)
        bd64 = SB("bd64", [128, 128], BF16)
        cst = SB("cst", [128, 8], F32)
        prm = SB("prm", [128, NPRM], F32)
        modT = SB("modT", [128, 96], F32)
        AA = SB("AA", [128, 32], F32)
        cact = SB("cact", [128, 16], BF16)
        biasb = SB("biasb", [128, 3, 1024], BF16)
        rb31 = SB("rb31", [128, 8], F32)
        cmask = SB("cmask", [128, 256], F32)
        wuk = SB("wuk", [128, 8, 256], BF16)
        wuv = SB("wuv", [128, 8, 2, 128], BF16)
        widxw = SB("widxw", [128, 16, 16], BF16)
        uh = SB("uh", [128, 8, 16, 2], F32)
        hv = SB("hv", [128, 32], F32)
        ring = [SB("ring%d" % i, [128, 8192], BF16) for i in range(2)]
        xo = SB("xo", [128, 2, 2048], F32)
        hT = SB("hT", [128, 16, 256], BF16)
        ymix = SB("ymix", [128, 16, 256], BF16)
        tf = SB("tf", [128, 3, 1024], F32)
        tb = SB("tb", [128, 3, 1024], BF16)
        rbuf = SB("rbuf", [128, 2, 512], F32)
        pTb = SB("pTb", [128, 2, 512], BF16)
        iqT = SB("iqT", [128, 8, 128], BF16)
        qa = SB("qa", [128, 2, 1024], BF16)
        cqT = SB("cqT", [128, 4, 256], BF16)
        ubuf = SB("ubuf", [128, 2, 130], F32)
        ycb = SB("ycb", [128, 256], F32)
        cols = SB("cols", [128, 64], F32)
        pw2 = SB("pw2", [128, NITER + 1], F32)
        wab = SB("wab", [128, 2, 16], F32)
        wsg = SB("wsg", [128, 2, 16], F32)
        ovl = SB("ovl", [128, 16384], BF16)
        score = ovl[:, 0:8192].bitcast(F32)
        mb = ovl[:, 8192:12288]
        gcb = ovl[:, 12288:16384].bitcast(F32).rearrange("p (g t) -> p g t", g=8)
        aT = ovl[:, :].rearrange("p (f t) -> p f t", f=64)
        wk = ovl[:, 0:6144].rearrange("p (k n) -> p k n", k=16)
        pb = [es.enter_context(nc.psum_tensor("pb%d" % i, [128, 512], F32)) for i in range(8)]
        block = es.enter_context(nc.Block())

        def pbf(i):
            return pb[i][:, :].bitcast(BF16)

        def TF(i, a=0, b=1024):
            return [('tf', i, q) for q in range(a // 256, (b + 255) // 256)]

        def TB(i, a=0, b=1024):
            return [('tb', i, q) for q in range(a // 256, (b + 255) // 256)]

        def PB(i, a=0, b=512):
            return [('pb', i)]

        def COL(*idx):
            return [('col', i) for i in idx]

        def mm(out, lhsT, rhs, start, stop, reads, writes):
            E('pe', lambda h: h.matmul(out, lhsT=lhsT, rhs=rhs, start=start, stop=stop), reads=reads, writes=writes)

        def tr(out, in_, idn, reads, writes):
            E('pe', lambda h: h.transpose(out, in_, idn), reads=reads, writes=writes)

        def act(out, in_, func, reads, writes, **kw):
            E('act', lambda h: h.activation(out=out, in_=in_, func=func, **kw), reads=reads, writes=writes)

        def ts(eng, out, in0, s1, s2, op0, op1, reads, writes, accum_out=None):
            def f(h):
                if op1 is None:
                    return h.tensor_scalar(out=out, in0=in0, scalar1=s1, scalar2=None, op0=op0)
                if accum_out is not None:
                    return h.tensor_scalar(out=out, in0=in0, scalar1=s1, scalar2=s2, op0=op0, op1=op1, accum_out=accum_out)
                return h.tensor_scalar(out=out, in0=in0, scalar1=s1, scalar2=s2, op0=op0, op1=op1)
            E(eng, f, reads=reads, writes=writes)

        def stt(out, in0, scalar, in1, op0, op1, reads, writes):
            E('dve', lambda h: h.scalar_tensor_tensor(out=out, in0=in0, scalar=scalar, in1=in1, op0=op0, op1=op1),
              reads=reads, writes=writes)

        def tt(eng, out, in0, in1, op, reads, writes):
            E(eng, lambda h: h.tensor_tensor(out=out, in0=in0, in1=in1, op=op), reads=reads, writes=writes)

        def cp(eng, out, in_, reads, writes):
            if eng == 'act':
                act(out, in_, AF.Copy, reads, writes)
            else:
                E(eng, lambda h: h.tensor_copy(out=out, in_=in_), reads=reads, writes=writes)

        def dma(eng, out, in_, reads, writes, key, total=False, slow=False):
            if slow:
                return E(eng, lambda h: h.dma_start(out=out, in_=in_, allow_slow_non_contiguous=True), reads=reads, writes=writes, dma_key=key, total_mode=total)
            return E(eng, lambda h: h.dma_start(out=out, in_=in_), reads=reads, writes=writes, dma_key=key, total_mode=total)

        def rsqrt(dst, src, scale, np_, reads, writes):
            act(dst, src, AF.Ln, reads, writes, scale=scale, bias=cst[0:np_, 0:1])
            act(dst, dst, AF.Exp, writes, writes, scale=-0.5)

        def dump(name, src, keys):
            if name in dbg_d:
                dma('pool', dbg_d[name], src, keys, [('dbg', name)], ('dbg',), total=True)

        def pc(name, i=None):
            a, b = PRM[name]
            if i is None:
                return prm[:, a:b]
            return prm[:, a + i:a + i + 1]

        class _Stop(Exception):
            pass

        def checkpoint(name):
            if stop == name:
                raise _Stop()

        def setup_dma(out, in_, key, eng='sp'):
            dma(eng, out, in_, [], [key], ('setup',), total=True)

        def emit_all():
            setup_dma(prm[:, :], prm_d, ('prm',))
            setup_dma(hv[:, :], hv_d, ('hv',))
            setup_dma(cmask[:, :], cmask_d, ('cmask',))
            setup_dma(rb31[:, :], rb31_d.partition_broadcast(128), ('rb31',))
            setup_dma(tf[:, 0, 0:128], ident_d, TF(0, 0, 128)[0])
            E('dve', lambda h: h.memset(cst[:, 0:1], EPS), writes=[('cst',)])
            E('dve', lambda h: h.memset(cst[:, 1:2], 0.5), writes=[('cst',)])
            for k in range(NITER + 1):
                E('dve', lambda h, k=k: h.memset(pw2[:, k:k + 1], 2.0 ** -(k + 1)), writes=[('pw2',)])
            E('dve', lambda h: h.memset(ones[:, :], 1.0), writes=[('ones',)])
            E('dve', lambda h: h.memset(bd64[:, :], 0.0), writes=[('bd64',)])
            E('dve', lambda h: h.memset(bd64[0:64, 0:64], 1.0 / 64), writes=[('bd64',)])
            E('dve', lambda h: h.memset(bd64[64:128, 64:128], 1.0 / 64), writes=[('bd64',)])
            cp('dve', ident[:, :], tf[:, 0, 0:128], TF(0, 0, 128), [('ident',)])
            for i in range(4):
                cp('dve', ident4[:, i * 128:(i + 1) * 128], tf[:, 0, 0:128], TF(0, 0, 128), [('ident4',)])
            dma('pool', wuk[:, :, :], w_uk.rearrange("h d c -> d h c"), [], [('wuk',)], ('setup2',), total=True)
            dma('pool', wuv[:, :, :, :], w_uv.rearrange("h (cc c) v -> c h cc v", cc=2), [], [('wuv',)], ('setup2',), total=True)
            dma('pool', widxw[:, :, :], w_in[:, 832:848].rearrange("(kc p) n -> p kc n", p=128), [], [('widxw',)], ('setup2',), total=True)
            for bi in range(3):
                dma('sp', tf[:, 1, :], bias3_d[bi], [], TF(1), ('bld',))
                tt('dve', tf[:, 1, :].rearrange("p (h t) -> p h t", h=8), tf[:, 1, :].rearrange("p (h t) -> p h t", h=8),
                   rb31[:, :].unsqueeze(2).to_broadcast([128, 8, 128]), ALU.subtract, TF(1) + [('rb31',)], TF(1))
                ts('dve', biasb[:, bi, :], tf[:, 1, :], 16.0, None, ALU.mult, None, TF(1), [('biasb',)])

            checkpoint('S')
            castkeys = {}
            ring_ctr = [0]

            def ring_view(slot, kc, n):
                return ring[slot][:, 0:kc * n].rearrange("p (k n) -> p k n", k=kc)

            tile_ids = {}

            def load_w(src, scr, r0, r1, c0, c1, tag):
                slot = ring_ctr[0] % 2
                ring_ctr[0] += 1
                kc = (r1 - r0) // 128
                n = c1 - c0
                view = ring_view(slot, kc, n)
                key = ('ring', slot)
                if scr is None:
                    dma('pool', view, src[r0:r1, c0:c1].rearrange("(k p) n -> p k n", p=128), [], [key], ('rl', slot))
                else:
                    tid = tile_ids[(tag, r0, c0)]
                    dma('sp', ring[slot][:, 0:kc * n], s_tiles[tid][:, 0:kc * n], [('scrw', tag, r0, c0)], [key], ('rl', slot))
                return view, key

            def cast_weight(src, tag, tiles):
                for (r0, r1, c0, c1) in tiles:
                    tid = len(tile_ids)
                    tile_ids[(tag, r0, c0)] = tid
                    kc = (r1 - r0) // 128
                    n = c1 - c0
                    dma('pool', s_tiles[tid][:, 0:kc * n].rearrange("p (k n) -> p k n", k=kc),
                        src[r0:r1, c0:c1].rearrange("(k p) n -> p k n", p=128), [], [('scrw', tag, r0, c0)], ('cast', tag), total=True)

            act(cact[:, :], pc('c'), AF.Silu, [('prm',)], [('cact',)])

            def mod_tile(nt):
                view, key = load_w(w_ada, None, 0, 2048, nt * 512, (nt + 1) * 512, 'ada')
                for mt in range(4):
                    m = nt * 4 + mt
                    for kc in range(16):
                        mm(pb[7][:, m:m + 1], view[:, kc, mt * 128:(mt + 1) * 128], cact[:, kc:kc + 1], kc == 0, kc == 15,
                           [key, ('cact',)], PB(7, 0, 96))
                tt('dve', modT[:, nt * 4:nt * 4 + 4], pb[7][:, nt * 4:nt * 4 + 4], prm[:, PRM['bada'][0] + nt * 4:PRM['bada'][0] + nt * 4 + 4],
                   ALU.add, PB(7, 0, 96) + [('prm',)], [('modT', nt)])

            for nt in range(8):
                mod_tile(nt)
            stt(AA[:, 0:16], modT[:, 16:32], 1.0, pc('n1g'), ALU.add, ALU.mult,
                [('modT', i) for i in range(4, 8)] + [('prm',)], [('A1',)])
            A1K = [('A1',)] + [('modT', i) for i in range(0, 4)]
            if stop == '0':
                dump('modT', modT[:, :], [('modT', i) for i in range(8)])
            checkpoint('0')

            def rms_T(src, np_, dstT, c0, Acol, Bcol, srckeys, dstkey, abkeys, trb):
                xh = tb[0:np_, 0:2, :].rearrange("p a b -> p (a b)")
                jk = mb[0:np_, 0:2048]
                act(jk, src, AF.Square, srckeys + [('mb',)], [('mb',)] + COL(0), accum_out=cols[0:np_, 0:1])
                rsqrt(cols[0:np_, 1:2], cols[0:np_, 0:1], 1.0 / 2048, np_, COL(0), COL(1))
                ts('dve', xh, src, cols[0:np_, 1:2], None, ALU.mult, None, srckeys + COL(1), TB(0) + TB(1))
                for half in range(2):
                    bank = trb[half]
                    for j in range(8):
                        kc = half * 8 + j
                        tr(pbf(bank)[:, j * 128:j * 128 + np_], xh[:, kc * 128:(kc + 1) * 128], ident[0:np_, 0:np_],
                           TB(0) + TB(1) + [('ident',)], PB(bank))
                    for j in range(8):
                        kc = half * 8 + j
                        o = dstT[:, kc, c0:c0 + np_]
                        i_ = pbf(bank)[:, j * 128:j * 128 + np_]
                        if j % 2 == 0:
                            act(o, i_, AF.Identity, PB(bank) + abkeys, [dstkey], scale=Acol(kc), bias=Bcol(kc))
                        else:
                            ts('dve', o, i_, Acol(kc), Bcol(kc), ALU.mult, ALU.add, PB(bank) + abkeys, [dstkey])

            A1c = lambda kc: AA[:, kc:kc + 1]
            B1c = lambda kc: modT[:, kc:kc + 1]
            A2c = lambda kc: AA[:, 16 + kc:17 + kc]
            B2c = lambda kc: modT[:, 48 + kc:49 + kc]

            dma('pool', wk[:, :, 0:320], w_in[:, 512:832].rearrange("(k p) n -> p k n", p=128), [], [('wk', 0)], ('setup2',), total=True)
            dma('pool', wk[:, :, 320:384], w_in[:, 768:832].rearrange("(k p) n -> p k n", p=128), [], [('wk', 1)], ('setup2',), total=True)
            cast_weight(w_in, 'win', [(0, 2048, c0, c0 + 512) for c0 in (1872, 2384, 2896, 3408, 0, 848, 1360)])
            cast_weight(w_uq, 'wuq', [(0, 512, 0, 1024)])
            cast_weight(w_iq, 'wiq', [(0, 512, 0, 1024)])
            cast_weight(w_out, 'wout', [(0, 2048, nb * 512, (nb + 1) * 512) for nb in range(4)])
            checkpoint('A0')
            NTA = 16
            for ta in range(NTA):
                for blk in range(2):
                    g = ta * 2 + blk
                    dma('sp', xo[:, blk, :], x_b[g * 128:(g + 1) * 128, :], [], [('xo', blk)], ('xl', blk))
                    rms_T(xo[:, blk, :], 128, hT, blk * 128, A1c, B1c, [('xo', blk)], ('hT', blk), A1K, (5, 6))
                    checkpoint('A1_%d' % ta)
                hk = [('hT', 0), ('hT', 1)]
                for mt in range(3):
                    for kc in range(16):
                        mm(pb[mt][:, 0:256], wk[:, kc, mt * 128:(mt + 1) * 128], hT[:, kc, :], kc == 0, kc == 15,
                           hk + [('wk', 0), ('wk', 1)], PB(mt, 0, 256))
                t0 = ta * 256
                checkpoint('A2_%d' % ta)
                for cc in range(2):
                    act(tb[:, 2, cc * 256:(cc + 1) * 256], pb[cc][:, 0:256], AF.Square, PB(cc, 0, 256), TB(2, cc * 256, cc * 256 + 256))
                for cc in range(2):
                    mm(pb[3][:, 0:256], ones[:, :], tb[:, 2, cc * 256:(cc + 1) * 256], cc == 0, cc == 1,
                       TB(2, cc * 256, cc * 256 + 256) + [('ones',)], PB(3, 0, 256))
                rsqrt(tf[:, 0, 0:256], pb[3][:, 0:256], 1.0 / 256, 128, PB(3, 0, 256), TF(0, 0, 256))
                for cc in range(2):
                    stt(ckvT[:, cc, t0:t0 + 256], pb[cc][:, 0:256], pc('kvg', cc), tf[:, 0, 0:256], ALU.mult, ALU.mult,
                        PB(cc, 0, 256) + TF(0, 0, 256) + [('prm',)], [('ckvT', ta)])
                checkpoint('A3_%d' % ta)
                for blk in range(2):
                    for cc in range(2):
                        j = blk * 2 + cc
                        tr(pbf(4)[:, j * 128:(j + 1) * 128], ckvT[:, cc, t0 + blk * 128:t0 + (blk + 1) * 128], ident[:, :],
                           [('ckvT', ta), ('ident',)], PB(4, 0, 256))
                cp('act', ckvk[:, 2 * ta:2 * ta + 2, :], pbf(4)[:, 0:512].rearrange("p (b c) -> p b c", b=2), PB(4, 0, 256), [('ckvk', ta)])
                checkpoint('A4_%d' % ta)
                cp('act', tf[:, 1, 0:256], pb[2][:, 0:256], PB(2, 0, 256), TF(1, 0, 256))
                cp('act', tb[:, 2, 0:256], tf[:, 1, 0:256], TF(1, 0, 256), TB(2, 0, 256))
                tt('dve', tb[:, 2, 256:512], tf[:, 1, 0:256], tb[:, 2, 0:256], ALU.subtract, TF(1, 0, 256) + TB(2, 0, 256), TB(2, 256, 512))
                mm(pb[3][:, 256:512], bd64[:, :], tb[:, 2, 0:256], True, False, TB(2, 0, 256) + [('bd64',)], PB(3, 256, 512))
                mm(pb[3][:, 256:512], bd64[:, :], tb[:, 2, 256:512], False, True, TB(2, 256, 512) + [('bd64',)], PB(3, 256, 512))
                tt('dve', tf[:, 1, 256:512], tf[:, 1, 0:256], pb[3][:, 256:512], ALU.subtract, TF(1, 0, 256) + PB(3, 256, 512), TF(1, 256, 512))
                act(tf[:, 1, 512:768], tf[:, 1, 256:512], AF.Square, TF(1, 256, 512), TF(1, 512, 768))
                cp('act', tb[:, 2, 512:768], tf[:, 1, 512:768], TF(1, 512, 768), TB(2, 512, 768))
                tt('dve', tb[:, 2, 768:1024], tf[:, 1, 512:768], tb[:, 2, 512:768], ALU.subtract, TF(1, 512, 768) + TB(2, 512, 768), TB(2, 768, 1024))
                mm(pb[4][:, 256:512], bd64[:, :], tb[:, 2, 512:768], True, False, TB(2, 512, 768) + [('bd64',)], PB(4, 256, 512))
                mm(pb[4][:, 256:512], bd64[:, :], tb[:, 2, 768:1024], False, True, TB(2, 768, 1024) + [('bd64',)], PB(4, 256, 512))
                rsqrt(tf[:, 1, 768:1024], pb[4][:, 256:512], 1.0, 128, PB(4, 256, 512), TF(1, 768, 1024))
                tt('dve', tf[:, 2, 0:256], tf[:, 1, 256:512], tf[:, 1, 768:1024], ALU.mult, TF(1, 256, 512) + TF(1, 768, 1024), TF(2, 0, 256))
                act(kx[:, t0:t0 + 256], tf[:, 2, 0:256], AF.Identity, TF(2, 0, 256) + [('prm',)], [('kx', ta)],
                    scale=pc('idxg', 0), bias=pc('idxb', 0))
                checkpoint('A5_%d' % ta)
                if ta < 8:
                    mod_tile(8 + 2 * ta)
                    mod_tile(9 + 2 * ta)
                checkpoint('A6_%d' % ta)
            cast_weight(w1, 'w1', [(0, 2048, fg * 512, (fg + 1) * 512) for fg in range(16)])
            cast_weight(w2, 'w2', [(fq * 2048, (fq + 1) * 2048, nb * 512, (nb + 1) * 512) for nb in range(4) for fq in range(4)])
            ALLMOD = [('modT', i) for i in range(24)]
            stt(AA[:, 16:32], modT[:, 64:80], 1.0, pc('n2g'), ALU.add, ALU.mult, ALLMOD + [('prm',)], [('A2',)])
            A2K = [('A2',)] + ALLMOD
            dma('sp', gsc[0].rearrange("(m p) -> p m", p=128), modT[:, 32:48], ALLMOD, [('gsc', 0)], ('gsc',), total=True, slow=True)
            dma('sp', gsc[1].rearrange("(m p) -> p m", p=128), modT[:, 80:96], ALLMOD, [('gsc', 1)], ('gsc',), total=True, slow=True)
            KEYS_ALL = [('ckvT', i) for i in range(NTA)] + [('ckvk', i) for i in range(NTA)] + [('kx', i) for i in range(NTA)]
            dump('ckvT', ckvT[:, 0, 0:2048], KEYS_ALL)
            dump('kx', kx[:, 0:2048], KEYS_ALL)
            dump('modT', modT[:, :], ALLMOD)

            OVK_A = [('score',), ('mb',), ('wk', 0), ('wk', 1)] + [('gcb', g) for g in range(8)]
            OVK_M = [('aT', f) for f in range(64)]

            def barrier(keys):
                E('pool', lambda h: h.memset(cols[:, 63:64], 0.0), reads=[], writes=list(keys) + COL(63))

            if stop != 'A':
                barrier(OVK_A + OVK_M)
                dma('sp', xo[0:32, 0, :], x_halo, [], [('xo', 0)], ('xl', 0))
                rms_T(xo[0:32, 0, :], 32, hT, 0, A1c, B1c, [('xo', 0)], ('hT', 0), A1K, (5, 6))
                for part in range(2):
                    for half in range(2):
                        c0 = 1872 + part * 1024 + half * 512
                        view, key = load_w(w_in, s_win, 0, 2048, c0, c0 + 512, 'win')
                        for gg in range(4):
                            g = half * 4 + gg
                            bank = gg % 2
                            for kc in range(16):
                                mm(pb[bank][:, 0:32], view[:, kc, gg * 128:(gg + 1) * 128], hT[:, kc, 0:32], kc == 0, kc == 15,
                                   [key, ('hT', 0)], PB(bank, 0, 32))
                            uv = uh[:, g, :, :].rearrange("p s j -> p (s j)")
                            if part == 0:
                                tt('dve', uv, pb[bank][:, 0:32], hv[:, :], ALU.mult, PB(bank, 0, 32) + [('hv',)], [('uh', g)])
                            else:
                                tt('dve', uv, pb[bank][:, 0:32], uv, ALU.mult, PB(bank, 0, 32) + [('uh', g)], [('uh', g)])
                dump('uh', uh[:, :, :, :].rearrange("p g s j -> p (g s j)"), [('uh', g) for g in range(8)])

            checkpoint('H')
            NT = 0 if stop == 'A' else (int(stop[1:].rstrip('a')) if (stop and stop[0] == 'T') else 8)
            for ti in range(NT):
                last_dbg = (ti == NT - 1)
                barrier(OVK_A + OVK_M)
                for blk in range(2):
                    r0 = ti * 256 + blk * 128
                    dma('sp', xo[:, blk, :], x_own[r0:r0 + 128, :], [], [('xo', blk)], ('xl', blk))
                    rms_T(xo[:, blk, :], 128, hT, blk * 128, A1c, B1c, [('xo', blk)], ('hT', blk), A1K, (5, 6))
                hk = [('hT', 0), ('hT', 1)]
                checkpoint('C0a_%d' % ti)
                view, key = load_w(w_in, s_win, 0, 2048, 0, 512, 'win')
                for mt in range(4):
                    for kc in range(16):
                        mm(pb[mt][:, 0:256], view[:, kc, mt * 128:(mt + 1) * 128], hT[:, kc, :], kc == 0, kc == 15, hk + [key], PB(mt, 0, 256))
                checkpoint('C0b_%d' % ti)
                for mt in range(4):
                    act(tb[:, 2, mt * 256:(mt + 1) * 256], pb[mt][:, 0:256], AF.Square, PB(mt, 0, 256), TB(2, mt * 256, mt * 256 + 256))
                    cp('act', tf[:, 0, mt * 256:(mt + 1) * 256], pb[mt][:, 0:256], PB(mt, 0, 256), TF(0, mt * 256, mt * 256 + 256))
                for mt in range(4):
                    mm(pb[4][:, 0:256], ones[:, :], tb[:, 2, mt * 256:(mt + 1) * 256], mt == 0, mt == 3, TB(2, mt * 256, mt * 256 + 256) + [('ones',)], PB(4, 0, 256))
                rsqrt(tf[:, 1, 0:256], pb[4][:, 0:256], 1.0 / 512, 128, PB(4, 0, 256), TF(1, 0, 256))
                for mt in range(4):
                    stt(cqT[:, mt, :], tf[:, 0, mt * 256:(mt + 1) * 256], pc('cqg', mt), tf[:, 1, 0:256], ALU.mult, ALU.mult,
                        TF(0, mt * 256, mt * 256 + 256) + TF(1, 0, 256) + [('prm',)], [('cqT',)])
                checkpoint('C1_%d' % ti)
                for blk in range(2):
                    for kc in range(16):
                        mm(pb[4][:, 256 + blk * 16:256 + (blk + 1) * 16], hT[:, kc, blk * 128:(blk + 1) * 128], widxw[:, kc, :], kc == 0, kc == 15,
                           hk + [('widxw',)], PB(4, 256, 288))
                act(wab[:, :, :], pb[4][:, 256:288].rearrange("p (b h) -> p b h", b=2), AF.Abs, PB(4, 256, 288), [('wab',)], scale=1.0 / 32)
                act(wsg[:, :, :], pb[4][:, 256:288].rearrange("p (b h) -> p b h", b=2), AF.Sign, PB(4, 256, 288), [('wsg',)])
                checkpoint('C2_%d' % ti)
                for half in range(2):
                    c0 = 1872 + half * 512
                    view, key = load_w(w_in, s_win, 0, 2048, c0, c0 + 512, 'win')
                    for gg in range(4):
                        g = half * 4 + gg
                        bank = gg % 2
                        for kc in range(16):
                            mm(pb[bank][:, 0:256], view[:, kc, gg * 128:(gg + 1) * 128], hT[:, kc, :], kc == 0, kc == 15, hk + [key], PB(bank, 0, 256))
                        cp('act', gcb[:, g, :], pb[bank][:, 0:256], PB(bank, 0, 256), [('gcb', g)])
                cwa, _ = PRM['cw']
                for half in range(2):
                    c0 = 2896 + half * 512
                    view, key = load_w(w_in, s_win, 0, 2048, c0, c0 + 512, 'win')
                    for gg in range(4):
                        g = half * 4 + gg
                        bank = gg % 2
                        for kc in range(16):
                            mm(pb[bank][:, 0:256], view[:, kc, gg * 128:(gg + 1) * 128], hT[:, kc, :], kc == 0, kc == 15, hk + [key], PB(bank, 0, 256))
                        tt('dve', ubuf[:, :, 2:130], pb[bank][:, 0:256].rearrange("p (b t) -> p b t", b=2),
                           gcb[:, g, :].rearrange("p (b t) -> p b t", b=2), ALU.mult, PB(bank, 0, 256) + [('gcb', g)], [('ubuf',)])
                        cp('pool', ubuf[:, :, 0:2], uh[:, g, 2 * ti:2 * ti + 2, :], [('uh', g)], [('ubuf', 'h')])
                        gv = gcb[:, g, :].rearrange("p (b t) -> p b t", b=2)
                        act(gv, ubuf[:, :, 2:130], AF.Identity, [('ubuf',), ('prm',)], [('gcb', g)],
                            scale=prm[:, cwa + 16 + g:cwa + 17 + g], bias=pc('cb', g))
                        stt(gv, ubuf[:, :, 1:129], prm[:, cwa + 8 + g:cwa + 9 + g], gv, ALU.mult, ALU.add,
                            [('ubuf',), ('ubuf', 'h'), ('gcb', g), ('prm',)], [('gcb', g)])
                        stt(gv, ubuf[:, :, 0:128], prm[:, cwa + g:cwa + g + 1], gv, ALU.mult, ALU.add,
                            [('ubuf',), ('ubuf', 'h'), ('gcb', g), ('prm',)], [('gcb', g)])
                for half in range(2):
                    c0 = 848 + half * 512
                    view, key = load_w(w_in, s_win, 0, 2048, c0, c0 + 512, 'win')
                    for gg in range(4):
                        g = half * 4 + gg
                        bank = gg % 2
                        for kc in range(16):
                            mm(pb[bank][:, 0:256], view[:, kc, gg * 128:(gg + 1) * 128], hT[:, kc, :], kc == 0, kc == 15, hk + [key], PB(bank, 0, 256))
                        tt('dve', ycb[:, :], pb[bank][:, 0:256], gcb[:, g, :], ALU.mult, PB(bank, 0, 256) + [('gcb', g)], [('ycb',)])
                        act(tb[:, 2, 0:256], ycb[:, :], AF.Square, [('ycb',)], TB(2, 0, 256))
                        mm(pb[2 + bank][:, 0:256], ones[:, :], tb[:, 2, 0:256], True, True, TB(2, 0, 256) + [('ones',)], PB(2 + bank, 0, 256))
                        rsqrt(tf[:, 1, 256:512], pb[2 + bank][:, 0:256], 1.0 / 128, 128, PB(2 + bank, 0, 256), TF(1, 256, 512))
                        stt(ymix[:, 8 + g, :], ycb[:, :], pc('cg', g), tf[:, 1, 256:512], ALU.mult, ALU.mult,
                            [('ycb',)] + TF(1, 256, 512) + [('prm',)], [('ymix', 8 + g)])
                checkpoint('C3_%d' % ti)
                if last_dbg:
                    dump('cqT', cqT[:, 0, :], [('cqT',)])
                    dump('yconv', ymix[:, 8, :], [('ymix', 8)])
                vq, kq = load_w(w_uq, s_wuq, 0, 512, 0, 1024, 'wuq')
                vi, ki = load_w(w_iq, s_wiq, 0, 512, 0, 1024, 'wiq')
                mbB = ovl[:, 12288:16384]
                MBAP = [mb, mbB]
                MBK = [[('mb',)], [('gcb', g) for g in range(8)]]

                def emit_iq(blk):
                    tsl = slice(blk * 128, (blk + 1) * 128)
                    for half in range(2):
                        for pp in range(4):
                            pr = half * 4 + pp
                            for kc in range(4):
                                mm(pb[half][:, pp * 128:(pp + 1) * 128], vi[:, kc, pr * 128:(pr + 1) * 128], cqT[:, kc, tsl], kc == 0, kc == 3,
                                   [ki, ('cqT',)], PB(half))
                        cp('act', iqT[:, half * 4:half * 4 + 4, :], pb[half][:, :].rearrange("p (a t) -> p a t", a=4), PB(half), [('iqT',)])

                def emit_qa(blk):
                    tsl = slice(blk * 128, (blk + 1) * 128)
                    qT = tb[:, 2, :]
                    for half in range(2):
                        for hh in range(4):
                            h_ = half * 4 + hh
                            for kc in range(4):
                                mm(pb[2 + half][:, hh * 128:(hh + 1) * 128], vq[:, kc, h_ * 128:(h_ + 1) * 128], cqT[:, kc, tsl], kc == 0, kc == 3,
                                   [kq, ('cqT',)], PB(2 + half))
                        cp('dve', qT[:, half * 512:(half + 1) * 512], pb[2 + half][:, :], PB(2 + half), TB(2, half * 512, half * 512 + 512))
                    QK = TB(2)
                    for cc in range(2):
                        for h_ in range(8):
                            bank = 4 + cc * 2 + h_ // 4
                            mm(pb[bank][:, (h_ % 4) * 128:(h_ % 4 + 1) * 128], wuk[:, h_, cc * 128:(cc + 1) * 128], qT[:, h_ * 128:(h_ + 1) * 128],
                               True, True, QK + [('wuk',)], PB(bank))
                    sq = tb[:, 0:2, :]
                    for cc in range(2):
                        for hf in range(2):
                            bank = 4 + cc * 2 + hf
                            act(sq[:, cc, hf * 512:(hf + 1) * 512], pb[bank][:, :], AF.Square, PB(bank), TB(cc, hf * 512, hf * 512 + 512))
                    for hf in range(2):
                        for cc in range(2):
                            mm(pb[hf][:, :], ones[:, :], sq[:, cc, hf * 512:(hf + 1) * 512], cc == 0, cc == 1, TB(cc, hf * 512, hf * 512 + 512) + [('ones',)], PB(hf))
                        rsqrt(tf[:, 0, hf * 512:(hf + 1) * 512], pb[hf][:, :], 1.0 / 256, 128, PB(hf), TF(0, hf * 512, hf * 512 + 512))
                    for cc in range(2):
                        for hf in range(2):
                            bank = 4 + cc * 2 + hf
                            stt(qa[:, cc, hf * 512:(hf + 1) * 512], pb[bank][:, :], pc('qag', cc), tf[:, 0, hf * 512:(hf + 1) * 512], ALU.mult, ALU.mult,
                                PB(bank) + TF(0, hf * 512, hf * 512 + 512) + [('prm',)], [('qa', hf)])

                def gen_ixbs(blk):
                    j = 2 * ti + blk
                    nk = (2 * j + 2) * 128
                    mbx, mbk = MBAP[blk], MBK[blk]
                    cnt = 0
                    for k0 in range(0, nk, 512):
                        kw = min(512, nk - k0)
                        for h_ in range(16):
                            hp = h_ % 2
                            pr = h_ // 2
                            bank = cnt % 2
                            rb = cnt % 2
                            cnt += 1
                            kta = [('kx', i) for i in range(k0 // 256, (k0 + kw) // 256)]
                            mm(pb[bank][:, 0:kw], iqT[hp * 64:(hp + 1) * 64, pr, :], kx[hp * 64:(hp + 1) * 64, k0:k0 + kw], True, True,
                               [('iqT',)] + kta, PB(bank))
                            act(rbuf[:, rb, 0:kw], pb[bank][:, 0:kw], AF.Relu, PB(bank) + [('wab',)], [('rbuf', rb)], scale=wab[:, blk, h_:h_ + 1])
                            if h_ == 0:
                                ts('dve', score[:, k0:k0 + kw], rbuf[:, rb, 0:kw], wsg[:, blk, 0:1], None, ALU.mult, None,
                                   [('rbuf', rb), ('wsg',)], [('score',)])
                            else:
                                stt(score[:, k0:k0 + kw], rbuf[:, rb, 0:kw], wsg[:, blk, h_:h_ + 1], score[:, k0:k0 + kw], ALU.mult, ALU.add,
                                    [('rbuf', rb), ('wsg',), ('score',)], [('score',)])
                            yield
                    lo = cols[:, 8:9]
                    hi = cols[:, 9:10]
                    mid = cols[:, 10:11]
                    cntc = cols[:, 11:12]
                    d1 = cols[:, 14:15]
                    halfs = cols[:, 16:16 + NITER + 1]
                    tcol = cols[:, 13:14]
                    if j >= 1:
                        E('dve', lambda h, nk=nk: h.tensor_reduce(out=hi, in_=score[:, 0:nk], axis=AX.X, op=ALU.max), reads=[('score',)], writes=COL(9))
                        E('dve', lambda h, nk=nk: h.tensor_reduce(out=lo, in_=score[:, 0:nk], axis=AX.X, op=ALU.min), reads=[('score',)], writes=COL(8))
                        ts('dve', lo, lo, -1.0, None, ALU.add, None, COL(8), COL(8))
                        tt('dve', d1, hi, lo, ALU.subtract, COL(8, 9), COL(14))
                        ts('dve', halfs, pw2[:, :], d1, None, ALU.mult, None, COL(14) + [('pw2',)], COL(16))
                        tt('dve', mid, lo, halfs[:, 0:1], ALU.add, COL(8, 16), COL(10))
                    tt('dve', score[:, nk - 256:nk], score[:, nk - 256:nk], cmask[:, :], ALU.add, [('score',), ('cmask',)], [('score',)])
                    yield
                    if j >= 1:
                        for it in range(NITER):
                            ts('dve', mbx[:, 0:nk], score[:, 0:nk], mid, 0.0, ALU.is_gt, ALU.add, [('score',)] + mbk + COL(10),
                               mbk + COL(11), accum_out=cntc)
                            ts('dve', tcol, cntc, 255.5, halfs[:, it:it + 1], ALU.is_ge, ALU.mult, COL(11, 16), COL(13))
                            stt(mid, mid, halfs[:, it + 1:it + 2], tcol, ALU.subtract, ALU.add, COL(10, 16, 13), COL(10))
                            yield
                        ts('dve', lo, mid, halfs[:, NITER:NITER + 1], None, ALU.subtract, None, COL(10, 16), COL(8))
                        ts('dve', mbx[:, 0:nk], score[:, 0:nk], lo, -30000.0, ALU.is_le, ALU.mult, [('score',)] + mbk + COL(8), mbk)
                    else:
                        ts('dve', mbx[:, 0:nk], score[:, 0:nk], -1.0e29, -30000.0, ALU.is_le, ALU.mult, [('score',)] + mbk, mbk)
                    if last_dbg and blk == 1:
                        dump('score', score[:, 0:512], [('score',)])
                        dump('thr', cols[:, 0:16], COL(8, 9, 11))
                    yield

                def n_ixbs(blk):
                    j = 2 * ti + blk
                    nk = (2 * j + 2) * 128
                    return 16 * ((nk + 511) // 512) + 2 + (NITER if j >= 1 else 0)

                def gen_attn(blk):
                    j = 2 * ti + blk
                    nkc = 2 * j + 2
                    tsl = slice(blk * 128, (blk + 1) * 128)
                    mbx, mbk = MBAP[blk], MBK[blk]
                    for hg in range(2):
                        hs = slice(hg * 512, (hg + 1) * 512)

                        def qk_step(kc):
                            lb = 2 + (kc % 2)
                            pk = kc % 2
                            ksl = slice(kc * 128, (kc + 1) * 128)
                            kta = [('ckvT', kc // 2)]
                            bi = nkc - 1 - kc
                            near = (bi <= 2)
                            mm(pb[lb][:, :], ckvT[:, 0, ksl], qa[:, 0, hs], True, False, kta + [('qa', hg)], PB(lb))
                            mm(pb[lb][:, :], ckvT[:, 1, ksl], qa[:, 1, hs], False, False, kta + [('qa', hg)], PB(lb))
                            mm(pb[lb][:, :], mbx[:, ksl], ident4[:, :], False, not near, mbk + [('ident4',)], PB(lb))
                            if near:
                                mm(pb[lb][:, :], ident[:, :], biasb[:, bi, hs], False, True, [('biasb',), ('ident',)], PB(lb))
                            act(pTb[:, pk, :], pb[lb][:, :], AF.Exp, PB(lb), [('pTb', pk)], scale=1.0 / 16)

                        def pv_step(kc):
                            pk = kc % 2
                            for cc in range(2):
                                mm(pb[4 + cc][:, :], ckvk[:, kc, cc * 128:(cc + 1) * 128], pTb[:, pk, :], kc == 0, kc == nkc - 1,
                                   [('ckvk', kc // 2), ('pTb', pk)], PB(4 + cc))
                            mm(pb[6][:, :], ones[:, :], pTb[:, pk, :], kc == 0, kc == nkc - 1, [('ones',), ('pTb', pk)], PB(6))

                        for kc in range(nkc + 1):
                            if kc < nkc:
                                qk_step(kc)
                            if kc >= 1:
                                pv_step(kc - 1)
                            yield
                        oT = tb[:, 0:2, 0:512]
                        for cc in range(2):
                            cp('dve' if cc == 0 else 'act', oT[:, cc, :], pb[4 + cc][:, :], PB(4 + cc), TB(cc, 0, 512))
                        act(tf[:, 2, 0:512], pb[6][:, :], AF.Square, PB(6), TF(2, 0, 512), scale=math.sqrt(EPS))
                        for hl in range(4):
                            h_ = hg * 4 + hl
                            for cc in range(2):
                                mm(pb[7][:, hl * 128:(hl + 1) * 128], wuv[:, h_, cc, :], oT[:, cc, hl * 128:(hl + 1) * 128], cc == 0, cc == 1,
                                   TB(cc, 0, 512) + [('wuv',)], PB(7))
                        yield
                        act(tb[:, 2, 0:512], pb[7][:, :], AF.Square, PB(7), TB(2, 0, 512))
                        mm(pb[6][:, :], ones[:, :], tb[:, 2, 0:512], True, True, TB(2, 0, 512) + [('ones',)], PB(6))
                        stt(tf[:, 2, 512:1024], pb[6][:, :], 1.0 / 128, tf[:, 2, 0:512], ALU.mult, ALU.add, PB(6) + TF(2, 0, 512), TF(2, 512, 1024))
                        act(tf[:, 2, 512:1024], tf[:, 2, 512:1024], AF.Ln, TF(2, 512, 1024), TF(2, 512, 1024))
                        act(tf[:, 2, 512:1024], tf[:, 2, 512:1024], AF.Exp, TF(2, 512, 1024), TF(2, 512, 1024), scale=-0.5)
                        for hl in range(4):
                            h_ = hg * 4 + hl
                            stt(ymix[:, h_, tsl], pb[7][:, hl * 128:(hl + 1) * 128], pc('ag', h_), tf[:, 2, 512 + hl * 128:512 + (hl + 1) * 128],
                                ALU.mult, ALU.mult, PB(7) + TF(2, 512, 1024) + [('prm',)], [('ymix', h_)])
                        yield

                def n_attn(blk):
                    j = 2 * ti + blk
                    return 2 * (2 * j + 2 + 1 + 2)

                def run_all(g):
                    for _ in g:
                        pass

                def interleave(ga, na, gb, nb_):
                    ia = ib = 0
                    da = db = False
                    while not (da and db):
                        pick_a = (not da) and (db or ia * nb_ <= ib * na)
                        if pick_a:
                            try:
                                next(ga)
                                ia += 1
                            except StopIteration:
                                da = True
                        else:
                            try:
                                next(gb)
                                ib += 1
                            except StopIteration:
                                db = True

                emit_iq(0)
                emit_qa(0)
                run_all(gen_ixbs(0))
                emit_iq(1)
                interleave(gen_attn(0), n_attn(0), gen_ixbs(1), n_ixbs(1))
                emit_qa(1)
                run_all(gen_attn(1))
                checkpoint('AT_%d' % ti)
                if last_dbg:
                    dump('yattn', ymix[:, 0, :], [('ymix', 0)])
                YK = [('ymix', i) for i in range(16)]
                for nb in range(4):
                    view, key = load_w(w_out, s_wout, 0, 2048, nb * 512, (nb + 1) * 512, 'wout')
                    dma('sp', tf[:, 0, 0:512], gsc[0:1, nb * 512:(nb + 1) * 512].partition_broadcast(128), [('gsc', 0)], TF(0, 0, 512), ('bc', 0))
                    for blk in range(2):
                        for kc in range(16):
                            mm(pb[blk][:, :], ymix[:, kc, blk * 128:(blk + 1) * 128], view[:, kc, :], kc == 0, kc == 15, YK + [key], PB(blk))
                        tt('dve', tf[:, 1, blk * 512:(blk + 1) * 512], pb[blk][:, :], tf[:, 0, 0:512], ALU.mult, PB(blk) + TF(0, 0, 512), TF(1, blk * 512, blk * 512 + 512))
                        tt('pool', xo[:, blk, nb * 512:(nb + 1) * 512], xo[:, blk, nb * 512:(nb + 1) * 512], tf[:, 1, blk * 512:(blk + 1) * 512], ALU.add,
                           TF(1, blk * 512, blk * 512 + 512) + [('xo', blk)], [('xo', blk)])
                if last_dbg:
                    dump('x1', xo[:, 0, :], [('xo', 0)])
                if stop == 'T%da' % NT and last_dbg:
                    break
                barrier(OVK_A + OVK_M)
                for blk in range(2):
                    rms_T(xo[:, blk, :], 128, hT, blk * 128, A2c, B2c, [('xo', blk)], ('hT', blk), A2K, (5, 6))
                b1a, _ = PRM['b1']
                for fg in range(16):
                    view, key = load_w(w1, s_w1, 0, 2048, fg * 512, (fg + 1) * 512, 'w1')
                    for ft in range(4):
                        f = fg * 4 + ft
                        bank = f % 4
                        rb = f % 2
                        for kc in range(16):
                            mm(pb[bank][:, 0:256], view[:, kc, ft * 128:(ft + 1) * 128], hT[:, kc, :], kc == 0, kc == 15, hk + [key], PB(bank, 0, 256))
                        act(rbuf[:, rb, 0:256], pb[bank][:, 0:256], AF.Relu, PB(bank, 0, 256) + [('prm',)], [('rbuf', rb)], bias=prm[:, b1a + f:b1a + f + 1])
                        stt(aT[:, f, :], pb[bank][:, 0:256], prm[:, b1a + f:b1a + f + 1], rbuf[:, rb, 0:256], ALU.add, ALU.mult,
                            PB(bank, 0, 256) + [('rbuf', rb), ('prm',)], [('aT', f)])
                for nb in range(4):
                    nsl = slice(nb * 512, (nb + 1) * 512)
                    dma('sp', tf[:, 0, 0:512], gsc[1:2, nsl].partition_broadcast(128), [('gsc', 1)], TF(0, 0, 512), ('bc', 0))
                    dma('sp', tf[:, 0, 512:1024], b2_d[0:1, nsl].partition_broadcast(128), [], TF(0, 512, 1024), ('bc', 1))
                    tt('pool', tf[:, 0, 512:1024], tf[:, 0, 512:1024], tf[:, 0, 0:512], ALU.mult, TF(0, 0, 512) + TF(0, 512, 1024), TF(0, 512, 1024))
                    for blk in range(2):
                        tt('pool', xo[:, blk, nsl], xo[:, blk, nsl], tf[:, 0, 512:1024], ALU.add, TF(0, 512, 1024) + [('xo', blk)], [('xo', blk)])
                    for fq in range(4):
                        view, key = load_w(w2, s_w2, fq * 2048, (fq + 1) * 2048, nb * 512, (nb + 1) * 512, 'w2')
                        for kc in range(16):
                            f = fq * 16 + kc
                            for blk in range(2):
                                mm(pb[4 + blk][:, :], aT[:, f, blk * 128:(blk + 1) * 128], view[:, kc, :], f == 0, f == 63, [('aT', f), key], PB(4 + blk))
                    for blk in range(2):
                        tt('dve', tf[:, 1, blk * 512:(blk + 1) * 512], pb[4 + blk][:, :], tf[:, 0, 0:512], ALU.mult, PB(4 + blk) + TF(0, 0, 512), TF(1, blk * 512, blk * 512 + 512))
                        tt('pool', xo[:, blk, nsl], xo[:, blk, nsl], tf[:, 1, blk * 512:(blk + 1) * 512], ALU.add, TF(1, blk * 512, blk * 512 + 512) + [('xo', blk)], [('xo', blk)])
                for blk in range(2):
                    r0 = ti * 256 + blk * 128
                    dma('sp', out_d[r0:r0 + 128, :], xo[:, blk, :], [('xo', blk)], [('out', ti, blk)], ('out', blk))
        try:
            emit_all()
        except _Stop:
            pass
        for k in list(P.dma_sems.keys()):
            P.final_waits.append(k)
        stats = P.finalize(block)
        if os.environ.get("KDEBUG"):
            print("ops/waits per engine:", stats)
    return nc


def _t5_bucket(n):
    n = np.asarray(n, np.int32)
    nf = np.maximum(n, 1).astype(np.float32)
    large = 16 + (np.log(nf / np.float32(16)) / np.float32(math.log(128 / 16)) * np.float32(16)).astype(np.int32)
    large = np.minimum(large, 31)
    return np.where(n < 16, n, large)


def make_inputs(inp, core):
    b, r = core // 2, core % 2
    x = np.asarray(inp['x'], np.float32)
    qbs = [2 * j + r for j in range(16)]
    x_own = np.concatenate([x[b, q * 128:(q + 1) * 128] for q in qbs], 0)
    x_halo = np.zeros((32, 2048), np.float32)
    hvv = np.zeros((32,), np.float32)
    for j, q in enumerate(qbs):
        if q > 0:
            x_halo[2 * j:2 * j + 2] = x[b, q * 128 - 2:q * 128]
            hvv[2 * j:2 * j + 2] = 1.0
    prm = np.zeros((128, NPRM), np.float32)

    def put(name, arr):
        a, e = PRM[name]
        prm[:, a:e] = arr
    put('c', _pcol(inp['c'][b]))
    put('bada', _pcol(inp['b_ada'][0]))
    put('n1g', _pcol(inp['norm1_g'][0]))
    put('n2g', _pcol(inp['norm2_g'][0]))
    put('cqg', _pcol(inp['cq_norm_g'][0]))
    put('kvg', _pcol(inp['kv_norm_g'][0]))
    put('qag', _pcol(inp['q_abs_norm_g'][0]))
    put('idxg', np.tile(np.asarray(inp['idx_k_norm_g'][0], np.float32), 2)[:, None])
    put('idxb', np.tile(np.asarray(inp['idx_k_norm_b'][0], np.float32), 2)[:, None])
    cw = np.asarray(inp['conv_w'][0], np.float32)
    put('cw', np.concatenate([_pcol(cw[jj]) for jj in range(3)], 1))
    put('cb', _pcol(inp['conv_b'][0]))
    put('ag', _pcol(np.asarray(inp['attn_out_norm_g'][0]).reshape(-1)))
    put('cg', _pcol(np.asarray(inp['conv_out_norm_g'][0]).reshape(-1)))
    put('b1', _pcol(inp['b_mlp1'][0]))
    NEG = np.float32(-1.0e30)
    tri = np.where(np.arange(128)[None, :] <= np.arange(128)[:, None], np.float32(0), NEG).astype(np.float32)
    cm = np.zeros((128, 256), np.float32)
    if r == 0:
        cm[:, 0:128] = tri
        cm[:, 128:256] = NEG
    else:
        cm[:, 128:256] = tri
    rel = np.asarray(inp['rel_bias'], np.float32)
    sI = np.arange(128)[:, None]
    tI = np.arange(128)[None, :]
    idx0 = _t5_bucket(np.maximum(tI - sI, 0))
    idx1 = _t5_bucket(tI - sI + 128)
    idxf = np.full((128, 128), 31, np.int64)
    order = [idxf, idx0, idx1] if r == 0 else [idx0, idx1, idxf]
    bias3 = np.stack([np.transpose(rel[ix], (0, 2, 1)).reshape(128, 1024) for ix in order], 0).astype(np.float32)
    m = {
        'x_b': np.ascontiguousarray(x[b]), 'x_own': x_own, 'x_halo': x_halo, 'prm': prm,
        'hv': np.ascontiguousarray(np.broadcast_to(hvv[None, :], (128, 32))).astype(np.float32),
        'cmask': cm, 'bias3': bias3, 'rb31': np.ascontiguousarray(rel[31:32, :]),
        'ident': np.eye(128, dtype=np.float32), 'b2': np.ascontiguousarray(np.asarray(inp['b_mlp2'], np.float32).reshape(1, 2048)),
        'w_ada': np.asarray(inp['w_ada'][0], np.float32), 'w_in': np.asarray(inp['w_in'][0], np.float32),
        'w_uq': np.asarray(inp['w_uq'][0], np.float32), 'w_uk': np.asarray(inp['w_uk'][0], np.float32),
        'w_uv': np.asarray(inp['w_uv'][0], np.float32), 'w_iq': np.asarray(inp['w_iq'][0], np.float32),
        'w_out': np.asarray(inp['w_out'][0], np.float32), 'w1': np.asarray(inp['w_mlp1'][0], np.float32),
        'w2': np.asarray(inp['w_mlp2'][0], np.float32),
    }
    return m


_NC_CACHE = {}


def kernel(**inp):
    if 'nc' not in _NC_CACHE:
        _NC_CACHE['nc'] = build()
    nc = _NC_CACHE['nc']
    in_maps = [make_inputs(inp, i) for i in range(8)]
    res = run_bass_kernel_spmd(nc, in_maps, core_ids=list(range(8)))
    out = np.zeros((4, 4096, 2048), np.float32)
    for i in range(8):
        b, r = i // 2, i % 2
        o = res.results[i]['out']
        for j in range(16):
            q = 2 * j + r
            out[b, q * 128:(q + 1) * 128] = o[j * 128:(j + 1) * 128]
    return out
```

```python
import contextlib
import math
import os

import numpy as np
import concourse.bass as bass
import concourse.mybir as mybir
from concourse.bass_utils import run_bass_kernel_spmd

F32 = mybir.dt.float32
BF16 = mybir.dt.bfloat16
AF = mybir.ActivationFunctionType
ALU = mybir.AluOpType
AX = mybir.AxisListType

ENGS = ['pe', 'act', 'dve', 'pool', 'sp']
EPS = 1e-6
NITER = 20


class Op:
    __slots__ = ('eng', 'fn', 'deps', 'ms', 'is_dma', 'sem', 'semval', 'needed', 'grp', 'pos')


class Prog:
    def __init__(self, nc):
        self.nc = nc
        self.ops = {e: [] for e in ENGS}
        self.lastw = {}
        self.readers = {}
        self.dma_sems = {}
        self.esem = {}
        self.final_waits = []

    def dma_sem(self, key, total_mode=False):
        if key not in self.dma_sems:
            h = self.nc.alloc_semaphore(name="d_" + "_".join(str(k) for k in (key if isinstance(key, tuple) else (key,))))
            self.dma_sems[key] = [h, 0, total_mode]
        return self.dma_sems[key]

    def emit(self, eng, fn, reads=(), writes=(), dma_key=None, total_mode=False):
        o = Op()
        o.eng = eng
        o.fn = fn
        o.is_dma = dma_key is not None
        o.needed = False
        o.ms = 0
        o.grp = None
        deps = []
        for r in reads:
            w = self.lastw.get(r)
            if w is not None:
                deps.append(w)
            if isinstance(r, tuple) and r[0] == 'pb':
                deps.extend(x for x in self.readers.get(r, ()) if x.eng != eng)
        for w_ in writes:
            w = self.lastw.get(w_)
            if w is not None:
                deps.append(w)
            deps.extend(self.readers.get(w_, ()))
        best = {}
        dl = []
        for d in deps:
            if d.is_dma:
                if all(d is not q for q in dl):
                    dl.append(d)
                continue
            if d.eng == 'pe' and eng == 'pe' and not o.is_dma:
                continue
            b = best.get(d.eng)
            if b is None or d.pos > b.pos:
                best[d.eng] = d
        o.deps = dl + list(best.values())
        for r in reads:
            self.readers.setdefault(r, []).append(o)
        for w_ in writes:
            self.lastw[w_] = o
            self.readers[w_] = []
        if o.is_dma:
            s = self.dma_sem(dma_key, total_mode)
            if total_mode:
                for d in o.deps:
                    assert not (d.is_dma and d.grp is s), "total-mode DMA group has an internal dependency: %r" % (dma_key,)
            s[1] += 16
            o.sem = s[0]
            o.semval = s[1]
            o.grp = s
        o.pos = len(self.ops[eng])
        self.ops[eng].append(o)
        for d in o.deps:
            d.needed = True
        return o

    def finalize(self, block):
        nc = self.nc
        for e in ENGS:
            self.esem[e] = nc.alloc_semaphore(name="e_" + e)
        for e in ENGS:
            c = 0
            for o in self.ops[e]:
                if (not o.is_dma) and o.needed:
                    c += 1
                    o.ms = c
        bname = {'pe': 'tensor', 'act': 'scalar', 'dve': 'vector', 'pool': 'gpsimd', 'sp': 'sync'}
        stats = {}
        for e in ENGS:
            ops = self.ops[e]
            esem = self.esem
            final_waits = self.final_waits if e == 'sp' else []
            nwait = [0]

            def body(h, ops=ops, e=e, final_waits=final_waits, nwait=nwait):
                waited = {}
                for o in ops:
                    for d in o.deps:
                        if d.is_dma:
                            sem = d.sem
                            val = d.grp[1] if d.grp[2] else d.semval
                            k = ('d', id(d.grp))
                        else:
                            sem = esem[d.eng]
                            val = d.ms
                            k = ('e', d.eng)
                        if waited.get(k, 0) >= val:
                            continue
                        h.wait_ge(sem, val)
                        nwait[0] += 1
                        waited[k] = val
                    ins = o.fn(h)
                    if o.is_dma:
                        ins.then_inc(o.sem, 16)
                    elif o.needed:
                        ins.then_inc(esem[e], 1)
                for key in final_waits:
                    s = self.dma_sems[key]
                    h.wait_ge(s[0], s[1])
            getattr(block, bname[e])(body)
            stats[e] = (len(ops), nwait[0])
        return stats


PRM = {}
_o = 0
for _n, _w in [('c', 16), ('bada', 96), ('n1g', 16), ('n2g', 16), ('cqg', 4), ('kvg', 2), ('qag', 2),
               ('idxg', 1), ('idxb', 1), ('cw', 24), ('cb', 8), ('ag', 8), ('cg', 8), ('b1', 64)]:
    PRM[_n] = (_o, _o + _w)
    _o += _w
NPRM = _o


def _pcol(v):
    v = np.asarray(v, np.float32).reshape(-1, 128)
    return np.ascontiguousarray(v.T)


def build(stop=None, dbg=()):
    nc = bass.Bass("TRN2", target_bir_lowering=False)
    P = Prog(nc)
    E = P.emit

    def din(name, shape, dt=F32):
        return nc.dram_tensor(name, list(shape), dt, kind="ExternalInput").ap()

    x_b = din("x_b", [4096, 2048])
    x_own = din("x_own", [2048, 2048])
    x_halo = din("x_halo", [32, 2048])
    prm_d = din("prm", [128, NPRM])
    hv_d = din("hv", [128, 32])
    cmask_d = din("cmask", [128, 256])
    bias3_d = din("bias3", [3, 128, 1024])
    rb31_d = din("rb31", [1, 8])
    ident_d = din("ident", [128, 128])
    b2_d = din("b2", [1, 2048])
    w_ada = din("w_ada", [2048, 12288])
    w_in = din("w_in", [2048, 3920])
    w_uq = din("w_uq", [512, 1024])
    w_uk = din("w_uk", [8, 128, 256])
    w_uv = din("w_uv", [8, 256, 128])
    w_iq = din("w_iq", [512, 1024])
    w_out = din("w_out", [2048, 2048])
    w1 = din("w1", [2048, 8192])
    w2 = din("w2", [8192, 2048])
    out_d = nc.dram_tensor("out", [2048, 2048], F32, kind="ExternalOutput").ap()
    dbg_d = {}
    for name, shape in dbg:
        dbg_d[name] = nc.dram_tensor("dbg_" + name, list(shape), F32, kind="ExternalOutput").ap()

    def dscr(name, shape, dt=BF16):
        return nc.dram_tensor(name, list(shape), dt, kind="Internal").ap()

    s_tiles = dscr("s_tiles", [45, 128, 8192])
    s_win = s_wout = s_w1 = s_w2 = s_wuq = s_wiq = s_tiles
    gsc = dscr("gsc", [2, 2048], F32)

    with contextlib.ExitStack() as es:
        def SB(name, shape, dt):
            return es.enter_context(nc.sbuf_tensor("sb_" + name, list(shape), dt))

        ckvT = SB("ckvT", [128, 2, 4096], BF16)
        ckvk = SB("ckvk", [128, 32, 256], BF16)
        kx = SB("kx", [128, 4096], BF16)
        ident = SB("identb", [128, 128], BF16)
        ident4 = SB("ident4", [128, 512], BF16)
        ones = SB("ones", [128, 128], BF16)
        bd64 = SB("bd64", [128, 128], BF16)
        cst = SB("cst", [128, 8], F32)
        prm = SB("prm", [128, NPRM], F32)
        modT = SB("modT", [128, 96], F32)
        AA = SB("AA", [128, 32], F32)
        cact = SB("cact", [128, 16], BF16)
        biasb = SB("biasb", [128, 3, 1024], BF16)
        rb31 = SB("rb31", [128, 8], F32)
        cmask = SB("cmask", [128, 256], F32)
        wuk = SB("wuk", [128, 8, 256], BF16)
        wuv = SB("wuv", [128, 8, 2, 128], BF16)
        widxw = SB("widxw", [128, 16, 16], BF16)
        uh = SB("uh", [128, 8, 16, 2], F32)
        hv = SB("hv", [128, 32], F32)
        ring = [SB("ring%d" % i, [128, 8192], BF16) for i in range(2)]
        xo = SB("xo", [128, 2, 2048], F32)
        hT = SB("hT", [128, 16, 256], BF16)
        ymix = SB("ymix", [128, 16, 256], BF16)
        tf = SB("tf", [128, 3, 1024], F32)
        tb = SB("tb", [128, 3, 1024], BF16)
        rbuf = SB("rbuf", [128, 2, 512], F32)
        pTb = SB("pTb", [128, 2, 512], BF16)
        iqT = SB("iqT", [128, 8, 128], BF16)
        qa = SB("qa", [128, 2, 1024], BF16)
        cqT = SB("cqT", [128, 4, 256], BF16)
        ubuf = SB("ubuf", [128, 2, 130], F32)
        ycb = SB("ycb", [128, 256], F32)
        cols = SB("cols", [128, 64], F32)
        pw2 = SB("pw2", [128, NITER + 1], F32)
        wab = SB("wab", [128, 2, 16], F32)
        wsg = SB("wsg", [128, 2, 16], F32)
        ovl = SB("ovl", [128, 16384], BF16)
        score = ovl[:, 0:8192].bitcast(F32)
        mb = ovl[:, 8192:12288]
        gcb = ovl[:, 12288:16384].bitcast(F32).rearrange("p (g t) -> p g t", g=8)
        aT = ovl[:, :].rearrange("p (f t) -> p f t", f=64)
        wk = ovl[:, 0:6144].rearrange("p (k n) -> p k n", k=16)
        pb = [es.enter_context(nc.psum_tensor("pb%d" % i, [128, 512], F32)) for i in range(8)]
        block = es.enter_context(nc.Block())

        def pbf(i):
            return pb[i][:, :].bitcast(BF16)

        def TF(i, a=0, b=1024):
            return [('tf', i, q) for q in range(a // 256, (b + 255) // 256)]

        def TB(i, a=0, b=1024):
            return [('tb', i, q) for q in range(a // 256, (b + 255) // 256)]

        def PB(i, a=0, b=512):
            return [('pb', i)]

        def COL(*idx):
            return [('col', i) for i in idx]

        def mm(out, lhsT, rhs, start, stop, reads, writes):
            E('pe', lambda h: h.matmul(out, lhsT=lhsT, rhs=rhs, start=start, stop=stop), reads=reads, writes=writes)

        def tr(out, in_, idn, reads, writes):
            E('pe', lambda h: h.transpose(out, in_, idn), reads=reads, writes=writes)

        def act(out, in_, func, reads, writes, **kw):
            E('act', lambda h: h.activation(out=out, in_=in_, func=func, **kw), reads=reads, writes=writes)

        def ts(eng, out, in0, s1, s2, op0, op1, reads, writes, accum_out=None):
            def f(h):
                if op1 is None:
                    return h.tensor_scalar(out=out, in0=in0, scalar1=s1, scalar2=None, op0=op0)
                if accum_out is not None:
                    return h.tensor_scalar(out=out, in0=in0, scalar1=s1, scalar2=s2, op0=op0, op1=op1, accum_out=accum_out)
                return h.tensor_scalar(out=out, in0=in0, scalar1=s1, scalar2=s2, op0=op0, op1=op1)
            E(eng, f, reads=reads, writes=writes)

        def stt(out, in0, scalar, in1, op0, op1, reads, writes):
            E('dve', lambda h: h.scalar_tensor_tensor(out=out, in0=in0, scalar=scalar, in1=in1, op0=op0, op1=op1),
              reads=reads, writes=writes)

        def tt(eng, out, in0, in1, op, reads, writes):
            E(eng, lambda h: h.tensor_tensor(out=out, in0=in0, in1=in1, op=op), reads=reads, writes=writes)

        def cp(eng, out, in_, reads, writes):
            if eng == 'act':
                act(out, in_, AF.Copy, reads, writes)
            else:
                E(eng, lambda h: h.tensor_copy(out=out, in_=in_), reads=reads, writes=writes)

        def dma(eng, out, in_, reads, writes, key, total=False, slow=False):
            if slow:
                return E(eng, lambda h: h.dma_start(out=out, in_=in_, allow_slow_non_contiguous=True), reads=reads, writes=writes, dma_key=key, total_mode=total)
            return E(eng, lambda h: h.dma_start(out=out, in_=in_), reads=reads, writes=writes, dma_key=key, total_mode=total)

        def rsqrt(dst, src, scale, np_, reads, writes):
            act(dst, src, AF.Ln, reads, writes, scale=scale, bias=cst[0:np_, 0:1])
            act(dst, dst, AF.Exp, writes, writes, scale=-0.5)

        def dump(name, src, keys):
            if name in dbg_d:
                dma('pool', dbg_d[name], src, keys, [('dbg', name)], ('dbg',), total=True)

        def pc(name, i=None):
            a, b = PRM[name]
            if i is None:
                return prm[:, a:b]
            return prm[:, a + i:a + i + 1]

        class _Stop(Exception):
            pass

        def checkpoint(name):
            if stop == name:
                raise _Stop()

        def setup_dma(out, in_, key, eng='sp'):
            dma(eng, out, in_, [], [key], ('setup',), total=True)

        def emit_all():
            setup_dma(prm[:, :], prm_d, ('prm',))
            setup_dma(hv[:, :], hv_d, ('hv',))
            setup_dma(cmask[:, :], cmask_d, ('cmask',))
            setup_dma(rb31[:, :], rb31_d.partition_broadcast(128), ('rb31',))
            setup_dma(tf[:, 0, 0:128], ident_d, TF(0, 0, 128)[0])
            E('dve', lambda h: h.memset(cst[:, 0:1], EPS), writes=[('cst',)])
            E('dve', lambda h: h.memset(cst[:, 1:2], 0.5), writes=[('cst',)])
            for k in range(NITER + 1):
                E('dve', lambda h, k=k: h.memset(pw2[:, k:k + 1], 2.0 ** -(k + 1)), writes=[('pw2',)])
            E('dve', lambda h: h.memset(ones[:, :], 1.0), writes=[('ones',)])
            E('dve', lambda h: h.memset(bd64[:, :], 0.0), writes=[('bd64',)])
            E('dve', lambda h: h.memset(bd64[0:64, 0:64], 1.0 / 64), writes=[('bd64',)])
            E('dve', lambda h: h.memset(bd64[64:128, 64:128], 1.0 / 64), writes=[('bd64',)])
            cp('dve', ident[:, :], tf[:, 0, 0:128], TF(0, 0, 128), [('ident',)])
            for i in range(4):
                cp('dve', ident4[:, i * 128:(i + 1) * 128], tf[:, 0, 0:128], TF(0, 0, 128), [('ident4',)])
            dma('pool', wuk[:, :, :], w_uk.rearrange("h d c -> d h c"), [], [('wuk',)], ('setup2',), total=True)
            dma('pool', wuv[:, :, :, :], w_uv.rearrange("h (cc c) v -> c h cc v", cc=2), [], [('wuv',)], ('setup2',), total=True)
            dma('pool', widxw[:, :, :], w_in[:, 832:848].rearrange("(kc p) n -> p kc n", p=128), [], [('widxw',)], ('setup2',), total=True)
            for bi in range(3):
                dma('sp', tf[:, 1, :], bias3_d[bi], [], TF(1), ('bld',))
                tt('dve', tf[:, 1, :].rearrange("p (h t) -> p h t", h=8), tf[:, 1, :].rearrange("p (h t) -> p h t", h=8),
                   rb31[:, :].unsqueeze(2).to_broadcast([128, 8, 128]), ALU.subtract, TF(1) + [('rb31',)], TF(1))
                ts('dve', biasb[:, bi, :], tf[:, 1, :], 16.0, None, ALU.mult, None, TF(1), [('biasb',)])

            checkpoint('S')
            castkeys = {}
            ring_ctr = [0]

            def ring_view(slot, kc, n):
                return ring[slot][:, 0:kc * n].rearrange("p (k n) -> p k n", k=kc)

            tile_ids = {}

            def load_w(src, scr, r0, r1, c0, c1, tag):
                slot = ring_ctr[0] % 2
                ring_ctr[0] += 1
                kc = (r1 - r0) // 128
                n = c1 - c0
                view = ring_view(slot, kc, n)
                key = ('ring', slot)
                if scr is None:
                    dma('pool', view, src[r0:r1, c0:c1].rearrange("(k p) n -> p k n", p=128), [], [key], ('rl', slot))
                else:
                    tid = tile_ids[(tag, r0, c0)]
                    dma('sp', ring[slot][:, 0:kc * n], s_tiles[tid][:, 0:kc * n], [('scrw', tag, r0, c0)], [key], ('rl', slot))
                return view, key

            def cast_weight(src, tag, tiles):
                for (r0, r1, c0, c1) in tiles:
                    tid = len(tile_ids)
                    tile_ids[(tag, r0, c0)] = tid
                    kc = (r1 - r0) // 128
                    n = c1 - c0
                    dma('pool', s_tiles[tid][:, 0:kc * n].rearrange("p (k n) -> p k n", k=kc),
                        src[r0:r1, c0:c1].rearrange("(k p) n -> p k n", p=128), [], [('scrw', tag, r0, c0)], ('cast', tag), total=True)

            act(cact[:, :], pc('c'), AF.Silu, [('prm',)], [('cact',)])

            def mod_tile(nt):
                view, key = load_w(w_ada, None, 0, 2048, nt * 512, (nt + 1) * 512, 'ada')
                for mt in range(4):
                    m = nt * 4 + mt
                    for kc in range(16):
                        mm(pb[7][:, m:m + 1], view[:, kc, mt * 128:(mt + 1) * 128], cact[:, kc:kc + 1], kc == 0, kc == 15,
                           [key, ('cact',)], PB(7, 0, 96))
                tt('dve', modT[:, nt * 4:nt * 4 + 4], pb[7][:, nt * 4:nt * 4 + 4], prm[:, PRM['bada'][0] + nt * 4:PRM['bada'][0] + nt * 4 + 4],
                   ALU.add, PB(7, 0, 96) + [('prm',)], [('modT', nt)])

            for nt in range(8):
                mod_tile(nt)
            stt(AA[:, 0:16], modT[:, 16:32], 1.0, pc('n1g'), ALU.add, ALU.mult,
                [('modT', i) for i in range(4, 8)] + [('prm',)], [('A1',)])
            A1K = [('A1',)] + [('modT', i) for i in range(0, 4)]
            if stop == '0':
                dump('modT', modT[:, :], [('modT', i) for i in range(8)])
            checkpoint('0')

            def rms_T(src, np_, dstT, c0, Acol, Bcol, srckeys, dstkey, abkeys, trb):
                xh = tb[0:np_, 0:2, :].rearrange("p a b -> p (a b)")
                jk = mb[0:np_, 0:2048]
                act(jk, src, AF.Square, srckeys + [('mb',)], [('mb',)] + COL(0), accum_out=cols[0:np_, 0:1])
                rsqrt(cols[0:np_, 1:2], cols[0:np_, 0:1], 1.0 / 2048, np_, COL(0), COL(1))
                ts('dve', xh, src, cols[0:np_, 1:2], None, ALU.mult, None, srckeys + COL(1), TB(0) + TB(1))
                for half in range(2):
                    bank = trb[half]
                    for j in range(8):
                        kc = half * 8 + j
                        tr(pbf(bank)[:, j * 128:j * 128 + np_], xh[:, kc * 128:(kc + 1) * 128], ident[0:np_, 0:np_],
                           TB(0) + TB(1) + [('ident',)], PB(bank))
                    for j in range(8):
                        kc = half * 8 + j
                        o = dstT[:, kc, c0:c0 + np_]
                        i_ = pbf(bank)[:, j * 128:j * 128 + np_]
                        if j % 2 == 0:
                            act(o, i_, AF.Identity, PB(bank) + abkeys, [dstkey], scale=Acol(kc), bias=Bcol(kc))
                        else:
                            ts('dve', o, i_, Acol(kc), Bcol(kc), ALU.mult, ALU.add, PB(bank) + abkeys, [dstkey])

            A1c = lambda kc: AA[:, kc:kc + 1]
            B1c = lambda kc: modT[:, kc:kc + 1]
            A2c = lambda kc: AA[:, 16 + kc:17 + kc]
            B2c = lambda kc: modT[:, 48 + kc:49 + kc]

            dma('pool', wk[:, :, 0:320], w_in[:, 512:832].rearrange("(k p) n -> p k n", p=128), [], [('wk', 0)], ('setup2',), total=True)
            dma('pool', wk[:, :, 320:384], w_in[:, 768:832].rearrange("(k p) n -> p k n", p=128), [], [('wk', 1)], ('setup2',), total=True)
            checkpoint('A0')
            NTA = 16
            for ta in range(NTA):
                for blk in range(2):
                    g = ta * 2 + blk
                    dma('sp', xo[:, blk, :], x_b[g * 128:(g + 1) * 128, :], [], [('xo', blk)], ('xl', blk))
                    rms_T(xo[:, blk, :], 128, hT, blk * 128, A1c, B1c, [('xo', blk)], ('hT', blk), A1K, (5, 6))
                    checkpoint('A1_%d' % ta)
                hk = [('hT', 0), ('hT', 1)]
                for mt in range(3):
                    for kc in range(16):
                        mm(pb[mt][:, 0:256], wk[:, kc, mt * 128:(mt + 1) * 128], hT[:, kc, :], kc == 0, kc == 15,
                           hk + [('wk', 0), ('wk', 1)], PB(mt, 0, 256))
                t0 = ta * 256
                checkpoint('A2_%d' % ta)
                for cc in range(2):
                    act(tb[:, 2, cc * 256:(cc + 1) * 256], pb[cc][:, 0:256], AF.Square, PB(cc, 0, 256), TB(2, cc * 256, cc * 256 + 256))
                for cc in range(2):
                    mm(pb[3][:, 0:256], ones[:, :], tb[:, 2, cc * 256:(cc + 1) * 256], cc == 0, cc == 1,
                       TB(2, cc * 256, cc * 256 + 256) + [('ones',)], PB(3, 0, 256))
                rsqrt(tf[:, 0, 0:256], pb[3][:, 0:256], 1.0 / 256, 128, PB(3, 0, 256), TF(0, 0, 256))
                for cc in range(2):
                    stt(ckvT[:, cc, t0:t0 + 256], pb[cc][:, 0:256], pc('kvg', cc), tf[:, 0, 0:256], ALU.mult, ALU.mult,
                        PB(cc, 0, 256) + TF(0, 0, 256) + [('prm',)], [('ckvT', ta)])
                checkpoint('A3_%d' % ta)
                for blk in range(2):
                    for cc in range(2):
                        j = blk * 2 + cc
                        tr(pbf(4)[:, j * 128:(j + 1) * 128], ckvT[:, cc, t0 + blk * 128:t0 + (blk + 1) * 128], ident[:, :],
                           [('ckvT', ta), ('ident',)], PB(4, 0, 256))
                cp('act', ckvk[:, 2 * ta:2 * ta + 2, :], pbf(4)[:, 0:512].rearrange("p (b c) -> p b c", b=2), PB(4, 0, 256), [('ckvk', ta)])
                checkpoint('A4_%d' % ta)
                cp('act', tf[:, 1, 0:256], pb[2][:, 0:256], PB(2, 0, 256), TF(1, 0, 256))
                cp('act', tb[:, 2, 0:256], tf[:, 1, 0:256], TF(1, 0, 256), TB(2, 0, 256))
                tt('dve', tb[:, 2, 256:512], tf[:, 1, 0:256], tb[:, 2, 0:256], ALU.subtract, TF(1, 0, 256) + TB(2, 0, 256), TB(2, 256, 512))
                mm(pb[3][:, 256:512], bd64[:, :], tb[:, 2, 0:256], True, False, TB(2, 0, 256) + [('bd64',)], PB(3, 256, 512))
                mm(pb[3][:, 256:512], bd64[:, :], tb[:, 2, 256:512], False, True, TB(2, 256, 512) + [('bd64',)], PB(3, 256, 512))
                tt('dve', tf[:, 1, 256:512], tf[:, 1, 0:256], pb[3][:, 256:512], ALU.subtract, TF(1, 0, 256) + PB(3, 256, 512), TF(1, 256, 512))
                act(tf[:, 1, 512:768], tf[:, 1, 256:512], AF.Square, TF(1, 256, 512), TF(1, 512, 768))
                cp('act', tb[:, 2, 512:768], tf[:, 1, 512:768], TF(1, 512, 768), TB(2, 512, 768))
                tt('dve', tb[:, 2, 768:1024], tf[:, 1, 512:768], tb[:, 2, 512:768], ALU.subtract, TF(1, 512, 768) + TB(2, 512, 768), TB(2, 768, 1024))
                mm(pb[4][:, 256:512], bd64[:, :], tb[:, 2, 512:768], True, False, TB(2, 512, 768) + [('bd64',)], PB(4, 256, 512))
                mm(pb[4][:, 256:512], bd64[:, :], tb[:, 2, 768:1024], False, True, TB(2, 768, 1024) + [('bd64',)], PB(4, 256, 512))
                rsqrt(tf[:, 1, 768:1024], pb[4][:, 256:512], 1.0, 128, PB(4, 256, 512), TF(1, 768, 1024))
                tt('dve', tf[:, 2, 0:256], tf[:, 1, 256:512], tf[:, 1, 768:1024], ALU.mult, TF(1, 256, 512) + TF(1, 768, 1024), TF(2, 0, 256))
                act(kx[:, t0:t0 + 256], tf[:, 2, 0:256], AF.Identity, TF(2, 0, 256) + [('prm',)], [('kx', ta)],
                    scale=pc('idxg', 0), bias=pc('idxb', 0))
                checkpoint('A5_%d' % ta)
                if ta < 8:
                    mod_tile(8 + 2 * ta)
                    mod_tile(9 + 2 * ta)
                if ta == 1:
                    cast_weight(w_in, 'win', [(0, 2048, c0, c0 + 512) for c0 in (1872, 2384, 2896, 3408, 0, 848, 1360)])
                if ta == 9:
                    cast_weight(w_uq, 'wuq', [(0, 512, 0, 1024)])
                    cast_weight(w_iq, 'wiq', [(0, 512, 0, 1024)])
                if ta == 11:
                    cast_weight(w_out, 'wout', [(0, 2048, nb * 512, (nb + 1) * 512) for nb in range(4)])
                checkpoint('A6_%d' % ta)
            ALLMOD = [('modT', i) for i in range(24)]
            stt(AA[:, 16:32], modT[:, 64:80], 1.0, pc('n2g'), ALU.add, ALU.mult, ALLMOD + [('prm',)], [('A2',)])
            A2K = [('A2',)] + ALLMOD
            dma('sp', gsc[0].rearrange("(m p) -> p m", p=128), modT[:, 32:48], ALLMOD, [('gsc', 0)], ('gsc',), total=True, slow=True)
            dma('sp', gsc[1].rearrange("(m p) -> p m", p=128), modT[:, 80:96], ALLMOD, [('gsc', 1)], ('gsc',), total=True, slow=True)
            KEYS_ALL = [('ckvT', i) for i in range(NTA)] + [('ckvk', i) for i in range(NTA)] + [('kx', i) for i in range(NTA)]
            dump('ckvT', ckvT[:, 0, 0:2048], KEYS_ALL)
            dump('kx', kx[:, 0:2048], KEYS_ALL)
            dump('modT', modT[:, :], ALLMOD)

            OVK_A = [('score',), ('mb',), ('wk', 0), ('wk', 1)] + [('gcb', g) for g in range(8)]
            OVK_M = [('aT', f) for f in range(64)]

            def barrier(keys):
                E('pool', lambda h: h.memset(cols[:, 63:64], 0.0), reads=[], writes=list(keys) + COL(63))

            if stop != 'A':
                barrier(OVK_A + OVK_M)
                dma('sp', xo[0:32, 0, :], x_halo, [], [('xo', 0)], ('xl', 0))
                rms_T(xo[0:32, 0, :], 32, hT, 0, A1c, B1c, [('xo', 0)], ('hT', 0), A1K, (5, 6))
                for part in range(2):
                    for half in range(2):
                        c0 = 1872 + part * 1024 + half * 512
                        view, key = load_w(w_in, s_win, 0, 2048, c0, c0 + 512, 'win')
                        for gg in range(4):
                            g = half * 4 + gg
                            bank = gg % 2
                            for kc in range(16):
                                mm(pb[bank][:, 0:32], view[:, kc, gg * 128:(gg + 1) * 128], hT[:, kc, 0:32], kc == 0, kc == 15,
                                   [key, ('hT', 0)], PB(bank, 0, 32))
                            uv = uh[:, g, :, :].rearrange("p s j -> p (s j)")
                            if part == 0:
                                tt('dve', uv, pb[bank][:, 0:32], hv[:, :], ALU.mult, PB(bank, 0, 32) + [('hv',)], [('uh', g)])
                            else:
                                tt('dve', uv, pb[bank][:, 0:32], uv, ALU.mult, PB(bank, 0, 32) + [('uh', g)], [('uh', g)])
                dump('uh', uh[:, :, :, :].rearrange("p g s j -> p (g s j)"), [('uh', g) for g in range(8)])

            if stop != 'A':
                cast_weight(w1, 'w1', [(0, 2048, fg * 512, (fg + 1) * 512) for fg in range(16)])
            checkpoint('H')
            NT = 0 if stop == 'A' else (int(stop[1:].rstrip('a')) if (stop and stop[0] == 'T') else 8)
            for ti in range(NT):
                last_dbg = (ti == NT - 1)
                barrier(OVK_A + OVK_M)
                for blk in range(2):
                    r0 = ti * 256 + blk * 128
                    dma('sp', xo[:, blk, :], x_own[r0:r0 + 128, :], [], [('xo', blk)], ('xl', blk))
                    rms_T(xo[:, blk, :], 128, hT, blk * 128, A1c, B1c, [('xo', blk)], ('hT', blk), A1K, (5, 6))
                hk = [('hT', 0), ('hT', 1)]
                checkpoint('C0a_%d' % ti)
                view, key = load_w(w_in, s_win, 0, 2048, 0, 512, 'win')
                for mt in range(4):
                    for kc in range(16):
                        mm(pb[mt][:, 0:256], view[:, kc, mt * 128:(mt + 1) * 128], hT[:, kc, :], kc == 0, kc == 15, hk + [key], PB(mt, 0, 256))
                checkpoint('C0b_%d' % ti)
                for mt in range(4):
                    act(tb[:, 2, mt * 256:(mt + 1) * 256], pb[mt][:, 0:256], AF.Square, PB(mt, 0, 256), TB(2, mt * 256, mt * 256 + 256))
                    cp('act', tf[:, 0, mt * 256:(mt + 1) * 256], pb[mt][:, 0:256], PB(mt, 0, 256), TF(0, mt * 256, mt * 256 + 256))
                for mt in range(4):
                    mm(pb[4][:, 0:256], ones[:, :], tb[:, 2, mt * 256:(mt + 1) * 256], mt == 0, mt == 3, TB(2, mt * 256, mt * 256 + 256) + [('ones',)], PB(4, 0, 256))
                rsqrt(tf[:, 1, 0:256], pb[4][:, 0:256], 1.0 / 512, 128, PB(4, 0, 256), TF(1, 0, 256))
                for mt in range(4):
                    stt(cqT[:, mt, :], tf[:, 0, mt * 256:(mt + 1) * 256], pc('cqg', mt), tf[:, 1, 0:256], ALU.mult, ALU.mult,
                        TF(0, mt * 256, mt * 256 + 256) + TF(1, 0, 256) + [('prm',)], [('cqT',)])
                checkpoint('C1_%d' % ti)
                for blk in range(2):
                    for kc in range(16):
                        mm(pb[4][:, 256 + blk * 16:256 + (blk + 1) * 16], hT[:, kc, blk * 128:(blk + 1) * 128], widxw[:, kc, :], kc == 0, kc == 15,
                           hk + [('widxw',)], PB(4, 256, 288))
                act(wab[:, :, :], pb[4][:, 256:288].rearrange("p (b h) -> p b h", b=2), AF.Abs, PB(4, 256, 288), [('wab',)], scale=1.0 / 32)
                act(wsg[:, :, :], pb[4][:, 256:288].rearrange("p (b h) -> p b h", b=2), AF.Sign, PB(4, 256, 288), [('wsg',)])
                checkpoint('C2_%d' % ti)
                for half in range(2):
                    c0 = 1872 + half * 512
                    view, key = load_w(w_in, s_win, 0, 2048, c0, c0 + 512, 'win')
                    for gg in range(4):
                        g = half * 4 + gg
                        bank = gg % 2
                        for kc in range(16):
                            mm(pb[bank][:, 0:256], view[:, kc, gg * 128:(gg + 1) * 128], hT[:, kc, :], kc == 0, kc == 15, hk + [key], PB(bank, 0, 256))
                        cp('act', gcb[:, g, :], pb[bank][:, 0:256], PB(bank, 0, 256), [('gcb', g)])
                cwa, _ = PRM['cw']
                for half in range(2):
                    c0 = 2896 + half * 512
                    view, key = load_w(w_in, s_win, 0, 2048, c0, c0 + 512, 'win')
                    for gg in range(4):
                        g = half * 4 + gg
                        bank = gg % 2
                        for kc in range(16):
                            mm(pb[bank][:, 0:256], view[:, kc, gg * 128:(gg + 1) * 128], hT[:, kc, :], kc == 0, kc == 15, hk + [key], PB(bank, 0, 256))
                        tt('dve', ubuf[:, :, 2:130], pb[bank][:, 0:256].rearrange("p (b t) -> p b t", b=2),
                           gcb[:, g, :].rearrange("p (b t) -> p b t", b=2), ALU.mult, PB(bank, 0, 256) + [('gcb', g)], [('ubuf',)])
                        cp('pool', ubuf[:, :, 0:2], uh[:, g, 2 * ti:2 * ti + 2, :], [('uh', g)], [('ubuf', 'h')])
                        gv = gcb[:, g, :].rearrange("p (b t) -> p b t", b=2)
                        act(gv, ubuf[:, :, 2:130], AF.Identity, [('ubuf',), ('prm',)], [('gcb', g)],
                            scale=prm[:, cwa + 16 + g:cwa + 17 + g], bias=pc('cb', g))
                        stt(gv, ubuf[:, :, 1:129], prm[:, cwa + 8 + g:cwa + 9 + g], gv, ALU.mult, ALU.add,
                            [('ubuf',), ('ubuf', 'h'), ('gcb', g), ('prm',)], [('gcb', g)])
                        stt(gv, ubuf[:, :, 0:128], prm[:, cwa + g:cwa + g + 1], gv, ALU.mult, ALU.add,
                            [('ubuf',), ('ubuf', 'h'), ('gcb', g), ('prm',)], [('gcb', g)])
                for half in range(2):
                    c0 = 848 + half * 512
                    view, key = load_w(w_in, s_win, 0, 2048, c0, c0 + 512, 'win')
                    for gg in range(4):
                        g = half * 4 + gg
                        bank = gg % 2
                        for kc in range(16):
                            mm(pb[bank][:, 0:256], view[:, kc, gg * 128:(gg + 1) * 128], hT[:, kc, :], kc == 0, kc == 15, hk + [key], PB(bank, 0, 256))
                        tt('dve', ycb[:, :], pb[bank][:, 0:256], gcb[:, g, :], ALU.mult, PB(bank, 0, 256) + [('gcb', g)], [('ycb',)])
                        act(tb[:, 2, 0:256], ycb[:, :], AF.Square, [('ycb',)], TB(2, 0, 256))
                        mm(pb[2 + bank][:, 0:256], ones[:, :], tb[:, 2, 0:256], True, True, TB(2, 0, 256) + [('ones',)], PB(2 + bank, 0, 256))
                        rsqrt(tf[:, 1, 256:512], pb[2 + bank][:, 0:256], 1.0 / 128, 128, PB(2 + bank, 0, 256), TF(1, 256, 512))
                        stt(ymix[:, 8 + g, :], ycb[:, :], pc('cg', g), tf[:, 1, 256:512], ALU.mult, ALU.mult,
                            [('ycb',)] + TF(1, 256, 512) + [('prm',)], [('ymix', 8 + g)])
                checkpoint('C3_%d' % ti)
                if last_dbg:
                    dump('cqT', cqT[:, 0, :], [('cqT',)])
                    dump('yconv', ymix[:, 8, :], [('ymix', 8)])
                if ti == 0:
                    cast_weight(w2, 'w2', [(fq * 2048, (fq + 1) * 2048, nb * 512, (nb + 1) * 512) for nb in range(4) for fq in range(4)])
                vq, kq = load_w(w_uq, s_wuq, 0, 512, 0, 1024, 'wuq')
                vi, ki = load_w(w_iq, s_wiq, 0, 512, 0, 1024, 'wiq')
                mbB = ovl[:, 12288:16384]
                MBAP = [mb, mbB]
                MBK = [[('mb',)], [('gcb', g) for g in range(8)]]

                def emit_iq(blk):
                    tsl = slice(blk * 128, (blk + 1) * 128)
                    for half in range(2):
                        for pp in range(4):
                            pr = half * 4 + pp
                            for kc in range(4):
                                mm(pb[half][:, pp * 128:(pp + 1) * 128], vi[:, kc, pr * 128:(pr + 1) * 128], cqT[:, kc, tsl], kc == 0, kc == 3,
                                   [ki, ('cqT',)], PB(half))
                        cp('act', iqT[:, half * 4:half * 4 + 4, :], pb[half][:, :].rearrange("p (a t) -> p a t", a=4), PB(half), [('iqT',)])

                def emit_qa(blk):
                    tsl = slice(blk * 128, (blk + 1) * 128)
                    qT = tb[:, 2, :]
                    for half in range(2):
                        for hh in range(4):
                            h_ = half * 4 + hh
                            for kc in range(4):
                                mm(pb[2 + half][:, hh * 128:(hh + 1) * 128], vq[:, kc, h_ * 128:(h_ + 1) * 128], cqT[:, kc, tsl], kc == 0, kc == 3,
                                   [kq, ('cqT',)], PB(2 + half))
                        cp('dve', qT[:, half * 512:(half + 1) * 512], pb[2 + half][:, :], PB(2 + half), TB(2, half * 512, half * 512 + 512))
                    QK = TB(2)
                    for cc in range(2):
                        for h_ in range(8):
                            bank = 4 + cc * 2 + h_ // 4
                            mm(pb[bank][:, (h_ % 4) * 128:(h_ % 4 + 1) * 128], wuk[:, h_, cc * 128:(cc + 1) * 128], qT[:, h_ * 128:(h_ + 1) * 128],
                               True, True, QK + [('wuk',)], PB(bank))
                    sq = tb[:, 0:2, :]
                    for cc in range(2):
                        for hf in range(2):
                            bank = 4 + cc * 2 + hf
                            act(sq[:, cc, hf * 512:(hf + 1) * 512], pb[bank][:, :], AF.Square, PB(bank), TB(cc, hf * 512, hf * 512 + 512))
                    for hf in range(2):
                        for cc in range(2):
                            mm(pb[hf][:, :], ones[:, :], sq[:, cc, hf * 512:(hf + 1) * 512], cc == 0, cc == 1, TB(cc, hf * 512, hf * 512 + 512) + [('ones',)], PB(hf))
                        rsqrt(tf[:, 0, hf * 512:(hf + 1) * 512], pb[hf][:, :], 1.0 / 256, 128, PB(hf), TF(0, hf * 512, hf * 512 + 512))
                    for cc in range(2):
                        for hf in range(2):
                            bank = 4 + cc * 2 + hf
                            stt(qa[:, cc, hf * 512:(hf + 1) * 512], pb[bank][:, :], pc('qag', cc), tf[:, 0, hf * 512:(hf + 1) * 512], ALU.mult, ALU.mult,
                                PB(bank) + TF(0, hf * 512, hf * 512 + 512) + [('prm',)], [('qa', hf)])

                def gen_ixbs(blk):
                    j = 2 * ti + blk
                    nk = (2 * j + 2) * 128
                    mbx, mbk = MBAP[blk], MBK[blk]
                    cnt = 0
                    for k0 in range(0, nk, 512):
                        kw = min(512, nk - k0)
                        for h_ in range(16):
                            hp = h_ % 2
                            pr = h_ // 2
                            bank = cnt % 2
                            rb = cnt % 2
                            cnt += 1
                            kta = [('kx', i) for i in range(k0 // 256, (k0 + kw) // 256)]
                            mm(pb[bank][:, 0:kw], iqT[hp * 64:(hp + 1) * 64, pr, :], kx[hp * 64:(hp + 1) * 64, k0:k0 + kw], True, True,
                               [('iqT',)] + kta, PB(bank))
                            act(rbuf[:, rb, 0:kw], pb[bank][:, 0:kw], AF.Relu, PB(bank) + [('wab',)], [('rbuf', rb)], scale=wab[:, blk, h_:h_ + 1])
                            if h_ == 0:
                                ts('dve', score[:, k0:k0 + kw], rbuf[:, rb, 0:kw], wsg[:, blk, 0:1], None, ALU.mult, None,
                                   [('rbuf', rb), ('wsg',)], [('score',)])
                            else:
                                stt(score[:, k0:k0 + kw], rbuf[:, rb, 0:kw], wsg[:, blk, h_:h_ + 1], score[:, k0:k0 + kw], ALU.mult, ALU.add,
                                    [('rbuf', rb), ('wsg',), ('score',)], [('score',)])
                            yield
                    lo = cols[:, 8:9]
                    hi = cols[:, 9:10]
                    mid = cols[:, 10:11]
                    cntc = cols[:, 11:12]
                    d1 = cols[:, 14:15]
                    halfs = cols[:, 16:16 + NITER + 1]
                    tcol = cols[:, 13:14]
                    if j >= 1:
                        E('dve', lambda h, nk=nk: h.tensor_reduce(out=hi, in_=score[:, 0:nk], axis=AX.X, op=ALU.max), reads=[('score',)], writes=COL(9))
                        E('dve', lambda h, nk=nk: h.tensor_reduce(out=lo, in_=score[:, 0:nk], axis=AX.X, op=ALU.min), reads=[('score',)], writes=COL(8))
                        ts('dve', lo, lo, -1.0, None, ALU.add, None, COL(8), COL(8))
                        tt('dve', d1, hi, lo, ALU.subtract, COL(8, 9), COL(14))
                        ts('dve', halfs, pw2[:, :], d1, None, ALU.mult, None, COL(14) + [('pw2',)], COL(16))
                        tt('dve', mid, lo, halfs[:, 0:1], ALU.add, COL(8, 16), COL(10))
                    tt('dve', score[:, nk - 256:nk], score[:, nk - 256:nk], cmask[:, :], ALU.add, [('score',), ('cmask',)], [('score',)])
                    yield
                    if j >= 1:
                        for it in range(NITER):
                            ts('dve', mbx[:, 0:nk], score[:, 0:nk], mid, 0.0, ALU.is_gt, ALU.add, [('score',)] + mbk + COL(10),
                               mbk + COL(11), accum_out=cntc)
                            ts('dve', tcol, cntc, 255.5, halfs[:, it:it + 1], ALU.is_ge, ALU.mult, COL(11, 16), COL(13))
                            stt(mid, mid, halfs[:, it + 1:it + 2], tcol, ALU.subtract, ALU.add, COL(10, 16, 13), COL(10))
                            yield
                        ts('dve', lo, mid, halfs[:, NITER:NITER + 1], None, ALU.subtract, None, COL(10, 16), COL(8))
                        ts('dve', mbx[:, 0:nk], score[:, 0:nk], lo, -30000.0, ALU.is_le, ALU.mult, [('score',)] + mbk + COL(8), mbk)
                    else:
                        ts('dve', mbx[:, 0:nk], score[:, 0:nk], -1.0e29, -30000.0, ALU.is_le, ALU.mult, [('score',)] + mbk, mbk)
                    if last_dbg and blk == 1:
                        dump('score', score[:, 0:512], [('score',)])
                        dump('thr', cols[:, 0:16], COL(8, 9, 11))
                    yield

                def n_ixbs(blk):
                    j = 2 * ti + blk
                    nk = (2 * j + 2) * 128
                    return 16 * ((nk + 511) // 512) + 2 + (NITER if j >= 1 else 0)

                def gen_attn(blk):
                    j = 2 * ti + blk
                    nkc = 2 * j + 2
                    tsl = slice(blk * 128, (blk + 1) * 128)
                    mbx, mbk = MBAP[blk], MBK[blk]
                    for hg in range(2):
                        hs = slice(hg * 512, (hg + 1) * 512)

                        def qk_step(kc):
                            lb = 2 + (kc % 2)
                            pk = kc % 2
                            ksl = slice(kc * 128, (kc + 1) * 128)
                            kta = [('ckvT', kc // 2)]
                            bi = nkc - 1 - kc
                            near = (bi <= 2)
                            mm(pb[lb][:, :], ckvT[:, 0, ksl], qa[:, 0, hs], True, False, kta + [('qa', hg)], PB(lb))
                            mm(pb[lb][:, :], ckvT[:, 1, ksl], qa[:, 1, hs], False, False, kta + [('qa', hg)], PB(lb))
                            mm(pb[lb][:, :], mbx[:, ksl], ident4[:, :], False, not near, mbk + [('ident4',)], PB(lb))
                            if near:
                                mm(pb[lb][:, :], ident[:, :], biasb[:, bi, hs], False, True, [('biasb',), ('ident',)], PB(lb))
                            act(pTb[:, pk, :], pb[lb][:, :], AF.Exp, PB(lb), [('pTb', pk)], scale=1.0 / 16)

                        def pv_step(kc):
                            pk = kc % 2
                            for cc in range(2):
                                mm(pb[4 + cc][:, :], ckvk[:, kc, cc * 128:(cc + 1) * 128], pTb[:, pk, :], kc == 0, kc == nkc - 1,
                                   [('ckvk', kc // 2), ('pTb', pk)], PB(4 + cc))
                            mm(pb[6][:, :], ones[:, :], pTb[:, pk, :], kc == 0, kc == nkc - 1, [('ones',), ('pTb', pk)], PB(6))

                        for kc in range(nkc + 1):
                            if kc < nkc:
                                qk_step(kc)
                            if kc >= 1:
                                pv_step(kc - 1)
                            yield
                        oT = tb[:, 0:2, 0:512]
                        for cc in range(2):
                            cp('dve' if cc == 0 else 'act', oT[:, cc, :], pb[4 + cc][:, :], PB(4 + cc), TB(cc, 0, 512))
                        act(tf[:, 2, 0:512], pb[6][:, :], AF.Square, PB(6), TF(2, 0, 512), scale=math.sqrt(EPS))
                        for hl in range(4):
                            h_ = hg * 4 + hl
                            for cc in range(2):
                                mm(pb[7][:, hl * 128:(hl + 1) * 128], wuv[:, h_, cc, :], oT[:, cc, hl * 128:(hl + 1) * 128], cc == 0, cc == 1,
                                   TB(cc, 0, 512) + [('wuv',)], PB(7))
                        yield
                        act(tb[:, 2, 0:512], pb[7][:, :], AF.Square, PB(7), TB(2, 0, 512))
                        mm(pb[6][:, :], ones[:, :], tb[:, 2, 0:512], True, True, TB(2, 0, 512) + [('ones',)], PB(6))
                        stt(tf[:, 2, 512:1024], pb[6][:, :], 1.0 / 128, tf[:, 2, 0:512], ALU.mult, ALU.add, PB(6) + TF(2, 0, 512), TF(2, 512, 1024))
                        act(tf[:, 2, 512:1024], tf[:, 2, 512:1024], AF.Ln, TF(2, 512, 1024), TF(2, 512, 1024))
                        act(tf[:, 2, 512:1024], tf[:, 2, 512:1024], AF.Exp, TF(2, 512, 1024), TF(2, 512, 1024), scale=-0.5)
                        for hl in range(4):
                            h_ = hg * 4 + hl
                            stt(ymix[:, h_, tsl], pb[7][:, hl * 128:(hl + 1) * 128], pc('ag', h_), tf[:, 2, 512 + hl * 128:512 + (hl + 1) * 128],
                                ALU.mult, ALU.mult, PB(7) + TF(2, 512, 1024) + [('prm',)], [('ymix', h_)])
                        yield

                def n_attn(blk):
                    j = 2 * ti + blk
                    return 2 * (2 * j + 2 + 1 + 2)

                def run_all(g):
                    for _ in g:
                        pass

                def interleave(ga, na, gb, nb_):
                    ia = ib = 0
                    da = db = False
                    while not (da and db):
                        pick_a = (not da) and (db or ia * nb_ <= ib * na)
                        if pick_a:
                            try:
                                next(ga)
                                ia += 1
                            except StopIteration:
                                da = True
                        else:
                            try:
                                next(gb)
                                ib += 1
                            except StopIteration:
                                db = True

                emit_iq(0)
                emit_qa(0)
                run_all(gen_ixbs(0))
                emit_iq(1)
                interleave(gen_attn(0), n_attn(0), gen_ixbs(1), n_ixbs(1))
                emit_qa(1)
                run_all(gen_attn(1))
                checkpoint('AT_%d' % ti)
                if last_dbg:
                    dump('yattn', ymix[:, 0, :], [('ymix', 0)])
                YK = [('ymix', i) for i in range(16)]
                for nb in range(4):
                    view, key = load_w(w_out, s_wout, 0, 2048, nb * 512, (nb + 1) * 512, 'wout')
                    dma('sp', tf[:, 0, 0:512], gsc[0:1, nb * 512:(nb + 1) * 512].partition_broadcast(128), [('gsc', 0)], TF(0, 0, 512), ('bc', 0))
                    for blk in range(2):
                        for kc in range(16):
                            mm(pb[blk][:, :], ymix[:, kc, blk * 128:(blk + 1) * 128], view[:, kc, :], kc == 0, kc == 15, YK + [key], PB(blk))
                        tt('dve', tf[:, 1, blk * 512:(blk + 1) * 512], pb[blk][:, :], tf[:, 0, 0:512], ALU.mult, PB(blk) + TF(0, 0, 512), TF(1, blk * 512, blk * 512 + 512))
                        tt('pool', xo[:, blk, nb * 512:(nb + 1) * 512], xo[:, blk, nb * 512:(nb + 1) * 512], tf[:, 1, blk * 512:(blk + 1) * 512], ALU.add,
                           TF(1, blk * 512, blk * 512 + 512) + [('xo', blk)], [('xo', blk)])
                if last_dbg:
                    dump('x1', xo[:, 0, :], [('xo', 0)])
                if stop == 'T%da' % NT and last_dbg:
                    break
                barrier(OVK_A + OVK_M)
                for blk in range(2):
                    rms_T(xo[:, blk, :], 128, hT, blk * 128, A2c, B2c, [('xo', blk)], ('hT', blk), A2K, (5, 6))
                b1a, _ = PRM['b1']
                for fg in range(16):
                    view, key = load_w(w1, s_w1, 0, 2048, fg * 512, (fg + 1) * 512, 'w1')
                    for ft in range(4):
                        f = fg * 4 + ft
                        bank = f % 4
                        rb = f % 2
                        for kc in range(16):
                            mm(pb[bank][:, 0:256], view[:, kc, ft * 128:(ft + 1) * 128], hT[:, kc, :], kc == 0, kc == 15, hk + [key], PB(bank, 0, 256))
                        act(rbuf[:, rb, 0:256], pb[bank][:, 0:256], AF.Relu, PB(bank, 0, 256) + [('prm',)], [('rbuf', rb)], bias=prm[:, b1a + f:b1a + f + 1])
                        stt(aT[:, f, :], pb[bank][:, 0:256], prm[:, b1a + f:b1a + f + 1], rbuf[:, rb, 0:256], ALU.add, ALU.mult,
                            PB(bank, 0, 256) + [('rbuf', rb), ('prm',)], [('aT', f)])
                for nb in range(4):
                    nsl = slice(nb * 512, (nb + 1) * 512)
                    dma('sp', tf[:, 0, 0:512], gsc[1:2, nsl].partition_broadcast(128), [('gsc', 1)], TF(0, 0, 512), ('bc', 0))
                    dma('sp', tf[:, 0, 512:1024], b2_d[0:1, nsl].partition_broadcast(128), [], TF(0, 512, 1024), ('bc', 1))
                    tt('pool', tf[:, 0, 512:1024], tf[:, 0, 512:1024], tf[:, 0, 0:512], ALU.mult, TF(0, 0, 512) + TF(0, 512, 1024), TF(0, 512, 1024))
                    for blk in range(2):
                        tt('pool', xo[:, blk, nsl], xo[:, blk, nsl], tf[:, 0, 512:1024], ALU.add, TF(0, 512, 1024) + [('xo', blk)], [('xo', blk)])
                    for fq in range(4):
                        view, key = load_w(w2, s_w2, fq * 2048, (fq + 1) * 2048, nb * 512, (nb + 1) * 512, 'w2')
                        for kc in range(16):
                            f = fq * 16 + kc
                            for blk in range(2):
                                mm(pb[4 + blk][:, :], aT[:, f, blk * 128:(blk + 1) * 128], view[:, kc, :], f == 0, f == 63, [('aT', f), key], PB(4 + blk))
                    for blk in range(2):
                        tt('dve', tf[:, 1, blk * 512:(blk + 1) * 512], pb[4 + blk][:, :], tf[:, 0, 0:512], ALU.mult, PB(4 + blk) + TF(0, 0, 512), TF(1, blk * 512, blk * 512 + 512))
                        tt('pool', xo[:, blk, nsl], xo[:, blk, nsl], tf[:, 1, blk * 512:(blk + 1) * 512], ALU.add, TF(1, blk * 512, blk * 512 + 512) + [('xo', blk)], [('xo', blk)])
                for blk in range(2):
                    r0 = ti * 256 + blk * 128
                    dma('sp', out_d[r0:r0 + 128, :], xo[:, blk, :], [('xo', blk)], [('out', ti, blk)], ('out', blk))
        try:
            emit_all()
        except _Stop:
            pass
        for k in list(P.dma_sems.keys()):
            P.final_waits.append(k)
        stats = P.finalize(block)
        if os.environ.get("KDEBUG"):
            print("ops/waits per engine:", stats)
    return nc


def _t5_bucket(n):
    n = np.asarray(n, np.int32)
    nf = np.maximum(n, 1).astype(np.float32)
    large = 16 + (np.log(nf / np.float32(16)) / np.float32(math.log(128 / 16)) * np.float32(16)).astype(np.int32)
    large = np.minimum(large, 31)
    return np.where(n < 16, n, large)


def make_inputs(inp, core):
    b, r = core // 2, core % 2
    x = np.asarray(inp['x'], np.float32)
    qbs = [2 * j + r for j in range(16)]
    x_own = np.concatenate([x[b, q * 128:(q + 1) * 128] for q in qbs], 0)
    x_halo = np.zeros((32, 2048), np.float32)
    hvv = np.zeros((32,), np.float32)
    for j, q in enumerate(qbs):
        if q > 0:
            x_halo[2 * j:2 * j + 2] = x[b, q * 128 - 2:q * 128]
            hvv[2 * j:2 * j + 2] = 1.0
    prm = np.zeros((128, NPRM), np.float32)

    def put(name, arr):
        a, e = PRM[name]
        prm[:, a:e] = arr
    put('c', _pcol(inp['c'][b]))
    put('bada', _pcol(inp['b_ada'][0]))
    put('n1g', _pcol(inp['norm1_g'][0]))
    put('n2g', _pcol(inp['norm2_g'][0]))
    put('cqg', _pcol(inp['cq_norm_g'][0]))
    put('kvg', _pcol(inp['kv_norm_g'][0]))
    put('qag', _pcol(inp['q_abs_norm_g'][0]))
    put('idxg', np.tile(np.asarray(inp['idx_k_norm_g'][0], np.float32), 2)[:, None])
    put('idxb', np.tile(np.asarray(inp['idx_k_norm_b'][0], np.float32), 2)[:, None])
    cw = np.asarray(inp['conv_w'][0], np.float32)
    put('cw', np.concatenate([_pcol(cw[jj]) for jj in range(3)], 1))
    put('cb', _pcol(inp['conv_b'][0]))
    put('ag', _pcol(np.asarray(inp['attn_out_norm_g'][0]).reshape(-1)))
    put('cg', _pcol(np.asarray(inp['conv_out_norm_g'][0]).reshape(-1)))
    put('b1', _pcol(inp['b_mlp1'][0]))
    NEG = np.float32(-1.0e30)
    tri = np.where(np.arange(128)[None, :] <= np.arange(128)[:, None], np.float32(0), NEG).astype(np.float32)
    cm = np.zeros((128, 256), np.float32)
    if r == 0:
        cm[:, 0:128] = tri
        cm[:, 128:256] = NEG
    else:
        cm[:, 128:256] = tri
    rel = np.asarray(inp['rel_bias'], np.float32)
    sI = np.arange(128)[:, None]
    tI = np.arange(128)[None, :]
    idx0 = _t5_bucket(np.maximum(tI - sI, 0))
    idx1 = _t5_bucket(tI - sI + 128)
    idxf = np.full((128, 128), 31, np.int64)
    order = [idxf, idx0, idx1] if r == 0 else [idx0, idx1, idxf]
    bias3 = np.stack([np.transpose(rel[ix], (0, 2, 1)).reshape(128, 1024) for ix in order], 0).astype(np.float32)
    m = {
        'x_b': np.ascontiguousarray(x[b]), 'x_own': x_own, 'x_halo': x_halo, 'prm': prm,
        'hv': np.ascontiguousarray(np.broadcast_to(hvv[None, :], (128, 32))).astype(np.float32),
        'cmask': cm, 'bias3': bias3, 'rb31': np.ascontiguousarray(rel[31:32, :]),
        'ident': np.eye(128, dtype=np.float32), 'b2': np.ascontiguousarray(np.asarray(inp['b_mlp2'], np.float32).reshape(1, 2048)),
        'w_ada': np.asarray(inp['w_ada'][0], np.float32), 'w_in': np.asarray(inp['w_in'][0], np.float32),
        'w_uq': np.asarray(inp['w_uq'][0], np.float32), 'w_uk': np.asarray(inp['w_uk'][0], np.float32),
        'w_uv': np.asarray(inp['w_uv'][0], np.float32), 'w_iq': np.asarray(inp['w_iq'][0], np.float32),
        'w_out': np.asarray(inp['w_out'][0], np.float32), 'w1': np.asarray(inp['w_mlp1'][0], np.float32),
        'w2': np.asarray(inp['w_mlp2'][0], np.float32),
    }
    return m


_NC_CACHE = {}


def kernel(**inp):
    if 'nc' not in _NC_CACHE:
        _NC_CACHE['nc'] = build()
    nc = _NC_CACHE['nc']
    in_maps = [make_inputs(inp, i) for i in range(8)]
    res = run_bass_kernel_spmd(nc, in_maps, core_ids=list(range(8)))
    out = np.zeros((4, 4096, 2048), np.float32)
    for i in range(8):
        b, r = i // 2, i % 2
        o = res.results[i]['out']
        for j in range(16):
            q = 2 * j + r
            out[b, q * 128:(q + 1) * 128] = o[j * 128:(j + 1) * 128]
    return out
```
